# Optimizing a Trainium2 kernel written in Bass

```python
import math
import jax, jax.numpy as jnp
from jax import lax
import numpy as np

D_MODEL = 2048
BATCH = 16
SEQ = 2048
DEPTH = 4

N_MIXERS = 3
N_NSA = (DEPTH + N_MIXERS - 1) // N_MIXERS
N_CONV = (DEPTH + N_MIXERS - 2) // N_MIXERS
N_GLA = DEPTH // N_MIXERS

DEEPNORM_ALPHA = (2.0 * DEPTH) ** 0.25
DEEPNORM_BETA = (8.0 * DEPTH) ** -0.25
LN_EPS = 1e-5
MACARON_WEIGHT = 0.5

D_FF = 5632

NSA_HEADS = 16
NSA_KV_GROUPS = 4
NSA_HEAD_DIM = D_MODEL // NSA_HEADS
NSA_Q = NSA_HEADS * NSA_HEAD_DIM
NSA_KV = NSA_KV_GROUPS * NSA_HEAD_DIM
NSA_IN = NSA_Q + 6 * NSA_KV + 3 * NSA_HEADS
CMP_BLOCK = 32
CMP_STRIDE = 16
SEL_BLOCK = 64
SEL_TOPK = 16
WINDOW = 512
SEL_Q_BLOCK = 16
ATTN_Q_BLOCK = 128
ROPE_THETA = 10000.0
MAX_POS_OFFSET = 4096
NEG = -1e30
FORCE = 1e3

CONV_WIDTH = 3

GLA_HEADS = 4
GLA_KEY_DIM = D_MODEL // 2
GLA_VAL_DIM = D_MODEL
GLA_HEAD_K = GLA_KEY_DIM // GLA_HEADS
GLA_HEAD_V = GLA_VAL_DIM // GLA_HEADS
GLA_GATE_RANK = 16
GLA_GATE_NORM = 16.0
GLA_CHUNK = 64
GLA_IN = 2 * GLA_KEY_DIM + 2 * GLA_VAL_DIM + GLA_GATE_RANK

kernel_name = "hybrid_nsa_shortconv_gla_macaron_deepnorm"


def layer_norm(x, g, b):
    xf = x.astype(jnp.float32)
    mu = jnp.mean(xf, axis=-1, keepdims=True)
    var = jnp.mean(jnp.square(xf - mu), axis=-1, keepdims=True)
    return ((xf - mu) * lax.rsqrt(var + LN_EPS) * g + b).astype(x.dtype)


def swiglu(x, w_in, w_out):
    gate, up = jnp.split(x @ w_in, 2, axis=-1)
    return (jax.nn.silu(gate) * up) @ w_out


def rope(t, positions):
    hd = t.shape[-1]
    inv = ROPE_THETA ** (-jnp.arange(0, hd, 2, dtype=jnp.float32) / hd)
    ang = positions.astype(jnp.float32)[..., None] * inv
    cos = jnp.cos(ang)[:, :, None, :]
    sin = jnp.sin(ang)[:, :, None, :]
    t1, t2 = jnp.split(t.astype(jnp.float32), 2, axis=-1)
    return jnp.concatenate([t1 * cos - t2 * sin, t2 * cos + t1 * sin], axis=-1).astype(t.dtype)


def masked_softmax(s, valid):
    p = jax.nn.softmax(jnp.where(valid, s.astype(jnp.float32), NEG), axis=-1)
    return jnp.where(valid, p, 0.0)


def nsa_mixer(x, positions, w_in, gate_b, cmp_pos, cmp_w1, cmp_w2, w_out):
    B, T, _ = x.shape
    H, G, hd = NSA_HEADS, NSA_KV_GROUPS, NSA_HEAD_DIM
    R = H // G
    scale = hd ** -0.5
    splits = [NSA_Q + i * NSA_KV for i in range(7)]
    q, kc, vc, ks, vs, kw, vw, gl = jnp.split(x @ w_in, splits, axis=-1)
    q = rope(q.reshape(B, T, H, hd), positions)
    kc, ks, kw = [rope(t.reshape(B, T, G, hd), positions) for t in (kc, ks, kw)]
    vc, vs, vw = [t.reshape(B, T, G, hd) for t in (vc, vs, vw)]
    gates = jax.nn.sigmoid(gl + gate_b).reshape(B, T, 3, G, R).transpose(2, 0, 3, 4, 1)
    qg = q.reshape(B, T, G, R, hd).transpose(0, 2, 3, 1, 4) * scale
    tq = jnp.arange(T)

    n_cmp = (T - CMP_BLOCK) // CMP_STRIDE + 1
    tok = np.arange(n_cmp)[:, None] * CMP_STRIDE + np.arange(CMP_BLOCK)[None, :]

    def compress(t, pos_emb, w1, w2):
        blocks = t[:, tok] + pos_emb[:, None, :]
        blocks = blocks.transpose(0, 1, 3, 2, 4).reshape(B, n_cmp, G, CMP_BLOCK * hd)
        return jax.nn.gelu(blocks @ w1) @ w2

    k_cmp = compress(kc, cmp_pos[0], cmp_w1[0], cmp_w2[0])
    v_cmp = compress(vc, cmp_pos[1], cmp_w1[1], cmp_w2[1])
    cmp_end = jnp.arange(n_cmp) * CMP_STRIDE + CMP_BLOCK - 1
    s_cmp = jnp.einsum('bgrtd,bngd->bgrtn', qg, k_cmp)
    p_cmp = masked_softmax(s_cmp, cmp_end[None, :] <= tq[:, None])
    o_cmp = jnp.einsum('bgrtn,bngd->bgrtd', p_cmp.astype(v_cmp.dtype), v_cmp)

    n_sel = T // SEL_BLOCK
    k_sel = min(SEL_TOPK, n_sel)
    c_start = jnp.arange(n_cmp) * CMP_STRIDE
    s_start = jnp.arange(n_sel) * SEL_BLOCK
    overlap = ((c_start[:, None] < s_start[None, :] + SEL_BLOCK)
               & (c_start[:, None] + CMP_BLOCK > s_start[None, :])).astype(jnp.float32)
    importance = jnp.einsum('bgrtn,nm->bgtm', p_cmp, overlap)
    cur = tq // SEL_BLOCK
    blk = jnp.arange(n_sel)
    sel_valid = blk[None, :] <= cur[:, None]
    forced = (blk[None, :] == 0) | (blk[None, :] == cur[:, None]) | (blk[None, :] == cur[:, None] - 1)
    score = jnp.where(sel_valid, importance + jnp.where(forced, FORCE, 0.0), NEG)
    _, idx = lax.top_k(score, k_sel)
    ks_blk = ks.reshape(B, n_sel, SEL_BLOCK, G, hd).transpose(0, 3, 1, 2, 4)
    vs_blk = vs.reshape(B, n_sel, SEL_BLOCK, G, hd).transpose(0, 3, 1, 2, 4)
    bi = jnp.arange(B)[:, None, None, None]
    gi = jnp.arange(G)[None, :, None, None]

    def sel_block(c):
        t0 = c * SEL_Q_BLOCK
        qc = lax.dynamic_slice_in_dim(qg, t0, SEL_Q_BLOCK, axis=3)
        ic = lax.dynamic_slice_in_dim(idx, t0, SEL_Q_BLOCK, axis=2)
        kg = ks_blk[bi, gi, ic]
        vg = vs_blk[bi, gi, ic]
        s = jnp.einsum('bgrqd,bgqkld->bgrqkl', qc, kg).reshape(B, G, R, SEL_Q_BLOCK, k_sel * SEL_BLOCK)
        kpos = ic[..., None] * SEL_BLOCK + jnp.arange(SEL_BLOCK)
        qpos = t0 + jnp.arange(SEL_Q_BLOCK)
        valid = (kpos <= qpos[None, None, :, None, None]).reshape(B, G, 1, SEL_Q_BLOCK, k_sel * SEL_BLOCK)
        p = masked_softmax(s, valid).astype(vg.dtype).reshape(B, G, R, SEL_Q_BLOCK, k_sel, SEL_BLOCK)
        return jnp.einsum('bgrqkl,bgqkld->bgrqd', p, vg)

    o_sel = lax.map(sel_block, jnp.arange(T // SEL_Q_BLOCK))
    o_sel = o_sel.transpose(1, 2, 3, 0, 4, 5).reshape(B, G, R, T, hd)

    pad = ((0, 0), (0, 0), (WINDOW, 0), (0, 0))
    kw_p = jnp.pad(kw.transpose(0, 2, 1, 3), pad)
    vw_p = jnp.pad(vw.transpose(0, 2, 1, 3), pad)
    band = WINDOW + ATTN_Q_BLOCK

    def win_block(c):
        t0 = c * ATTN_Q_BLOCK
        qb = lax.dynamic_slice_in_dim(qg, t0, ATTN_Q_BLOCK, axis=3)
        kb = lax.dynamic_slice_in_dim(kw_p, t0, band, axis=2)
        vb = lax.dynamic_slice_in_dim(vw_p, t0, band, axis=2)
        s = jnp.einsum('bgrqd,bgkd->bgrqk', qb, kb)
        kpos = t0 - WINDOW + jnp.arange(band)
        qpos = t0 + jnp.arange(ATTN_Q_BLOCK)
        valid = ((kpos[None, :] <= qpos[:, None]) & (kpos[None, :] > qpos[:, None] - WINDOW)
                 & (kpos[None, :] >= 0))
        p = masked_softmax(s, valid).astype(vb.dtype)
        return jnp.einsum('bgrqk,bgkd->bgrqd', p, vb)

    o_win = lax.map(win_block, jnp.arange(T // ATTN_Q_BLOCK))
    o_win = o_win.transpose(1, 2, 3, 0, 4, 5).reshape(B, G, R, T, hd)

    o = gates[0][..., None] * o_cmp + gates[1][..., None] * o_sel + gates[2][..., None] * o_win
    o = o.transpose(0, 3, 1, 2, 4).reshape(B, T, H * hd)
    return o @ w_out


def conv_mixer(x, w_in, conv_w, w_out):
    D = x.shape[-1]
    b_gate, c_gate, h = jnp.split(x @ w_in, 3, axis=-1)
    u = c_gate * h
    y = lax.conv_general_dilated(u, conv_w[:, None, :], window_strides=(1,),
                                 padding=[(CONV_WIDTH - 1, 0)],
                                 dimension_numbers=('NWC', 'WIO', 'NWC'),
                                 feature_group_count=D)
    return (b_gate * y) @ w_out


def gla_chunked(q, k, v, g):
    B, T, H, dk = q.shape
    dv = v.shape[-1]
    C = GLA_CHUNK
    N = T // C

    def chunks(t):
        return t.reshape(B, N, C, H, t.shape[-1]).transpose(1, 0, 3, 2, 4)

    q, k, v, g = chunks(q), chunks(k), chunks(v), chunks(g)
    b = jnp.cumsum(g, axis=3)
    b_last = b[..., -1:, :]
    qd = q * jnp.exp(b)
    kd = k * jnp.exp(-b)
    kl = k * jnp.exp(b_last - b)
    causal = jnp.tril(jnp.ones((C, C), dtype=bool))
    a = jnp.where(causal, jnp.einsum('nbhcd,nbhsd->nbhcs', qd, kd), 0.0)
    o_intra = jnp.einsum('nbhcs,nbhsv->nbhcv', a, v)
    decay = jnp.exp(b_last[..., 0, :])

    def step(state, inp):
        qd_n, kl_n, v_n, dl_n = inp
        o_n = jnp.einsum('bhcd,bhdv->bhcv', qd_n, state)
        state = state * dl_n[..., None] + jnp.einsum('bhcd,bhcv->bhdv', kl_n, v_n)
        return state, o_n

    s0 = jnp.zeros((B, H, dk, dv), q.dtype)
    _, o_inter = lax.scan(step, s0, (qd, kl, v, decay))
    return (o_intra + o_inter).transpose(1, 0, 3, 2, 4).reshape(B, T, H, dv)


def gla_mixer(x, w_in, w_a2, b_a, norm_g, w_out):
    B, T, _ = x.shape
    splits = [GLA_KEY_DIM, 2 * GLA_KEY_DIM, 2 * GLA_KEY_DIM + GLA_VAL_DIM, 2 * GLA_KEY_DIM + 2 * GLA_VAL_DIM]
    q, k, v, r, a = jnp.split(x @ w_in, splits, axis=-1)
    gk = jax.nn.log_sigmoid((a @ w_a2 + b_a).astype(jnp.float32)) / GLA_GATE_NORM
    f32 = jnp.float32
    q = q.astype(f32).reshape(B, T, GLA_HEADS, GLA_HEAD_K) * (GLA_HEAD_K ** -0.5)
    k = k.astype(f32).reshape(B, T, GLA_HEADS, GLA_HEAD_K)
    v = v.astype(f32).reshape(B, T, GLA_HEADS, GLA_HEAD_V)
    gk = gk.reshape(B, T, GLA_HEADS, GLA_HEAD_K)
    o = gla_chunked(q, k, v, gk)
    o = o * lax.rsqrt(jnp.mean(jnp.square(o), axis=-1, keepdims=True) + LN_EPS) * norm_g
    o = o.astype(x.dtype).reshape(B, T, GLA_VAL_DIM) * jax.nn.silu(r)
    return o @ w_out


def setup_inputs(seed: int = 0) -> dict:
    key = jax.random.key(seed)
    ks = jax.random.split(key, 20)

    def nrm(k, shape, scale):
        return jax.random.normal(k, shape, jnp.float32) * scale

    D = D_MODEL
    hd = NSA_HEAD_DIM
    x = nrm(ks[0], (BATCH, SEQ, D), 1.0)
    positions = (jax.random.randint(ks[1], (BATCH, 1), 0, MAX_POS_OFFSET, dtype=jnp.int32)
                 + jnp.arange(SEQ, dtype=jnp.int32)[None, :])
    return {
        'x': x,
        'positions': positions,
        'ln_g': 1.0 + nrm(ks[2], (DEPTH, 3, D), 0.02),
        'ln_b': nrm(ks[3], (DEPTH, 3, D), 0.02),
        'ffn_w_in': nrm(ks[4], (DEPTH, 2, D, 2 * D_FF), D ** -0.5),
        'ffn_w_out': nrm(ks[5], (DEPTH, 2, D_FF, D), D_FF ** -0.5 * DEEPNORM_BETA),
        'nsa_w_in': nrm(ks[6], (N_NSA, D, NSA_IN), D ** -0.5),
        'nsa_gate_b': nrm(ks[7], (N_NSA, 3 * NSA_HEADS), 0.1),
        'nsa_cmp_pos': nrm(ks[8], (N_NSA, 2, CMP_BLOCK, hd), 0.1),
        'nsa_cmp_w1': nrm(ks[9], (N_NSA, 2, CMP_BLOCK * hd, hd), (CMP_BLOCK * hd) ** -0.5),
        'nsa_cmp_w2': nrm(ks[10], (N_NSA, 2, hd, hd), hd ** -0.5),
        'nsa_w_out': nrm(ks[11], (N_NSA, NSA_Q, D), NSA_Q ** -0.5 * DEEPNORM_BETA),
        'conv_w_in': nrm(ks[12], (N_CONV, D, 3 * D), D ** -0.5),
        'conv_w': nrm(ks[13], (N_CONV, CONV_WIDTH, D), CONV_WIDTH ** -0.5),
        'conv_w_out': nrm(ks[14], (N_CONV, D, D), D ** -0.5 * DEEPNORM_BETA),
        'gla_w_in': nrm(ks[15], (N_GLA, D, GLA_IN), D ** -0.5),
        'gla_w_a2': nrm(ks[16], (N_GLA, GLA_GATE_RANK, GLA_KEY_DIM), GLA_GATE_RANK ** -0.5),
        'gla_b_a': nrm(ks[17], (N_GLA, GLA_KEY_DIM), 0.1),
        'gla_norm_g': 1.0 + nrm(ks[18], (N_GLA, GLA_HEAD_V), 0.02),
        'gla_w_out': nrm(ks[19], (N_GLA, GLA_VAL_DIM, D), GLA_VAL_DIM ** -0.5 * DEEPNORM_BETA),
    }


def reference(x, positions, ln_g, ln_b, ffn_w_in, ffn_w_out,
              nsa_w_in, nsa_gate_b, nsa_cmp_pos, nsa_cmp_w1, nsa_cmp_w2, nsa_w_out,
              conv_w_in, conv_w, conv_w_out,
              gla_w_in, gla_w_a2, gla_b_a, gla_norm_g, gla_w_out):
    for i in range(DEPTH):
        x = layer_norm(DEEPNORM_ALPHA * x + MACARON_WEIGHT * swiglu(x, ffn_w_in[i, 0], ffn_w_out[i, 0]),
                       ln_g[i, 0], ln_b[i, 0])
        kind, j = i % N_MIXERS, i // N_MIXERS
        if kind == 0:
            y = nsa_mixer(x, positions, nsa_w_in[j], nsa_gate_b[j], nsa_cmp_pos[j],
                          nsa_cmp_w1[j], nsa_cmp_w2[j], nsa_w_out[j])
        elif kind == 1:
            y = conv_mixer(x, conv_w_in[j], conv_w[j], conv_w_out[j])
        else:
            y = gla_mixer(x, gla_w_in[j], gla_w_a2[j], gla_b_a[j], gla_norm_g[j], gla_w_out[j])
        x = layer_norm(DEEPNORM_ALPHA * x + y, ln_g[i, 1], ln_b[i, 1])
        x = layer_norm(DEEPNORM_ALPHA * x + MACARON_WEIGHT * swiglu(x, ffn_w_in[i, 1], ffn_w_out[i, 1]),
                       ln_g[i, 2], ln_b[i, 2])
    return x
```

```python
import math
import os
import numpy as np
import ml_dtypes
import concourse.bass as bass
import concourse.mybir as mybir
from concourse.bass_utils import run_bass_kernel_spmd

F32 = mybir.dt.float32
BF16 = mybir.dt.bfloat16
I32 = mybir.dt.int32
ALU = mybir.AluOpType
AF = mybir.ActivationFunctionType
AX = mybir.AxisListType

D = 2048
T = 2048
DEPTH = 4
DFF = 5632
NCORES = 8
ALPHA = (2.0 * DEPTH) ** 0.25
LN_EPS = 1e-5
TT = 512
NDT = D // 128
NHT = DFF // 128
RING_SLOT = 44 * 256
NRING = 3


PSUM_NAMES = {"pst", "opst", "pin", "py", "ps1", "ps2", "gb", "hb", "htb", "nb", "cb", "ab", "atb"}


class Buf:
    __slots__ = ("name", "writer", "readers", "dsem", "dcnt", "excl")

    def __init__(self, name):
        self.name = name
        self.writer = None
        self.readers = []
        self.dsem = None
        self.dcnt = 0
        self.excl = bool(name) and name[0] in PSUM_NAMES


class KB:
    def __init__(self, nc):
        self.nc = nc
        self.eng = {"pe": nc.tensor, "act": nc.scalar, "dve": nc.vector, "pool": nc.gpsimd, "sp": nc.sync}
        self.sem = {}
        self.cnt = {}
        self.waited = {e: {} for e in self.eng}
        self._stack = []
        for e in ("pe", "act", "dve", "pool"):
            self.sem[e] = self._enter(nc.semaphore("sem_" + e))
            self.cnt[e] = 0
        self.semid = {}
        self.bufs = {}
        self.ndma = 0

    def _enter(self, cm):
        v = cm.__enter__()
        self._stack.append(cm)
        return v

    def close(self):
        while self._stack:
            self._stack.pop().__exit__(None, None, None)

    def buf(self, *key):
        b = self.bufs.get(key)
        if b is None:
            b = Buf(key)
            self.bufs[key] = b
        return b

    def _wait(self, eng, tick):
        if tick is None:
            return
        if tick[0] == "e":
            pe, c = tick[1], tick[2]
            if pe == eng and eng == "pe":
                return
            key = pe
            sem, val = self.sem[pe], c
        else:
            b = tick[1]
            key = id(b)
            sem, val = b.dsem, 16 * b.dcnt
        w = self.waited[eng]
        if w.get(key, 0) >= val:
            return
        w[key] = val
        self.eng[eng].wait_ge(sem, val)

    def _sync(self, eng, reads, writes, same_war=False):
        for r in reads:
            self._wait(eng, r.writer)
            if r.excl:
                for t in r.readers:
                    if not (t[0] == "e" and t[1] == eng):
                        self._wait(eng, t)
        for wb in writes:
            self._wait(eng, wb.writer)
            for t in wb.readers:
                if t[0] == "e" and t[1] == eng and not same_war:
                    continue
                self._wait(eng, t)

    def op(self, eng, fn, reads=(), writes=()):
        self._sync(eng, reads, writes)
        ins = fn()
        self.cnt[eng] += 1
        ins.then_inc(self.sem[eng], 1)
        tick = ("e", eng, self.cnt[eng])
        for r in reads:
            r.readers.append(tick)
            if len(r.readers) > 24:
                r.readers = self._prune(r.readers)
        for wb in writes:
            wb.writer = tick
            wb.readers = []
        return ins

    def _prune(self, readers):
        best = {}
        out = []
        for t in readers:
            if t[0] == "e":
                if t[1] not in best or best[t[1]][2] < t[2]:
                    best[t[1]] = t
            else:
                if all(o[1] is not t[1] for o in out):
                    out.append(t)
        return out + list(best.values())

    def dma(self, q, out, in_, owner, reads=(), writes=()):
        if owner.dsem is None:
            owner.dsem = self._enter(self.nc.semaphore("dsem%d" % len(self.semid)))
            self.semid[id(owner)] = owner
        self._sync(q, reads, writes, same_war=True)
        ins = self.eng[q].dma_start(out=out, in_=in_)
        owner.dcnt += 1
        ins.then_inc(owner.dsem, 16)
        if q == "pool":
            pass
        tick = ("d", owner)
        for r in reads:
            r.readers.append(tick)
            if len(r.readers) > 24:
                r.readers = self._prune(r.readers)
        for wb in writes:
            wb.writer = tick
            wb.readers = []
        self.ndma += 1
        return ins

    def barrier(self, bufs=()):
        for e in ("pe", "act", "dve", "pool", "sp"):
            for pe in ("pe", "act", "dve", "pool"):
                if self.cnt[pe] > 0 and not (pe == e):
                    self._wait(e, ("e", pe, self.cnt[pe]))
            for b in self.semid.values():
                if b.dcnt > 0 and b.name[0] != "wbf":
                    self._wait(e, ("d", b))


class Prog:
    def __init__(self, nseq=2, layers=(0, 1, 2, 3), stop_after=None):
        self.nseq = nseq
        self.NT = nseq * T
        self.layers = layers
        self.stop_after = stop_after
        self.nc = bass.Bass("TRN2", target_bir_lowering=False)
        self.k = KB(self.nc)
        self.inputs = {}
        self._cms = []

    def dram_in(self, name, shape, dt=F32):
        t = self.nc.dram_tensor(name, list(shape), dt, kind="ExternalInput").ap()
        self.inputs[name] = (tuple(shape), dt)
        return t

    def dram_tmp(self, name, shape, dt):
        return self.nc.dram_tensor(name, list(shape), dt, kind="Internal").ap()

    def enter(self, cm):
        v = cm.__enter__()
        self._cms.append(cm)
        return v

    def sb(self, name, shape, dt):
        self._uid = getattr(self, "_uid", 0) + 1
        return self.enter(self.nc.sbuf_tensor("%s_s%d" % (name, self._uid), list(shape), dt))

    def ps(self, name, shape, dt=F32):
        self._uid = getattr(self, "_uid", 0) + 1
        return self.enter(self.nc.psum_tensor("%s_p%d" % (name, self._uid), list(shape), dt))

    def mark(self):
        return len(self._cms)

    def release(self, mark):
        while len(self._cms) > mark:
            self._cms.pop().__exit__(None, None, None)

    def convert_weight(self, name, w_ap, K, col_blocks, bw, group=None):
        k = self.k
        KT = K // 128
        nblk = len(col_blocks)
        wb = self.dram_tmp(name + "_bf", [nblk, 128, KT, bw], BF16)
        bufs = []
        wv = w_ap.rearrange("(kt p) n -> p kt n", p=128)
        b = k.buf("wbf", group if group is not None else name)
        for bi, ranges in enumerate(col_blocks):
            off = 0
            for (c0, ncol) in ranges:
                k.dma("pool", wb[bi, :, :, off:off + ncol], wv[:, :, c0:c0 + ncol], owner=b)
                off += ncol
            bufs.append(b)
        b.writer = ("d", b)
        return wb, bufs

    def ring_init(self):
        self.ring = self.sb("wring", [128, NRING, RING_SLOT], BF16)
        self.ring_bufs = [self.k.buf("ring", i) for i in range(NRING)]
        self.ring_pos = 0
        self.ring_queue = []
        self.ring_loaded = []

    def ring_plan(self, blocks):
        self.ring_queue.extend(blocks)

    def _ring_issue_one(self):
        if not self.ring_queue:
            return False
        ap, srcbuf, a, b = self.ring_queue.pop(0)
        slot = self.ring_pos % NRING
        self.ring_pos += 1
        rb = self.ring_bufs[slot]
        view = self.ring[:, slot, 0:a * b].rearrange("p (a b) -> p a b", a=a)
        self.k.dma("sp", view, ap, owner=rb, reads=[srcbuf], writes=[rb])
        self.ring_loaded.append((slot, view))
        return True

    def ring_next(self):
        while len(self.ring_loaded) < NRING - 1 and self._ring_issue_one():
            pass
        if not self.ring_loaded:
            self._ring_issue_one()
        slot, view = self.ring_loaded.pop(0)
        return self.ring_bufs[slot], view

    def ring_prefetch(self):
        while len(self.ring_loaded) < NRING - 1 and self._ring_issue_one():
            pass


def default_plan(layers=(0, 1, 2, 3)):
    plan = []
    for L in layers:
        plan.append(("ffn", L, 0))
        plan.append((("nsa", "conv", "gla")[L % 3], L))
        plan.append(("ffn", L, 1))
    return plan


def build_program(nseq=2, layers=(0, 1, 2, 3), plan=None):
    if plan is None:
        plan = default_plan(layers)
    P = Prog(nseq=nseq, layers=layers)
    nc, k = P.nc, P.k
    NT = P.NT
    NTT = NT // TT

    x_in = P.dram_in("x", [NT, D])
    out_ap = nc.dram_tensor("out", [NT, D], F32, kind="ExternalOutput").ap()
    ln_gb = P.dram_in("ln_gb", [128, DEPTH * 3 * 2 * NDT])
    ident_in = P.dram_in("ident", [128, 128])
    ffn_win = {}
    ffn_wout = {}
    for p_ in (plan or []):
        if p_[0] == "ffn":
            L, j = p_[1], p_[2]
            ffn_win[(L, j)] = P.dram_in("ffn_w_in_%d_%d" % (L, j), [D, 2 * DFF])
            ffn_wout[(L, j)] = P.dram_in("ffn_w_out_%d_%d" % (L, j), [DFF, D])
    if any(p_[0] == "conv" for p_ in plan):
        conv_win = P.dram_in("conv_w_in", [D, 3 * D])
        conv_wout = P.dram_in("conv_w_out", [D, D])
        convw_in = P.dram_in("conv_w", [128, 3 * NDT])

    if any(p_[0] == "gla" for p_ in plan):
        gla_win = P.dram_in("gla_w_in", [D, 6160])
        gla_wout = P.dram_in("gla_w_out", [D, D])
        gla_const_in = P.dram_in("gla_const", [128, 256])
        gla_wa2_in = P.dram_in("gla_wa2", [17, 1024])
        gla_ng_in = P.dram_in("gla_ng", [64, 512])
        gla_cm_in = P.dram_in("gla_cm", [64, 64])
    nsa_in = {}
    if any(p_[0] == "nsa" for p_ in plan):
        for p_ in plan:
            if p_[0] == "nsa":
                jn = p_[1] // 3
                nsa_in[jn] = dict(
                    w_in=P.dram_in("nsa_w_in_%d" % jn, [D, 5168]),
                    w_out=P.dram_in("nsa_w_out_%d" % jn, [D, D]),
                    w1=P.dram_in("nsa_w1_%d" % jn, [2, 4096, 128]),
                    w2=P.dram_in("nsa_w2_%d" % jn, [2, 128, 128]),
                    posT=P.dram_in("nsa_posT_%d" % jn, [128, 64]),
                    gate_b=P.dram_in("nsa_gateb_%d" % jn, [128, 48]),
                )
        nsa_pos_in = P.dram_in("nsa_pos", [128, NT], I32)
        nsa_invs_in = P.dram_in("nsa_invs", [128, 2])
        nsa_validc_in = P.dram_in("nsa_validc", [128, T], BF16)
        nsa_cam_in = P.dram_in("nsa_cam", [128, 256], BF16)
        nsa_addc_in = P.dram_in("nsa_addc", [128, 512])
        nsa_esel_in = P.dram_in("nsa_esel", [128, 2048], BF16)
        nsa_ovl_in = P.dram_in("nsa_ovl", [128, 33], BF16)
    XT = [P.dram_tmp("XT%d" % i, [D, NT], F32) for i in range(2)]
    XTb = [P.dram_tmp("XTb%d" % i, [D, NT], BF16) for i in range(2)]

    ident = P.sb("ident", [128, 128], F32)
    ones = P.sb("ones", [128, 128], F32)
    lngb = P.sb("lngb", [128, DEPTH * 3 * 2 * NDT], F32)
    b_const = k.buf("const")
    k.dma("sp", ident[:], ident_in[:, :], owner=b_const, writes=[b_const])
    k.dma("sp", lngb[:], ln_gb[:, :], owner=b_const, writes=[b_const])
    if any(p_[0] == "conv" for p_ in plan):
        convw = P.sb("convw", [128, 3 * NDT], F32)
        k.dma("sp", convw[:], convw_in[:, :], owner=b_const, writes=[b_const])
    b_ones = k.buf("ones")
    k.op("dve", lambda: nc.vector.memset(ones[:], 1.0), writes=[b_ones])
    onesb = P.sb("onesb", [128, 128], BF16)
    k.op("dve", lambda: nc.vector.memset(onesb[:], 1.0), writes=[b_ones])
    P.ring_init()

    conv = {}

    def conv_ffn(L, j):
        blocks_in = [[(nb * 256, 256), (DFF + nb * 256, 256)] for nb in range(DFF // 256)]
        wb_in, bufs_in = P.convert_weight("fin_%d_%d" % (L, j), ffn_win[(L, j)], D, blocks_in, 512, group="ffn%d%d" % (L, j))
        blocks_out = [[(db * 256, 256)] for db in range(D // 256)]
        wb_out, bufs_out = P.convert_weight("fout_%d_%d" % (L, j), ffn_wout[(L, j)], DFF, blocks_out, 256, group="ffn%d%d" % (L, j))
        conv[("ffn", L, j)] = (wb_in, bufs_in, wb_out, bufs_out)

    def conv_conv(L):
        blocks_in = [[(d * 128, 128), (D + d * 128, 128), (2 * D + d * 128, 128)] for d in range(NDT)]
        wb_in, bufs_in = P.convert_weight("cin_%d" % L, conv_win, D, blocks_in, 384, group="conv")
        blocks_out = [[(db * 256, 256)] for db in range(D // 256)]
        wb_out, bufs_out = P.convert_weight("cout_%d" % L, conv_wout, D, blocks_out, 256, group="conv")
        conv[("conv", L)] = (wb_in, bufs_in, wb_out, bufs_out)

    def conv_gla(L):
        cwd = {}
        cwd["a"] = P.convert_weight("ga_%d" % L, gla_win, D, [[(6144, 16)]], 16, group="gla")
        cwd["fm"] = P.convert_weight("gfm_%d" % L, gla_win, D, [[(i * 256, 256), (1024 + i * 256, 256)] for i in range(4)], 512, group="gla")
        tmb = [[(1024 + i * 512, 512)] for i in range(2)] + [[(2048 + i * 512, 512)] for i in range(4)] + [[(4096 + i * 512, 512)] for i in range(4)]
        cwd["tm"] = P.convert_weight("gtm_%d" % L, gla_win, D, tmb, 512, group="gla")
        cwd["out"] = P.convert_weight("gout_%d" % L, gla_wout, D, [[(db * 256, 256)] for db in range(D // 256)], 256, group="gla")
        conv[("gla", L)] = cwd

    def conv_nsa(L):
        jn = L // 3
        w = nsa_in[jn]["w_in"]
        cwd = {}
        tiles = [h * 128 for h in range(16)] + [2048 + g * 128 for g in range(4)] + [3072 + g * 128 for g in range(4)] + [4096 + g * 128 for g in range(4)]
        cwd["rope"] = P.convert_weight("nrope_%d" % L, w, D, [[(c0, 128), (c0 + 64, 64), (c0, 64)] for c0 in tiles], 256, group="nsa%d" % L)
        cwd["vc"] = P.convert_weight("nvc_%d" % L, w, D, [[(2560, 512)]], 512, group="nsa%d" % L)
        cwd["tm"] = P.convert_weight("ntm_%d" % L, w, D, [[(3584, 512)], [(4608, 512)]], 512, group="nsa%d" % L)
        cwd["gl"] = P.convert_weight("ngl_%d" % L, w, D, [[(5120, 48)]], 48, group="nsa%d" % L)
        cwd["out"] = P.convert_weight("nout_%d" % L, nsa_in[jn]["w_out"], D, [[(db * 256, 256)] for db in range(D // 256)], 256, group="nsa%d" % L)
        conv[("nsa", L)] = cwd

    def conv_phase(p_):
        if p_[0] == "nsa":
            conv_nsa(p_[1])
        elif p_[0] == "ffn":
            conv_ffn(p_[1], p_[2])
        elif p_[0] == "conv":
            conv_conv(p_[1])
        elif p_[0] == "gla":
            conv_gla(p_[1])

    bsmall = k.buf("wbf", "small")
    small = {}
    for p_ in plan:
        if p_[0] == "nsa":
            jn = p_[1] // 3
            d_ = {}
            d_["w1"] = P.dram_tmp("nsa_w1b_%d" % jn, [2, 4096, 128], BF16)
            d_["w2"] = P.dram_tmp("nsa_w2b_%d" % jn, [2, 128, 128], BF16)
            d_["posT"] = P.dram_tmp("nsa_posTb_%d" % jn, [128, 64], BF16)
            for kv in range(2):
                k.dma("pool", d_["w1"][kv].rearrange("(a b) j -> a b j", a=128), nsa_in[jn]["w1"][kv].rearrange("(a b) j -> a b j", a=128), owner=bsmall)
                k.dma("pool", d_["w2"][kv], nsa_in[jn]["w2"][kv], owner=bsmall)
            k.dma("pool", d_["posT"][:, :], nsa_in[jn]["posT"][:, :], owner=bsmall)
            small[("nsa", jn)] = d_
        elif p_[0] == "gla":
            d_ = {"wa2": P.dram_tmp("gla_wa2b", [17, 1024], BF16)}
            k.dma("pool", d_["wa2"][:, :], gla_wa2_in[:, :], owner=bsmall)
            small["gla"] = d_
    bsmall.writer = ("d", bsmall)
    NAHEAD = 3
    for p_ in plan[:NAHEAD]:
        conv_phase(p_)

    def phase_in_transpose(dst):
        m = P.mark()
        xin = [P.sb("xin%d" % i, [128, D], F32) for i in range(2)]
        stg = [P.sb("stg%d" % i, [128, NDT, TT], F32) for i in range(2)]
        stgb = [P.sb("stgb%d" % i, [128, NDT, TT], BF16) for i in range(2)]
        pst = [P.ps("pst%d" % i, [128, 512], F32) for i in range(4)]
        bx = [k.buf("xin", i) for i in range(2)]
        bs = [k.buf("stg", i) for i in range(2)]
        bsb = [k.buf("stgb", i) for i in range(2)]
        bp = [k.buf("pst", i) for i in range(4)]
        cnt = 0
        pcnt = 0
        for tt in range(NTT):
            s = tt % 2
            for sub in range(TT // 128):
                t0 = tt * TT + sub * 128
                xi = cnt % 2
                cnt += 1
                k.dma("sp", xin[xi][:], x_in[t0:t0 + 128, :], owner=bx[xi], writes=[bx[xi]])
                for g in range(NDT // 4):
                    pb = pcnt % 4
                    pcnt += 1
                    for q in range(4):
                        dt_ = g * 4 + q
                        k.op("pe", lambda: nc.tensor.transpose(pst[pb][:, q * 128:(q + 1) * 128],
                                                               xin[xi][:, dt_ * 128:(dt_ + 1) * 128], ident[:]),
                             reads=[bx[xi], b_const], writes=[bp[pb]])
                    dst_v = stg[s][:, g * 4:(g + 1) * 4, sub * 128:(sub + 1) * 128]
                    dst_b = stgb[s][:, g * 4:(g + 1) * 4, sub * 128:(sub + 1) * 128]
                    src_v = pst[pb][:, :].rearrange("p (q t) -> p q t", q=4)
                    k.op("dve", lambda: nc.vector.tensor_copy(out=dst_v, in_=src_v), reads=[bp[pb]], writes=[bs[s]])
                    if os.environ.get("NO_BF16") != "1":
                        for q in range(4):
                            k.op("act", lambda: nc.scalar.copy(out=stgb[s][:, g * 4 + q, sub * 128:(sub + 1) * 128],
                                                               in_=pst[pb][:, q * 128:(q + 1) * 128]), reads=[bp[pb]], writes=[bsb[s]])
            k.dma("sp", XT[dst].rearrange("(dt p) t -> p dt t", p=128)[:, :, tt * TT:(tt + 1) * TT], stg[s][:],
                  owner=bs[s], reads=[bs[s]], writes=[k.buf("XT", dst, tt)])
            if not os.environ.get("NO_BF16"):
                k.dma("sp", XTb[dst].rearrange("(dt p) t -> p dt t", p=128)[:, :, tt * TT:(tt + 1) * TT], stgb[s][:],
                      owner=bsb[s], reads=[bsb[s]], writes=[k.buf("XTb", dst, tt)])
        k.barrier()
        P.release(m)

    def phase_out_transpose(src):
        m = P.mark()
        xin = [P.sb("oin%d" % i, [128, NDT, TT], F32) for i in range(2)]
        stg = [P.sb("ostg%d" % i, [128, D], F32) for i in range(2)]
        pst = [P.ps("opst%d" % i, [128, 512], F32) for i in range(4)]
        bx = [k.buf("oin", i) for i in range(2)]
        bs = [k.buf("ostg", i) for i in range(2)]
        bp = [k.buf("opst", i) for i in range(4)]
        cnt = 0
        pcnt = 0
        for tt in range(NTT):
            s = tt % 2
            k.dma("sp", xin[s][:], XT[src].rearrange("(dt p) t -> p dt t", p=128)[:, :, tt * TT:(tt + 1) * TT],
                  owner=bx[s], reads=[k.buf("XT", src, tt)], writes=[bx[s]])
            for sub in range(TT // 128):
                t0 = tt * TT + sub * 128
                si = cnt % 2
                cnt += 1
                for g in range(NDT // 4):
                    pb = pcnt % 4
                    pcnt += 1
                    for q in range(4):
                        dt_ = g * 4 + q
                        k.op("pe", lambda: nc.tensor.transpose(pst[pb][:, q * 128:(q + 1) * 128],
                                                               xin[s][:, dt_, sub * 128:(sub + 1) * 128], ident[:]),
                             reads=[bx[s], b_const], writes=[bp[pb]])
                    dst_v = stg[si][:, g * 512:(g + 1) * 512]
                    if g % 2 == 0:
                        k.op("dve", lambda: nc.vector.tensor_copy(out=dst_v, in_=pst[pb][:, :]), reads=[bp[pb]], writes=[bs[si]])
                    else:
                        k.op("act", lambda: nc.scalar.copy(out=dst_v, in_=pst[pb][:, :]), reads=[bp[pb]], writes=[bs[si]])
                k.dma("sp", out_ap[t0:t0 + 128, :], stg[si][:], owner=bs[si], reads=[bs[si]], writes=[k.buf("out", t0)])
        k.barrier()
        P.release(m)

    def ln_cols(L, j):
        base = ((L * 3 + j) * 2) * NDT
        return base, base + NDT

    def phase_rowlocal(L, lnj, src, dst, KT, c_res, wout, stage_blocks, stage_alloc, stage_run, need_xT=True):
        wb_out, bufs_out = wout
        NBO = D // 256
        blocks = []
        for tt in range(NTT):
            blocks.extend(stage_blocks(tt))
            for db in range(NBO):
                blocks.append((wb_out[db], bufs_out[db], KT, 256))
        P.ring_plan(blocks)
        m = P.mark()
        C = {}
        nz = 1 if need_xT else 2
        zs = [P.sb("z%d" % i, [128, NDT, TT], F32) for i in range(nz)]
        xT = P.sb("xT", [128, NDT, TT], BF16) if need_xT else None
        hT = P.sb("hT", [128, KT, TT], BF16)
        zsq = [P.sb("zsq%d" % i, [128, TT], F32) for i in range(2)]
        spl = [[P.sb("spl%d_%d" % (i, j), [128, TT], BF16) for j in range(4)] for i in range(2)]
        bspl = [[k.buf("spl", i, j) for j in range(4)] for i in range(2)]
        zb = [P.sb("zb%d" % i, [128, TT], BF16) for i in range(2)]
        mean = P.sb("mean", [128, TT], F32)
        rstd = P.sb("rstd", [128, TT], F32)
        tmpv = P.sb("tmpv", [128, TT], F32)
        pin = [P.ps("pin%d" % i, [128, TT]) for i in range(4)]
        py = [P.ps("py%d" % i, [128, TT]) for i in range(2)]
        ps1 = P.ps("ps1", [128, TT])
        ps2 = P.ps("ps2", [128, TT])
        bzs = [[k.buf("z", i, d) for d in range(NDT)] for i in range(nz)]
        bxT = k.buf("xT")
        bh = [k.buf("hT", n) for n in range(KT)]
        bzsq = [k.buf("zsq", i) for i in range(2)]
        bzb = [k.buf("zb", i) for i in range(2)]
        bzls = [[k.buf("zl", j, i) for i in range(4)] for j in range(nz)]
        bzalls = [k.buf("zstore", j) for j in range(nz)]
        bmean, brstd, btmpv = k.buf("mean"), k.buf("rstd"), k.buf("tmpv")
        bpin = [k.buf("pin", i) for i in range(4)]
        bpy = [k.buf("py", i) for i in range(2)]
        bps1, bps2 = k.buf("ps1"), k.buf("ps2")
        gcol, bcol = ln_cols(L, lnj)
        eps_p = LN_EPS / (ALPHA * ALPHA)
        XTs = XT[src].rearrange("(dt p) t -> p dt t", p=128)
        XTd = XT[dst].rearrange("(dt p) t -> p dt t", p=128)
        XTbs = XTb[src].rearrange("(dt p) t -> p dt t", p=128)
        XTbd = XTb[dst].rearrange("(dt p) t -> p dt t", p=128)
        C.update(xT=xT, bxT=bxT, hT=hT, bh=bh, pin=pin, bpin=bpin, rot=0)
        stage_alloc(C)
        dq = []

        def pump(n):
            while n > 0 and dq:
                dq.pop(0)[1]()
                n -= 1

        def flush_upto(tile):
            while dq and dq[0][0] <= tile:
                dq.pop(0)[1]()

        C["pump"] = pump

        def load_xT(tt):
            if need_xT:
                k.dma("sp", xT[:], XTbs[:, :, tt * TT:(tt + 1) * TT], owner=bxT,
                      reads=[k.buf("XTb", src, tt)], writes=[bxT])

        def load_z(tt):
            flush_upto(tt - nz)
            z, bz, bzl = zs[tt % nz], bzs[tt % nz], bzls[tt % nz]
            for g4 in range(4):
                k.dma("sp", z[:, g4 * 4:(g4 + 1) * 4, :], XTs[:, g4 * 4:(g4 + 1) * 4, tt * TT:(tt + 1) * TT], owner=bzl[g4],
                      reads=[k.buf("XT", src, tt)], writes=bz[g4 * 4:(g4 + 1) * 4])

        def make_epilogue(tt):
            tsl = slice(tt * TT, (tt + 1) * TT)
            z, bz, bzall = zs[tt % nz], bzs[tt % nz], bzalls[tt % nz]
            items = []

            def add(fn):
                items.append((tt, fn))

            add(lambda: k.op("dve", lambda: nc.vector.tensor_scalar(out=mean[:], in0=ps1[:], scalar1=1.0 / D, scalar2=None, op0=ALU.mult),
                             reads=[bps1], writes=[bmean]))
            add(lambda: k.op("dve", lambda: nc.vector.tensor_tensor(out=tmpv[:], in0=mean[:], in1=mean[:], op=ALU.mult),
                             reads=[bmean], writes=[btmpv]))
            add(lambda: k.op("dve", lambda: nc.vector.scalar_tensor_tensor(out=tmpv[:], in0=ps2[:], scalar=1.0 / D, in1=tmpv[:],
                                                                           op0=ALU.mult, op1=ALU.subtract),
                             reads=[bps2, btmpv], writes=[btmpv]))
            add(lambda: k.op("dve", lambda: nc.vector.tensor_scalar(out=tmpv[:], in0=tmpv[:], scalar1=eps_p, scalar2=None, op0=ALU.add),
                             reads=[btmpv], writes=[btmpv]))
            add(lambda: k.op("act", lambda: nc.scalar.activation(out=rstd[:], in_=tmpv[:], func=AF.Sqrt),
                             reads=[btmpv], writes=[brstd]))
            add(lambda: k.op("dve", lambda: nc.vector.reciprocal(out=rstd[:], in_=rstd[:]), reads=[brstd], writes=[brstd]))
            for d in range(NDT):
                zi = d % 2
                add(lambda d=d: k.op("dve", lambda: nc.vector.tensor_tensor(out=z[:, d, :], in0=z[:, d, :], in1=mean[:], op=ALU.subtract),
                                     reads=[bz[d], bmean], writes=[bz[d]]))
                add(lambda d=d: k.op("dve", lambda: nc.vector.tensor_tensor(out=z[:, d, :], in0=z[:, d, :], in1=rstd[:], op=ALU.mult),
                                     reads=[bz[d], brstd], writes=[bz[d]]))
                add(lambda d=d: k.op("act", lambda: nc.scalar.activation(out=z[:, d, :], in_=z[:, d, :], func=AF.Identity,
                                                                         scale=lngb[:, gcol + d:gcol + d + 1], bias=lngb[:, bcol + d:bcol + d + 1]),
                                     reads=[bz[d], b_const], writes=[bz[d]]))
                add(lambda d=d, zi=zi: k.op("act", lambda: nc.scalar.copy(out=zb[zi][:], in_=z[:, d, :]), reads=[bz[d]], writes=[bzb[zi]]))
                add(lambda d=d, zi=zi: k.dma("act", XTbd[:, d, tsl], zb[zi][:], owner=bzb[zi], reads=[bzb[zi]], writes=[k.buf("XTb", dst, tt)]))
            add(lambda: k.dma("act", XTd[:, :, tsl], z[:], owner=bzall, reads=bz, writes=[k.buf("XT", dst, tt)]))
            return items

        def stats_mm(pd, pyi):
            k.op("pe", lambda: nc.tensor.matmul(ps1[:], lhsT=onesb[:], rhs=spl[pyi][0][:], start=(pd == 0), stop=False),
                 reads=[b_ones, bspl[pyi][0]], writes=[bps1])
            k.op("pe", lambda: nc.tensor.matmul(ps1[:], lhsT=onesb[:], rhs=spl[pyi][1][:], start=False, stop=(pd == NDT - 1)),
                 reads=[b_ones, bspl[pyi][1]], writes=[bps1])
            k.op("pe", lambda: nc.tensor.matmul(ps2[:], lhsT=onesb[:], rhs=spl[pyi][2][:], start=(pd == 0), stop=False),
                 reads=[b_ones, bspl[pyi][2]], writes=[bps2])
            k.op("pe", lambda: nc.tensor.matmul(ps2[:], lhsT=onesb[:], rhs=spl[pyi][3][:], start=False, stop=(pd == NDT - 1)),
                 reads=[b_ones, bspl[pyi][3]], writes=[bps2])

        C["load_z"] = load_z
        load_xT(0)
        cy = 0
        for tt in range(NTT):
            C["z_loaded"] = False
            z, bz = zs[tt % nz], bzs[tt % nz]
            stage_run(tt, C)
            if not C["z_loaded"]:
                load_z(tt)
            if tt + 1 < NTT:
                load_xT(tt + 1)
            pend = None
            for db in range(NBO):
                wbuf, wv = P.ring_next()
                for jd in range(2):
                    d = db * 2 + jd
                    yi = cy % 2
                    cy += 1
                    for kk in range(KT):
                        k.op("pe", lambda: nc.tensor.matmul(py[yi][:], lhsT=wv[:, kk, jd * 128:(jd + 1) * 128],
                                                            rhs=hT[:, kk, :], start=(kk == 0), stop=(kk == KT - 1)),
                             reads=[wbuf, bh[kk]], writes=[bpy[yi]])
                    k.op("dve", lambda: nc.vector.scalar_tensor_tensor(out=z[:, d, :], in0=py[yi][:], scalar=c_res,
                                                                       in1=z[:, d, :], op0=ALU.mult, op1=ALU.add),
                         reads=[bpy[yi], bz[d]], writes=[bz[d]])
                    k.op("act", lambda: nc.scalar.activation(out=zsq[yi][:], in_=z[:, d, :], func=AF.Square),
                         reads=[bz[d]], writes=[bzsq[yi]])
                    k.op("act", lambda: nc.scalar.copy(out=spl[yi][0][:], in_=z[:, d, :]), reads=[bz[d]], writes=[bspl[yi][0]])
                    k.op("dve", lambda: nc.vector.tensor_tensor(out=spl[yi][1][:], in0=z[:, d, :], in1=spl[yi][0][:], op=ALU.subtract),
                         reads=[bz[d], bspl[yi][0]], writes=[bspl[yi][1]])
                    k.op("act", lambda: nc.scalar.copy(out=spl[yi][2][:], in_=zsq[yi][:]), reads=[bzsq[yi]], writes=[bspl[yi][2]])
                    k.op("dve", lambda: nc.vector.tensor_tensor(out=spl[yi][3][:], in0=zsq[yi][:], in1=spl[yi][2][:], op=ALU.subtract),
                         reads=[bzsq[yi], bspl[yi][2]], writes=[bspl[yi][3]])
                    if pend is not None:
                        pd, pyi = pend
                        stats_mm(pd, pyi)
                    pend = (d, yi)
                    if not need_xT:
                        pump(6)
            pd, pyi = pend
            stats_mm(pd, pyi)
            P.ring_prefetch()
            dq.extend(make_epilogue(tt))
            if tt == NTT - 1:
                flush_upto(tt)
        k.barrier()
        P.release(m)

    def run_ffn(L, j, src, dst):
        wb_in, bufs_in, wb_out, bufs_out = conv[("ffn", L, j)]
        NBI = DFF // 256

        def blocks(tt):
            return [(wb_in[nb], bufs_in[nb], 16, 512) for nb in range(NBI)]

        def alloc(C):
            C["sg"] = [P.sb("sg%d" % i, [128, TT], F32) for i in range(2)]
            C["bsg"] = [k.buf("sg", i) for i in range(2)]
            C["cg"] = 0

        def run(tt, C):
            xT, bxT, hT, bh, pin, bpin, sg, bsg = (C[n] for n in ("xT", "bxT", "hT", "bh", "pin", "bpin", "sg", "bsg"))
            for nb in range(NBI):
                wbuf, wv = P.ring_next()
                if nb == 8:
                    C["load_z"](tt)
                    C["z_loaded"] = True
                for jn in range(2):
                    n = nb * 2 + jn
                    pi = C["cg"] % 2
                    C["cg"] += 1
                    pgb, pub = pin[pi], pin[2 + pi]
                    for kk in range(NDT):
                        k.op("pe", lambda: nc.tensor.matmul(pgb[:], lhsT=wv[:, kk, jn * 128:(jn + 1) * 128],
                                                            rhs=xT[:, kk, :], start=(kk == 0), stop=(kk == NDT - 1)),
                             reads=[wbuf, bxT], writes=[bpin[pi]])
                    for kk in range(NDT):
                        k.op("pe", lambda: nc.tensor.matmul(pub[:], lhsT=wv[:, kk, 256 + jn * 128:256 + (jn + 1) * 128],
                                                            rhs=xT[:, kk, :], start=(kk == 0), stop=(kk == NDT - 1)),
                             reads=[wbuf, bxT], writes=[bpin[2 + pi]])
                    k.op("act", lambda: nc.scalar.activation(out=sg[pi][:], in_=pgb[:], func=AF.Silu),
                         reads=[bpin[pi]], writes=[bsg[pi]])
                    k.op("dve", lambda: nc.vector.tensor_tensor(out=hT[:, n, :], in0=pub[:], in1=sg[pi][:], op=ALU.mult),
                         reads=[bpin[2 + pi], bsg[pi]], writes=[bh[n]])
                    C["pump"](8)

        phase_rowlocal(L, 0 if j == 0 else 2, src, dst, NHT, 0.5 / ALPHA, (wb_out, bufs_out), blocks, alloc, run)

    def run_conv(L, src, dst):
        wb_in, bufs_in, wb_out, bufs_out = conv[("conv", L)]

        def blocks(tt):
            return [(wb_in[d], bufs_in[d], 16, 384) for d in range(NDT)]

        def alloc(C):
            C["csb"] = [P.sb("csb%d" % i, [128, TT], F32) for i in range(2)]
            C["u"] = [P.sb("u%d" % i, [128, TT + 2], F32) for i in range(2)]
            C["yy"] = [P.sb("yy%d" % i, [128, TT], F32) for i in range(2)]
            C["uh"] = P.sb("uh", [128, NDT, 2], F32)
            C["bcsb"] = [k.buf("csb", i) for i in range(2)]
            C["bu"] = [k.buf("u", i) for i in range(2)]
            C["byy"] = [k.buf("yy", i) for i in range(2)]
            C["buh"] = [k.buf("uh", d) for d in range(NDT)]
            C["ci"] = 0

        def run(tt, C):
            xT, bxT, hT, bh, pin, bpin = (C[n] for n in ("xT", "bxT", "hT", "bh", "pin", "bpin"))
            seq_start = (tt % (T // TT) == 0)
            for d in range(NDT):
                wbuf, wv = P.ring_next()
                if d == 4:
                    C["load_z"](tt)
                    C["z_loaded"] = True
                i2 = C["ci"] % 2
                C["ci"] += 1
                banks = []
                for which in (1, 2, 0):
                    r = C["rot"] % 4
                    C["rot"] += 1
                    for kk in range(NDT):
                        k.op("pe", lambda: nc.tensor.matmul(pin[r][:], lhsT=wv[:, kk, which * 128:(which + 1) * 128],
                                                            rhs=xT[:, kk, :], start=(kk == 0), stop=(kk == NDT - 1)),
                             reads=[wbuf, bxT], writes=[bpin[r]])
                    banks.append(r)
                rc, rh, rb = banks
                csb, u, yy = C["csb"][i2], C["u"][i2], C["yy"][i2]
                bcsb, bu, byy, buh = C["bcsb"][i2], C["bu"][i2], C["byy"][i2], C["buh"][d]
                uh = C["uh"]
                k.op("act", lambda: nc.scalar.copy(out=csb[:], in_=pin[rc][:]), reads=[bpin[rc]], writes=[bcsb])
                k.op("dve", lambda: nc.vector.tensor_tensor(out=u[:, 2:TT + 2], in0=pin[rh][:], in1=csb[:], op=ALU.mult),
                     reads=[bpin[rh], bcsb], writes=[bu])
                if seq_start:
                    k.op("dve", lambda: nc.vector.memset(u[:, 0:2], 0.0), reads=[], writes=[bu])
                else:
                    k.op("dve", lambda: nc.vector.tensor_copy(out=u[:, 0:2], in_=uh[:, d, :]), reads=[buh], writes=[bu])
                k.op("dve", lambda: nc.vector.tensor_copy(out=uh[:, d, :], in_=u[:, TT:TT + 2]), reads=[bu], writes=[buh])
                k.op("dve", lambda: nc.vector.tensor_scalar(out=yy[:], in0=u[:, 2:TT + 2], scalar1=convw[:, 2 * NDT + d:2 * NDT + d + 1],
                                                            scalar2=None, op0=ALU.mult),
                     reads=[bu, b_const], writes=[byy])
                k.op("dve", lambda: nc.vector.scalar_tensor_tensor(out=yy[:], in0=u[:, 1:TT + 1], scalar=convw[:, NDT + d:NDT + d + 1],
                                                                   in1=yy[:], op0=ALU.mult, op1=ALU.add),
                     reads=[bu, byy, b_const], writes=[byy])
                k.op("dve", lambda: nc.vector.scalar_tensor_tensor(out=yy[:], in0=u[:, 0:TT], scalar=convw[:, d:d + 1],
                                                                   in1=yy[:], op0=ALU.mult, op1=ALU.add),
                     reads=[bu, byy, b_const], writes=[byy])
                k.op("dve", lambda: nc.vector.tensor_tensor(out=hT[:, d, :], in0=pin[rb][:], in1=yy[:], op=ALU.mult),
                     reads=[bpin[rb], byy], writes=[bh[d]])
                C["pump"](24)

        phase_rowlocal(L, 1, src, dst, NDT, 1.0 / ALPHA, (wb_out, bufs_out), blocks, alloc, run)

    def run_outproj_from_dram(L, src, dst, OT, otname, wout):
        OTv = OT.rearrange("(dt p) t -> p dt t", p=128)

        def blocks(tt):
            return []

        def alloc(C):
            C["bhl"] = k.buf("hload")

        def run(tt, C):
            k.dma("sp", C["hT"][:, :, :], OTv[:, :, tt * TT:(tt + 1) * TT], owner=C["bhl"],
                  reads=[k.buf(otname, tt)], writes=C["bh"])

        phase_rowlocal(L, 1, src, dst, NDT, 1.0 / ALPHA, wout, blocks, alloc, run, need_xT=False)

    def run_gla(L, src, dst):
        cw = conv[("gla", L)]
        NCH = NT // 64
        QD = P.dram_tmp("gla_QD", [1024, NT], BF16)
        KD = P.dram_tmp("gla_KD", [1024, NT], BF16)
        KL = P.dram_tmp("gla_KL", [NT, 1024], BF16)
        VV = P.dram_tmp("gla_V", [NT, 2048], BF16)
        RS = P.dram_tmp("gla_RS", [NT, 2048], F32)
        OT = P.dram_tmp("gla_OT", [D, NT], BF16)
        m0 = P.mark()
        dec = P.sb("gdec", [128, 8, NCH], F32)
        bdec = k.buf("gdec")
        gc = P.sb("gconst", [128, 256], F32)
        wa2 = P.sb("gwa2", [32, 1024], BF16)
        gng = P.sb("gng", [64, 512], F32)
        cmask = P.sb("gcm", [64, 64], F32)
        identb = P.sb("gidb", [128, 128], BF16)
        bgc = k.buf("gconst")
        k.dma("sp", gc[:], gla_const_in[:, :], owner=bgc, writes=[bgc])
        bgc2 = k.buf("gconst2")
        k.dma("sp", wa2[0:17, :], small["gla"]["wa2"][:, :], owner=bgc2, reads=[bsmall], writes=[bgc2])
        k.dma("sp", gng[:], gla_ng_in[:, :], owner=bgc, writes=[bgc])
        k.dma("sp", cmask[:], gla_cm_in[:, :], owner=bgc, writes=[bgc])
        k.op("act", lambda: nc.scalar.copy(out=identb[:], in_=ident[:]), reads=[b_const], writes=[bgc])
        tri16 = gc[:, 0:128]
        m16 = gc[:, 128:256]

        wfm, bfm = cw["fm"]
        wa, ba = cw["a"]
        wtm, btm = cw["tm"]
        blocks = []
        for tt in range(NTT):
            blocks.append((wa[0], ba[0], 16, 16))
            for i in range(4):
                blocks.append((wfm[i], bfm[i], 16, 512))
            for i in range(10):
                blocks.append((wtm[i], btm[i], 16, 512))
        P.ring_plan(blocks)
        m = P.mark()
        xT = P.sb("xT", [128, NDT, TT], BF16)
        bxT = k.buf("xT")
        gk = P.sb("gk", [128, 4, 1024], F32)
        bgk = k.buf("gk")
        aT = P.sb("aT", [32, TT], BF16)
        baT = k.buf("aT")
        k.op("dve", lambda: nc.vector.memset(aT[:], 1.0), writes=[baT])
        tmpe = [P.sb("tmpe%d" % i, [128, TT], F32) for i in range(6)]
        btmpe = [k.buf("tmpe", i) for i in range(6)]
        qd = P.sb("qd", [128, 8, TT], BF16)
        kd = P.sb("kd", [128, 8, TT], BF16)
        klst = P.sb("klst", [128, 4, 1024], BF16)
        vst = P.sb("vst", [128, 4, 2048], BF16)
        rst = [P.sb("rst%d" % i, [128, 4, 512], F32) for i in range(2)]
        bqd, bkd, bklst, bvst = k.buf("qd"), k.buf("kd"), k.buf("klst"), k.buf("vst")
        brst = [k.buf("rst", i) for i in range(2)]
        banks = [P.ps("gb%d" % i, [128, 512]) for i in range(8)]
        bbanks = [k.buf("gb", i) for i in range(8)]
        st = {"r": 0, "e": 0}

        def bank():
            i = st["r"] % 8
            st["r"] += 1
            return banks[i], bbanks[i]

        def tmp():
            i = st["e"] % 6
            st["e"] += 1
            return tmpe[i], btmpe[i]

        XTbs = XTb[src].rearrange("(dt p) t -> p dt t", p=128)
        for tt in range(NTT):
            tsl = slice(tt * TT, (tt + 1) * TT)
            k.dma("sp", xT[:], XTbs[:, :, tsl], owner=bxT, reads=[k.buf("XTb", src, tt)], writes=[bxT])
            wbuf, wv = P.ring_next()
            pb, bpb = bank()
            for kk in range(NDT):
                k.op("pe", lambda: nc.tensor.matmul(pb[0:16, :], lhsT=wv[:, kk, 0:16], rhs=xT[:, kk, :],
                                                    start=(kk == 0), stop=(kk == NDT - 1)), reads=[wbuf, bxT], writes=[bpb])
            k.op("act", lambda: nc.scalar.copy(out=aT[0:16, :], in_=pb[0:16, :]), reads=[bpb], writes=[baT])
            for sub in range(4):
                for half in range(2):
                    pb, bpb = bank()
                    k.op("pe", lambda: nc.tensor.matmul(pb[:], lhsT=aT[0:17, sub * 128:(sub + 1) * 128],
                                                        rhs=wa2[0:17, half * 512:(half + 1) * 512], start=True, stop=True),
                         reads=[baT, bgc2], writes=[bpb])
                    k.op("act", lambda: nc.scalar.activation(out=gk[:, sub, half * 512:(half + 1) * 512], in_=pb[:], func=AF.Sigmoid),
                         reads=[bpb], writes=[bgk])
            k.op("act", lambda: nc.scalar.activation(out=gk[:], in_=gk[:], func=AF.Ln), reads=[bgk], writes=[bgk])
            for i in range(4):
                wbuf, wv = P.ring_next()
                for jj in range(2):
                    dt_ = 2 * i + jj
                    pb, bpb = bank()
                    for sub in range(4):
                        k.op("pe", lambda: nc.tensor.matmul(pb[:, sub * 128:(sub + 1) * 128], lhsT=gk[:, sub, dt_ * 128:(dt_ + 1) * 128],
                                                            rhs=tri16, start=True, stop=True), reads=[bgk, bgc], writes=[bpb])
                    ebq, bebq = tmp()
                    ebk, bebk = tmp()
                    k.op("act", lambda: nc.scalar.activation(out=ebq[:], in_=pb[:], func=AF.Exp), reads=[bpb], writes=[bebq])
                    k.op("act", lambda: nc.scalar.activation(out=ebk[:], in_=pb[:], func=AF.Exp, scale=-1.0), reads=[bpb], writes=[bebk])
                    k.op("act", lambda: nc.scalar.activation(out=dec[:, dt_, tt * 8:(tt + 1) * 8],
                                                             in_=pb[:, :].rearrange("p (c s) -> p c s", s=64)[:, :, 63], func=AF.Exp),
                         reads=[bpb], writes=[bdec])
                    pq, bpq = bank()
                    for kk in range(NDT):
                        k.op("pe", lambda: nc.tensor.matmul(pq[:], lhsT=wv[:, kk, jj * 128:(jj + 1) * 128], rhs=xT[:, kk, :],
                                                            start=(kk == 0), stop=(kk == NDT - 1)), reads=[wbuf, bxT], writes=[bpq])
                    k.op("dve", lambda: nc.vector.scalar_tensor_tensor(out=qd[:, dt_, :], in0=pq[:], scalar=1.0 / 16.0, in1=ebq[:],
                                                                       op0=ALU.mult, op1=ALU.mult), reads=[bpq, bebq], writes=[bqd])
                    pk, bpk = bank()
                    for kk in range(NDT):
                        k.op("pe", lambda: nc.tensor.matmul(pk[:], lhsT=wv[:, kk, 256 + jj * 128:256 + (jj + 1) * 128], rhs=xT[:, kk, :],
                                                            start=(kk == 0), stop=(kk == NDT - 1)), reads=[wbuf, bxT], writes=[bpk])
                    k.op("dve", lambda: nc.vector.tensor_tensor(out=kd[:, dt_, :], in0=pk[:], in1=ebk[:], op=ALU.mult),
                         reads=[bpk, bebk], writes=[bkd])
            k.dma("act", QD.rearrange("(dt p) t -> p dt t", p=128)[:, :, tsl], qd[:], owner=bqd, reads=[bqd], writes=[k.buf("gQD", tt)])
            k.dma("act", KD.rearrange("(dt p) t -> p dt t", p=128)[:, :, tsl], kd[:], owner=bkd, reads=[bkd], writes=[k.buf("gKD", tt)])
            for blk in range(2):
                wbuf, wv = P.ring_next()
                for sub in range(4):
                    pb, bpb = bank()
                    k.op("pe", lambda: nc.tensor.matmul(pb[:], lhsT=m16, rhs=gk[:, sub, blk * 512:(blk + 1) * 512], start=True, stop=True),
                         reads=[bgk, bgc], writes=[bpb])
                    kle, bkle = tmp()
                    k.op("act", lambda: nc.scalar.activation(out=kle[:], in_=pb[:], func=AF.Exp), reads=[bpb], writes=[bkle])
                    pk, bpk = bank()
                    for kk in range(NDT):
                        k.op("pe", lambda: nc.tensor.matmul(pk[:], lhsT=xT[:, kk, sub * 128:(sub + 1) * 128], rhs=wv[:, kk, :],
                                                            start=(kk == 0), stop=(kk == NDT - 1)), reads=[wbuf, bxT], writes=[bpk])
                    k.op("dve", lambda: nc.vector.tensor_tensor(out=klst[:, sub, blk * 512:(blk + 1) * 512], in0=pk[:], in1=kle[:], op=ALU.mult),
                         reads=[bpk, bkle], writes=[bklst])
            k.dma("act", KL[tsl, :].rearrange("(s p) d -> p s d", p=128), klst[:], owner=bklst, reads=[bklst], writes=[k.buf("gKL", tt)])
            for blk in range(4):
                wbuf, wv = P.ring_next()
                for sub in range(4):
                    pk, bpk = bank()
                    for kk in range(NDT):
                        k.op("pe", lambda: nc.tensor.matmul(pk[:], lhsT=xT[:, kk, sub * 128:(sub + 1) * 128], rhs=wv[:, kk, :],
                                                            start=(kk == 0), stop=(kk == NDT - 1)), reads=[wbuf, bxT], writes=[bpk])
                    k.op("act", lambda: nc.scalar.copy(out=vst[:, sub, blk * 512:(blk + 1) * 512], in_=pk[:]), reads=[bpk], writes=[bvst])
            k.dma("act", VV[tsl, :].rearrange("(s p) d -> p s d", p=128), vst[:], owner=bvst, reads=[bvst], writes=[k.buf("gV", tt)])
            for blk in range(4):
                wbuf, wv = P.ring_next()
                ri = blk % 2
                for sub in range(4):
                    pk, bpk = bank()
                    for kk in range(NDT):
                        k.op("pe", lambda: nc.tensor.matmul(pk[:], lhsT=xT[:, kk, sub * 128:(sub + 1) * 128], rhs=wv[:, kk, :],
                                                            start=(kk == 0), stop=(kk == NDT - 1)), reads=[wbuf, bxT], writes=[bpk])
                    k.op("act", lambda: nc.scalar.activation(out=rst[ri][:, sub, :], in_=pk[:], func=AF.Silu), reads=[bpk], writes=[brst[ri]])
                k.dma("act", RS[tsl, blk * 512:(blk + 1) * 512].rearrange("(s p) d -> p s d", p=128), rst[ri][:], owner=brst[ri],
                      reads=[brst[ri]], writes=[k.buf("gRS", tt, blk)])
        k.barrier()
        P.release(m)

        m = P.mark()
        TB = 256
        qdl = P.sb("qdl", [128, 8, TB], BF16)
        kdl = P.sb("kdl", [128, 8, TB], BF16)
        kll = P.sb("kll", [64, 4, 1024], BF16)
        vl = P.sb("vl", [64, 4, 2048], BF16)
        rsl = [P.sb("rsl%d" % i, [64, 2048], F32) for i in range(2)]
        S = P.sb("gS", [128, 8, 512], F32)
        Sb = P.sb("gSb", [128, 8, 512], BF16)
        atm = [P.sb("atm%d" % i, [64, 64], BF16) for i in range(2)]
        og = [P.sb("og%d" % i, [64, 512], F32) for i in range(2)]
        ogb = [P.sb("ogb%d" % i, [64, 2048], BF16) for i in range(2)]
        sq = P.sb("gsq", [64, 512], F32)
        ms = [P.sb("gms%d" % i, [64, 1], F32) for i in range(4)]
        otst = [P.sb("otst%d" % i, [128, NDT, TB], BF16) for i in range(2)]
        bqdl, bkdl, bkll, bvl = k.buf("qdl"), k.buf("kdl"), k.buf("kll"), k.buf("vl")
        brsl = [k.buf("rsl", i) for i in range(2)]
        bS = [k.buf("gS", i) for i in range(8)]
        bSb = [k.buf("gSb", i) for i in range(8)]
        batm = [k.buf("atm", i) for i in range(2)]
        bog = [k.buf("og", i) for i in range(2)]
        bogb = [k.buf("ogb", i) for i in range(2)]
        bsq = k.buf("gsq")
        bms = [k.buf("gms", i) for i in range(4)]
        botst = [k.buf("otst", i) for i in range(2)]
        banks = [P.ps("hb%d" % i, [128, 512]) for i in range(6)]
        bbanks = [k.buf("hb", i) for i in range(6)]
        tb = [P.ps("htb%d" % i, [128, 512], BF16) for i in range(2)]
        btb = [k.buf("htb", i) for i in range(2)]
        st = {"r": 0, "a": 0, "m": 0, "t": 0}

        def bank():
            i = st["r"] % 6
            st["r"] += 1
            return banks[i], bbanks[i]

        for tb_ in range(NT // TB):
            tsl = slice(tb_ * TB, (tb_ + 1) * TB)
            tt = (tb_ * TB) // TT
            oi = tb_ % 2
            if tb_ % (T // TB) == 0:
                for i in range(8):
                    k.op("dve", lambda: nc.vector.memset(S[:, i, :], 0.0), writes=[bS[i]])
                    k.op("dve", lambda: nc.vector.memset(Sb[:, i, :], 0.0), writes=[bSb[i]])
            k.dma("sp", qdl[:], QD.rearrange("(dt p) t -> p dt t", p=128)[:, :, tsl], owner=bqdl, reads=[k.buf("gQD", tt)], writes=[bqdl])
            k.dma("sp", kdl[:], KD.rearrange("(dt p) t -> p dt t", p=128)[:, :, tsl], owner=bkdl, reads=[k.buf("gKD", tt)], writes=[bkdl])
            k.dma("sp", kll[:], KL[tsl, :].rearrange("(n c) d -> c n d", c=64), owner=bkll, reads=[k.buf("gKL", tt)], writes=[bkll])
            k.dma("sp", vl[:], VV[tsl, :].rearrange("(n c) d -> c n d", c=64), owner=bvl, reads=[k.buf("gV", tt)], writes=[bvl])
            for n in range(TB // 64):
                ch = tb_ * (TB // 64) + n
                csl = slice(n * 64, (n + 1) * 64)
                ri = ch % 2
                k.dma("sp", rsl[ri][:], RS[ch * 64:(ch + 1) * 64, :], owner=brsl[ri],
                      reads=[k.buf("gRS", tt, b_) for b_ in range(4)], writes=[brsl[ri]])
                for h in range(4):
                    pa, bpa = bank()
                    for dt2 in range(2):
                        k.op("pe", lambda: nc.tensor.matmul(pa[0:64, 0:64], lhsT=kdl[:, h * 2 + dt2, csl], rhs=qdl[:, h * 2 + dt2, csl],
                                                            start=(dt2 == 0), stop=(dt2 == 1)), reads=[bkdl, bqdl], writes=[bpa])
                    ai = st["a"] % 2
                    st["a"] += 1
                    k.op("dve", lambda: nc.vector.tensor_tensor(out=atm[ai][:], in0=pa[0:64, 0:64], in1=cmask[:], op=ALU.mult),
                         reads=[bpa, bgc], writes=[batm[ai]])
                    po, bpo = bank()
                    for dt2 in range(2):
                        k.op("pe", lambda: nc.tensor.matmul(po[0:64, :], lhsT=qdl[:, h * 2 + dt2, csl], rhs=Sb[:, h * 2 + dt2, :],
                                                            start=(dt2 == 0), stop=False), reads=[bqdl, bSb[h * 2 + dt2]], writes=[bpo])
                    k.op("pe", lambda: nc.tensor.matmul(po[0:64, :], lhsT=atm[ai][:], rhs=vl[:, n, h * 512:(h + 1) * 512],
                                                        start=False, stop=True), reads=[batm[ai], bvl], writes=[bpo])
                    for dt2 in range(2):
                        si = h * 2 + dt2
                        psn, bpsn = bank()
                        k.op("pe", lambda: nc.tensor.matmul(psn[:], lhsT=kll[:, n, si * 128:(si + 1) * 128], rhs=vl[:, n, h * 512:(h + 1) * 512],
                                                            start=True, stop=True), reads=[bkll, bvl], writes=[bpsn])
                        k.op("dve", lambda: nc.vector.scalar_tensor_tensor(out=S[:, si, :], in0=S[:, si, :], scalar=dec[:, si, ch:ch + 1],
                                                                           in1=psn[:], op0=ALU.mult, op1=ALU.add),
                             reads=[bS[si], bdec, bpsn], writes=[bS[si]])
                        k.op("act", lambda: nc.scalar.copy(out=Sb[:, si, :], in_=S[:, si, :]), reads=[bS[si]], writes=[bSb[si]])
                    mi = st["m"] % 4
                    st["m"] += 1
                    k.op("act", lambda: nc.scalar.activation(out=sq[:], in_=po[0:64, :], func=AF.Square, accum_out=ms[mi][:]),
                         reads=[bpo], writes=[bsq, bms[mi]])
                    k.op("dve", lambda: nc.vector.tensor_scalar(out=ms[mi][:], in0=ms[mi][:], scalar1=1.0 / 512.0, scalar2=LN_EPS,
                                                                op0=ALU.mult, op1=ALU.add), reads=[bms[mi]], writes=[bms[mi]])
                    k.op("act", lambda: nc.scalar.activation(out=ms[mi][:], in_=ms[mi][:], func=AF.Sqrt), reads=[bms[mi]], writes=[bms[mi]])
                    k.op("dve", lambda: nc.vector.reciprocal(out=ms[mi][:], in_=ms[mi][:]), reads=[bms[mi]], writes=[bms[mi]])
                    gi = st["m"] % 2
                    k.op("dve", lambda: nc.vector.scalar_tensor_tensor(out=og[gi][:], in0=po[0:64, :], scalar=ms[mi][:, 0:1], in1=gng[:],
                                                                       op0=ALU.mult, op1=ALU.mult), reads=[bpo, bms[mi], bgc], writes=[bog[gi]])
                    k.op("dve", lambda: nc.vector.tensor_tensor(out=ogb[ri][:, h * 512:(h + 1) * 512], in0=og[gi][:],
                                                                in1=rsl[ri][:, h * 512:(h + 1) * 512], op=ALU.mult),
                         reads=[bog[gi], brsl[ri]], writes=[bogb[ri]])
                for g4 in range(4):
                    ti = st["t"] % 2
                    st["t"] += 1
                    for q in range(4):
                        dt_ = g4 * 4 + q
                        k.op("pe", lambda: nc.tensor.transpose(tb[ti][:, q * 64:(q + 1) * 64], ogb[ri][:, dt_ * 128:(dt_ + 1) * 128], identb[0:64, 0:64]),
                             reads=[bogb[ri], bgc], writes=[btb[ti]])
                    k.op("act", lambda: nc.scalar.copy(out=otst[oi][:, g4 * 4:(g4 + 1) * 4, csl],
                                                       in_=tb[ti][:, 0:256].rearrange("p (q c) -> p q c", q=4)),
                         reads=[btb[ti]], writes=[botst[oi]])
            k.dma("act", OT.rearrange("(dt p) t -> p dt t", p=128)[:, :, tsl], otst[oi][:], owner=botst[oi], reads=[botst[oi]],
                  writes=[k.buf("gOT", tt)])
        k.barrier()
        P.release(m)
        P.release(m0)
        run_outproj_from_dram(L, src, dst, OT, "gOT", cw["out"])

    def run_nsa(L, src, dst):
        jn = L // 3
        cw = conv[("nsa", L)]
        nin = nsa_in[jn]
        QT = P.dram_tmp("nsa_QT%d" % L, [2048, NT], BF16)
        KCT = P.dram_tmp("nsa_KCT%d" % L, [512, NT], BF16)
        KST = P.dram_tmp("nsa_KST%d" % L, [512, NT], BF16)
        KWT = P.dram_tmp("nsa_KWT%d" % L, [512, NT], BF16)
        VCT = P.dram_tmp("nsa_VCT%d" % L, [512, NT], BF16)
        VS = P.dram_tmp("nsa_VS%d" % L, [NT, 512], BF16)
        VW = P.dram_tmp("nsa_VW%d" % L, [NT, 512], BF16)
        GT = P.dram_tmp("nsa_GT%d" % L, [NT, 48], F32)
        OT = P.dram_tmp("nsa_OT%d" % L, [D, NT], BF16)
        SCALE = 128.0 ** -0.5
        m0 = P.mark()
        bnc = k.buf("nconst")
        bnc2 = k.buf("nconst2")
        invs = P.sb("n_invs", [128, 2], F32)
        gb = P.sb("n_gb", [128, 48], F32)
        validc = P.sb("n_validc", [128, T], BF16)
        cam = P.sb("n_cam", [128, 256], BF16)
        addc = P.sb("n_addc", [128, 16 * 32], F32)
        esel = P.sb("n_esel", [128, 16 * 128], BF16)
        identb = P.sb("n_idb", [128, 128], BF16)
        kcmp = P.sb("n_kcmp", [128, nseq * 4, 128], BF16)
        vaug = P.sb("n_vaug", [128, nseq * 4, 161], BF16)
        bkcmp, bvaug = k.buf("n_kcmp"), k.buf("n_vaug")
        k.dma("sp", invs[:], nsa_invs_in[:, :], owner=bnc, writes=[bnc])
        k.dma("sp", gb[:], nin["gate_b"][:, :], owner=bnc, writes=[bnc])
        k.dma("sp", validc[:], nsa_validc_in[:, :], owner=bnc, writes=[bnc])
        k.dma("sp", cam[:], nsa_cam_in[:, :], owner=bnc, writes=[bnc])
        k.dma("sp", addc[:], nsa_addc_in[:, :], owner=bnc, writes=[bnc])
        k.dma("sp", esel[:], nsa_esel_in[:, :], owner=bnc, writes=[bnc])
        for sg in range(nseq * 4):
            k.dma("sp", vaug[:, sg, 128:161], nsa_ovl_in[:, :], owner=bnc, writes=[bnc])
        k.op("act", lambda: nc.scalar.copy(out=identb[:], in_=ident[:]), reads=[b_const], writes=[bnc])

        class Rot:
            def __init__(self, items, bufs):
                self.items, self.bufs, self.i = items, bufs, 0

            def __call__(self):
                j = self.i % len(self.items)
                self.i += 1
                return self.items[j], self.bufs[j]

        wr, br_ = cw["rope"]
        wvc, bvc = cw["vc"]
        wtm, btm = cw["tm"]
        wgl, bgl = cw["gl"]
        blocks = []
        for tt in range(NTT):
            for i in range(28):
                blocks.append((wr[i], br_[i], 16, 256))
            blocks.append((wvc[0], bvc[0], 16, 512))
            blocks.append((wtm[0], btm[0], 16, 512))
            blocks.append((wtm[1], btm[1], 16, 512))
            blocks.append((wgl[0], bgl[0], 16, 48))
        P.ring_plan(blocks)
        m = P.mark()
        xT = P.sb("xT", [128, NDT, TT], BF16)
        bxT = k.buf("xT")
        posi = P.sb("posi", [128, TT], I32)
        ang = P.sb("ang", [128, TT], F32)
        kk_ = P.sb("kk_", [128, TT], F32)
        rr = P.sb("rr", [128, TT], F32)
        cos2 = P.sb("cos2", [128, TT], F32)
        sin2 = P.sb("sin2", [128, TT], F32)
        bposi, bang, bkk, brr, bcos, bsin = (k.buf(n) for n in ("posi", "ang", "kk_", "rr", "cos2", "sin2"))
        t1r = Rot([P.sb("t1_%d" % i, [128, TT], F32) for i in range(2)], [k.buf("t1", i) for i in range(2)])
        t2r = Rot([P.sb("t2_%d" % i, [128, TT], F32) for i in range(2)], [k.buf("t2", i) for i in range(2)])
        str_ = Rot([P.sb("rst_%d" % i, [128, TT], BF16) for i in range(4)], [k.buf("rst_", i) for i in range(4)])
        vst = Rot([P.sb("nvst%d" % i, [128, 4, 512], BF16) for i in range(2)], [k.buf("nvst", i) for i in range(2)])
        gst = P.sb("gst", [128, 4, 48], F32)
        gtmp = P.sb("gtmp", [128, 48], F32)
        bgst, bgtmp = k.buf("gst"), k.buf("gtmp")
        bank = Rot([P.ps("nb%d" % i, [128, 512]) for i in range(8)], [k.buf("nb", i) for i in range(8)])
        MAGIC = 12582912.0
        C1 = 6.28125
        C2 = 2.0 * math.pi - 6.28125
        XTbs = XTb[src].rearrange("(dt p) t -> p dt t", p=128)

        def sincos(dst_t, bdst, shift, signed):
            k.op("dve", lambda: nc.vector.tensor_scalar(out=rr[:], in0=ang[:], scalar1=shift, scalar2=None, op0=ALU.add),
                 reads=[bang], writes=[brr])
            k.op("dve", lambda: nc.vector.tensor_scalar(out=kk_[:], in0=rr[:], scalar1=1.0 / (2.0 * math.pi), scalar2=MAGIC,
                                                        op0=ALU.mult, op1=ALU.add), reads=[brr], writes=[bkk])
            k.op("dve", lambda: nc.vector.tensor_scalar(out=kk_[:], in0=kk_[:], scalar1=-MAGIC, scalar2=None, op0=ALU.add),
                 reads=[bkk], writes=[bkk])
            k.op("dve", lambda: nc.vector.scalar_tensor_tensor(out=rr[:], in0=kk_[:], scalar=-C1, in1=rr[:], op0=ALU.mult, op1=ALU.add),
                 reads=[bkk, brr], writes=[brr])
            k.op("dve", lambda: nc.vector.scalar_tensor_tensor(out=rr[:], in0=kk_[:], scalar=-C2, in1=rr[:], op0=ALU.mult, op1=ALU.add),
                 reads=[bkk, brr], writes=[brr])
            k.op("dve", lambda: nc.vector.tensor_scalar(out=rr[:], in0=rr[:], scalar1=3.1415925, scalar2=-3.1415925,
                                                        op0=ALU.min, op1=ALU.max), reads=[brr], writes=[brr])
            if signed:
                k.op("act", lambda: nc.scalar.activation(out=dst_t[:], in_=rr[:], func=AF.Sin, scale=invs[:, 1:2]),
                     reads=[brr, bnc], writes=[bdst])
            else:
                k.op("act", lambda: nc.scalar.activation(out=dst_t[:], in_=rr[:], func=AF.Sin), reads=[brr], writes=[bdst])

        for tt in range(NTT):
            tsl = slice(tt * TT, (tt + 1) * TT)
            k.dma("sp", xT[:], XTbs[:, :, tsl], owner=bxT, reads=[k.buf("XTb", src, tt)], writes=[bxT])
            k.dma("sp", posi[:], nsa_pos_in[:, tsl], owner=bposi, writes=[bposi])
            k.op("dve", lambda: nc.vector.tensor_copy(out=ang[:], in_=posi[:]), reads=[bposi], writes=[bang])
            k.op("dve", lambda: nc.vector.tensor_scalar(out=ang[:], in0=ang[:], scalar1=invs[:, 0:1], scalar2=None, op0=ALU.mult),
                 reads=[bang, bnc], writes=[bang])
            sincos(cos2, bcos, math.pi / 2.0, False)
            sincos(sin2, bsin, 0.0, False)
            for i in range(28):
                wbuf, wv = P.ring_next()
                sc = SCALE if i < 16 else 1.0
                p1, bp1 = bank()
                for kk in range(NDT):
                    k.op("pe", lambda: nc.tensor.matmul(p1[:], lhsT=wv[:, kk, 0:128], rhs=xT[:, kk, :], start=(kk == 0), stop=(kk == NDT - 1)),
                         reads=[wbuf, bxT], writes=[bp1])
                t1, bt1 = t1r()
                t2, bt2 = t2r()
                so, bso = str_()
                lo, hi = slice(0, 64), slice(64, 128)
                k.op("dve", lambda: nc.vector.scalar_tensor_tensor(out=t1[lo, :], in0=p1[lo, :], scalar=sc, in1=cos2[lo, :], op0=ALU.mult, op1=ALU.mult),
                     reads=[bp1, bcos], writes=[bt1])
                k.op("dve", lambda: nc.vector.scalar_tensor_tensor(out=t2[lo, :], in0=p1[hi, :], scalar=sc, in1=sin2[hi, :], op0=ALU.mult, op1=ALU.mult),
                     reads=[bp1, bsin], writes=[bt2])
                k.op("dve", lambda: nc.vector.tensor_tensor(out=so[lo, :], in0=t1[lo, :], in1=t2[lo, :], op=ALU.subtract), reads=[bt1, bt2], writes=[bso])
                k.op("dve", lambda: nc.vector.scalar_tensor_tensor(out=t1[hi, :], in0=p1[hi, :], scalar=sc, in1=cos2[hi, :], op0=ALU.mult, op1=ALU.mult),
                     reads=[bp1, bcos], writes=[bt1])
                k.op("dve", lambda: nc.vector.scalar_tensor_tensor(out=t2[hi, :], in0=p1[lo, :], scalar=sc, in1=sin2[lo, :], op0=ALU.mult, op1=ALU.mult),
                     reads=[bp1, bsin], writes=[bt2])
                k.op("dve", lambda: nc.vector.tensor_tensor(out=so[hi, :], in0=t1[hi, :], in1=t2[hi, :], op=ALU.add), reads=[bt1, bt2], writes=[bso])
                if i < 16:
                    dst_ap, dname = QT[i * 128:(i + 1) * 128, tsl], ("nQT", i, tt)
                elif i < 20:
                    dst_ap, dname = KCT[(i - 16) * 128:(i - 15) * 128, tsl], ("nKCT", i - 16, tt)
                elif i < 24:
                    dst_ap, dname = KST[(i - 20) * 128:(i - 19) * 128, tsl], ("nKST", i - 20, tt)
                else:
                    dst_ap, dname = KWT[(i - 24) * 128:(i - 23) * 128, tsl], ("nKWT", i - 24, tt)
                k.dma("act", dst_ap, so[:], owner=bso, reads=[bso], writes=[k.buf(*dname)])
            wbuf, wv = P.ring_next()
            for g in range(4):
                p1, bp1 = bank()
                for kk in range(NDT):
                    k.op("pe", lambda: nc.tensor.matmul(p1[:], lhsT=wv[:, kk, g * 128:(g + 1) * 128], rhs=xT[:, kk, :],
                                                        start=(kk == 0), stop=(kk == NDT - 1)), reads=[wbuf, bxT], writes=[bp1])
                so, bso = str_()
                k.op("act", lambda: nc.scalar.copy(out=so[:], in_=p1[:]), reads=[bp1], writes=[bso])
                k.dma("act", VCT[g * 128:(g + 1) * 128, tsl], so[:], owner=bso, reads=[bso], writes=[k.buf("nVCT", g, tt)])
            for which, DST, nm in ((0, VS, "nVS"), (1, VW, "nVW")):
                wbuf, wv = P.ring_next()
                vs_, bvs_ = vst()
                for sub in range(4):
                    p1, bp1 = bank()
                    for kk in range(NDT):
                        k.op("pe", lambda: nc.tensor.matmul(p1[:], lhsT=xT[:, kk, sub * 128:(sub + 1) * 128], rhs=wv[:, kk, :],
                                                            start=(kk == 0), stop=(kk == NDT - 1)), reads=[wbuf, bxT], writes=[bp1])
                    k.op("act", lambda: nc.scalar.copy(out=vs_[:, sub, :], in_=p1[:]), reads=[bp1], writes=[bvs_])
                k.dma("act", DST[tsl, :].rearrange("(s p) d -> p s d", p=128), vs_[:], owner=bvs_, reads=[bvs_], writes=[k.buf(nm, tt)])
            wbuf, wv = P.ring_next()
            for sub in range(4):
                p1, bp1 = bank()
                for kk in range(NDT):
                    k.op("pe", lambda: nc.tensor.matmul(p1[:, 0:48], lhsT=xT[:, kk, sub * 128:(sub + 1) * 128], rhs=wv[:, kk, 0:48],
                                                        start=(kk == 0), stop=(kk == NDT - 1)), reads=[wbuf, bxT], writes=[bp1])
                k.op("dve", lambda: nc.vector.tensor_tensor(out=gtmp[:], in0=p1[:, 0:48], in1=gb[:], op=ALU.add), reads=[bp1, bnc], writes=[bgtmp])
                k.op("act", lambda: nc.scalar.activation(out=gst[:, sub, :], in_=gtmp[:], func=AF.Sigmoid), reads=[bgtmp], writes=[bgst])
            k.dma("act", GT[tsl, :].rearrange("(s p) d -> p s d", p=128), gst[:], owner=bgst, reads=[bgst], writes=[k.buf("nGT", tt)])
        k.barrier()
        P.release(m)

        m = P.mark()
        w1 = P.sb("n_w1", [128, 2, 32, 128], BF16)
        w2 = P.sb("n_w2", [128, 2, 128], BF16)
        posT = P.sb("n_posT", [128, 64], BF16)
        sm_ = small[("nsa", jn)]
        for kv in range(2):
            k.dma("sp", w1[:, kv, :, :], sm_["w1"][kv].rearrange("(l p) j -> p l j", p=128), owner=bnc2, reads=[bsmall], writes=[bnc2])
            k.dma("sp", w2[:, kv, :], sm_["w2"][kv], owner=bnc2, reads=[bsmall], writes=[bnc2])
        k.dma("sp", posT[:], sm_["posT"][:, :], owner=bnc2, reads=[bsmall], writes=[bnc2])
        kct = P.sb("kct", [128, T], BF16)
        bkct = k.buf("kct")
        biasv = P.sb("biasv", [128, 2], F32)
        bbias = k.buf("biasv")
        xg = P.sb("xg", [128, 128], F32)
        x2 = P.sb("x2g", [128, 128], F32)
        gT = P.sb("gTg", [128, 128], BF16)
        bxg, bx2, bgT = k.buf("xg"), k.buf("x2g"), k.buf("gTg")
        bank = Rot([P.ps("cb%d" % i, [128, 512]) for i in range(4)], [k.buf("cb", i) for i in range(4)])
        for kv in range(2):
            pb, bpb = bank()
            for l in range(32):
                k.op("pe", lambda: nc.tensor.matmul(pb[:, 0:1], lhsT=w1[:, kv, l, :], rhs=posT[:, kv * 32 + l:kv * 32 + l + 1],
                                                    start=(l == 0), stop=(l == 31)), reads=[bnc2], writes=[bpb])
            k.op("dve", lambda: nc.vector.tensor_copy(out=biasv[:, kv:kv + 1], in_=pb[:, 0:1]), reads=[bpb], writes=[bbias])
        for sq_ in range(nseq):
            for g in range(4):
                sg = sq_ * 4 + g
                for kv in range(2):
                    SRC = KCT if kv == 0 else VCT
                    k.dma("sp", kct[:], SRC[g * 128:(g + 1) * 128, sq_ * T:(sq_ + 1) * T], owner=bkct,
                          reads=[k.buf("nKCT" if kv == 0 else "nVCT", g, tt_) for tt_ in range(sq_ * 4, sq_ * 4 + 4)], writes=[bkct])
                    pb, bpb = bank()
                    for l in range(32):
                        k.op("pe", lambda: nc.tensor.matmul(pb[:, 0:127], lhsT=w1[:, kv, l, :], rhs=kct[:, l:l + 16 * 126 + 1:16],
                                                            start=(l == 0), stop=(l == 31)), reads=[bnc2, bkct], writes=[bpb])
                    k.op("act", lambda: nc.scalar.activation(out=xg[:, 0:127], in_=pb[:, 0:127], func=AF.Identity, bias=biasv[:, kv:kv + 1]),
                         reads=[bpb, bbias], writes=[bxg])
                    k.op("dve", lambda: nc.vector.tensor_tensor(out=x2[:, 0:127], in0=xg[:, 0:127], in1=xg[:, 0:127], op=ALU.mult),
                         reads=[bxg], writes=[bx2])
                    k.op("dve", lambda: nc.vector.tensor_scalar(out=x2[:, 0:127], in0=x2[:, 0:127], scalar1=0.044715, scalar2=1.0,
                                                                op0=ALU.mult, op1=ALU.add), reads=[bx2], writes=[bx2])
                    k.op("dve", lambda: nc.vector.tensor_tensor(out=x2[:, 0:127], in0=x2[:, 0:127], in1=xg[:, 0:127], op=ALU.mult),
                         reads=[bx2, bxg], writes=[bx2])
                    k.op("act", lambda: nc.scalar.activation(out=x2[:, 0:127], in_=x2[:, 0:127], func=AF.Tanh, scale=math.sqrt(2.0 / math.pi)),
                         reads=[bx2], writes=[bx2])
                    k.op("dve", lambda: nc.vector.tensor_scalar(out=x2[:, 0:127], in0=x2[:, 0:127], scalar1=1.0, scalar2=0.5,
                                                                op0=ALU.add, op1=ALU.mult), reads=[bx2], writes=[bx2])
                    k.op("dve", lambda: nc.vector.tensor_tensor(out=gT[:, 0:127], in0=x2[:, 0:127], in1=xg[:, 0:127], op=ALU.mult),
                         reads=[bx2, bxg], writes=[bgT])
                    pc, bpc = bank()
                    if kv == 0:
                        k.op("pe", lambda: nc.tensor.matmul(pc[:, 0:127], lhsT=w2[:, 0, :], rhs=gT[:, 0:127], start=True, stop=True),
                             reads=[bnc2, bgT], writes=[bpc])
                        k.op("act", lambda: nc.scalar.copy(out=kcmp[:, sg, 0:127], in_=pc[:, 0:127]), reads=[bpc], writes=[bkcmp])
                    else:
                        k.op("pe", lambda: nc.tensor.matmul(pc[0:127, 0:128], lhsT=gT[:, 0:127], rhs=w2[:, 1, :], start=True, stop=True),
                             reads=[bnc2, bgT], writes=[bpc])
                        k.op("act", lambda: nc.scalar.copy(out=vaug[0:127, sg, 0:128], in_=pc[0:127, 0:128]), reads=[bpc, bnc], writes=[bvaug])
        k.barrier()
        P.release(m)

        m = P.mark()
        ksT = P.sb("ksT", [128, T], BF16)
        kwT = P.sb("kwT", [128, T], BF16)
        vsa = P.sb("vsa", [128, 16, 129], BF16)
        vwa = P.sb("vwa", [128, 16, 129], BF16)
        qT4 = P.sb("qT4", [128, 4, T], BF16)
        bksT, bkwT, bvsa, bvwa = k.buf("ksT"), k.buf("kwT"), k.buf("vsa"), k.buf("vwa")
        bqT4 = [k.buf("qT4", r) for r in range(4)]
        k.op("dve", lambda: nc.vector.memset(vsa[:, :, 128:129], 1.0), writes=[bvsa])
        k.op("dve", lambda: nc.vector.memset(vwa[:, :, 128:129], 1.0), writes=[bvwa])
        pTs = [P.sb("pTs%d" % i, [128, 16, TT], BF16) for i in range(2)]
        pTw = [P.sb("pTw%d" % i, [128, 8, TT], BF16) for i in range(2)]
        bpTs = [[k.buf("pTs", i, j) for j in range(16)] for i in range(2)]
        bpTw = [[k.buf("pTw", i, j) for j in range(8)] for i in range(2)]
        ec = Rot([P.sb("ec%d" % i, [128, TT], BF16) for i in range(2)], [k.buf("ec", i) for i in range(2)])
        ocmp2 = [P.sb("ocmp%d" % i, [128, 4, 4, 128], F32) for i in range(2)]
        bocmp2 = [[[k.buf("ocmp", i, r, q) for q in range(4)] for r in range(4)] for i in range(2)]
        imp2 = [P.sb("imp%d" % i, [128, 4, 32], F32) for i in range(2)]
        bimp2 = [[k.buf("imp", i, q) for q in range(4)] for i in range(2)]
        cnts = {"qt": 0, "h": 0}
        cmp3 = P.sb("cmp3", [128, 4, 32, 32], BF16)
        bcmp3 = k.buf("cmp3")
        rank = P.sb("rank", [128, 4, 32], F32)
        brank = k.buf("rank")
        selbT2 = [P.sb("selbT%d" % i, [128, TT], BF16) for i in range(2)]
        bselbT2 = [k.buf("selbT", i) for i in range(2)]
        for i in range(2):
            k.op("dve", lambda: nc.vector.memset(selbT2[i][:], 0.0), writes=[bselbT2[i]])
        impt = P.sb("impt", [128, 4, 32], F32)
        bimpt = k.buf("impt")
        gat2 = [P.sb("gat%d" % i, [128, 4, 48], F32) for i in range(2)]
        bgat2 = [k.buf("gat", i) for i in range(2)]
        sm = Rot([P.sb("sm%d" % i, [128, 16], F32) for i in range(4)], [k.buf("sm", i) for i in range(4)])
        acc = Rot([P.sb("acc%d" % i, [128, 4, 128], F32) for i in range(2)], [k.buf("acc", i) for i in range(2)])
        accb = Rot([P.sb("accb%d" % i, [128, 4, 128], BF16) for i in range(2)], [k.buf("accb", i) for i in range(2)])
        ots = Rot([P.sb("ots%d" % i, [128, TT], BF16) for i in range(2)], [k.buf("ots", i) for i in range(2)])
        bank = Rot([P.ps("ab%d" % i, [128, 512]) for i in range(6)], [k.buf("ab", i) for i in range(6)])
        tbank = Rot([P.ps("atb%d" % i, [128, 512], BF16) for i in range(2)], [k.buf("atb", i) for i in range(2)])

        for sq_ in range(nseq):
            s0 = sq_ * T
            for g in range(4):
                sg = sq_ * 4 + g
                tts = range(sq_ * 4, sq_ * 4 + 4)
                k.dma("sp", ksT[:], KST[g * 128:(g + 1) * 128, s0:s0 + T], owner=bksT, reads=[k.buf("nKST", g, t_) for t_ in tts], writes=[bksT])
                k.dma("sp", kwT[:], KWT[g * 128:(g + 1) * 128, s0:s0 + T], owner=bkwT, reads=[k.buf("nKWT", g, t_) for t_ in tts], writes=[bkwT])
                k.dma("sp", vsa[:, :, 0:128], VS[s0:s0 + T, g * 128:(g + 1) * 128].rearrange("(kt p) d -> p kt d", p=128), owner=bvsa,
                      reads=[k.buf("nVS", t_) for t_ in tts], writes=[bvsa])
                k.dma("sp", vwa[:, :, 0:128], VW[s0:s0 + T, g * 128:(g + 1) * 128].rearrange("(kt p) d -> p kt d", p=128), owner=bvwa,
                      reads=[k.buf("nVW", t_) for t_ in tts], writes=[bvwa])
                for r in range(4):
                    h = g * 4 + r
                    k.dma("sp", qT4[:, r, :], QT[h * 128:(h + 1) * 128, s0:s0 + T], owner=bqT4[r],
                          reads=[k.buf("nQT", h, t_) for t_ in tts], writes=[bqT4[r]])
                for qt in range(4):
                    q0 = qt * TT
                    qsl = slice(q0, q0 + TT)
                    gat, bgat = gat2[cnts["qt"] % 2], bgat2[cnts["qt"] % 2]
                    k.dma("sp", gat[:], GT[s0 + q0:s0 + q0 + TT, :].rearrange("(s p) d -> p s d", p=128), owner=bgat,
                          reads=[k.buf("nGT", sq_ * 4 + qt)], writes=[bgat])
                    oi = cnts["qt"] % 2
                    cnts["qt"] += 1
                    ocmp, bocmp, imp, bimp = ocmp2[oi], bocmp2[oi], imp2[oi], bimp2[oi]
                    for r in range(4):
                        h = g * 4 + r
                        pb, bpb = bank()
                        k.op("pe", lambda: nc.tensor.matmul(pb[0:127, :], lhsT=kcmp[:, sg, 0:127], rhs=qT4[:, r, qsl], start=True, stop=True),
                             reads=[bkcmp, bqT4[r]], writes=[bpb])
                        e_, be_ = ec()
                        k.op("act", lambda: nc.scalar.activation(out=e_[0:127, :], in_=pb[0:127, :], func=AF.Exp), reads=[bpb], writes=[be_])
                        k.op("dve", lambda: nc.vector.tensor_tensor(out=e_[0:127, :], in0=e_[0:127, :], in1=validc[0:127, qsl], op=ALU.mult),
                             reads=[be_, bnc], writes=[be_])
                        po, bpo = bank()
                        pil, bpil = bank()
                        for qs in range(4):
                            k.op("pe", lambda: nc.tensor.matmul(po[:, qs * 128:(qs + 1) * 128], lhsT=e_[0:127, qs * 128:(qs + 1) * 128],
                                                                rhs=vaug[0:127, sg, 0:128], start=True, stop=True), reads=[be_, bvaug], writes=[bpo])
                            k.op("pe", lambda: nc.tensor.matmul(pil[:, qs * 33:(qs + 1) * 33], lhsT=e_[0:127, qs * 128:(qs + 1) * 128],
                                                                rhs=vaug[0:127, sg, 128:161], start=True, stop=True), reads=[be_, bvaug], writes=[bpil])
                        s_, bs_ = sm()
                        ilv = pil[:, 0:132].rearrange("p (q c) -> p q c", q=4)
                        k.op("dve", lambda: nc.vector.tensor_scalar(out=s_[:, 0:4], in0=ilv[:, :, 32], scalar1=1e-30, scalar2=None, op0=ALU.add),
                             reads=[bpil], writes=[bs_])
                        k.op("dve", lambda: nc.vector.reciprocal(out=s_[:, 4:8], in_=s_[:, 0:4]), reads=[bs_], writes=[bs_])
                        k.op("dve", lambda: nc.vector.tensor_tensor(out=s_[:, 8:12], in0=s_[:, 4:8], in1=gat[:, :, h], op=ALU.mult),
                             reads=[bs_, bgat], writes=[bs_])
                        k.op("dve", lambda: nc.vector.tensor_tensor(out=ocmp[:, r, :, :], in0=po[:, :].rearrange("p (q d) -> p q d", q=4),
                                                                    in1=s_[:, 8:12].unsqueeze(2).to_broadcast([128, 4, 128]), op=ALU.mult),
                             reads=[bpo, bs_], writes=[bocmp[r][0]])
                        if r == 0:
                            k.op("dve", lambda: nc.vector.tensor_tensor(out=imp[:, :, :], in0=ilv[:, :, 0:32],
                                                                        in1=s_[:, 4:8].unsqueeze(2).to_broadcast([128, 4, 32]), op=ALU.mult),
                                 reads=[bpil, bs_], writes=[bimp[0]])
                        else:
                            k.op("dve", lambda: nc.vector.tensor_tensor(out=impt[:, :, :], in0=ilv[:, :, 0:32],
                                                                        in1=s_[:, 4:8].unsqueeze(2).to_broadcast([128, 4, 32]), op=ALU.mult),
                                 reads=[bpil, bs_], writes=[bimpt])
                            k.op("dve", lambda: nc.vector.tensor_tensor(out=imp[:, :, :], in0=imp[:, :, :], in1=impt[:, :, :], op=ALU.add),
                                 reads=[bimp[0], bimpt], writes=[bimp[0]])
                    selbT, bselbT = selbT2[oi], bselbT2[oi]
                    k.op("dve", lambda: nc.vector.tensor_tensor(out=imp[:, :, :], in0=imp[:, :, :],
                                                                in1=addc[:, qt * 128:(qt + 1) * 128].rearrange("p (q m) -> p q m", q=4), op=ALU.add),
                         reads=[bimp[0], bnc], writes=[bimp[0]])
                    k.op("dve", lambda: nc.vector.tensor_tensor(out=cmp3[:], in0=imp[:, :, :].unsqueeze(2).to_broadcast([128, 4, 32, 32]),
                                                                in1=imp[:, :, :].unsqueeze(3).to_broadcast([128, 4, 32, 32]), op=ALU.is_gt),
                         reads=[bimp[0]], writes=[bcmp3])
                    k.op("dve", lambda: nc.vector.reduce_sum(out=rank[:], in_=cmp3[:], axis=AX.X), reads=[bcmp3], writes=[brank])
                    k.op("dve", lambda: nc.vector.tensor_scalar(out=rank[:], in0=rank[:], scalar1=15.5, scalar2=-30000.0,
                                                                op0=ALU.is_gt, op1=ALU.mult), reads=[brank], writes=[brank])
                    pt, bpt = bank()
                    for qs in range(4):
                        k.op("pe", lambda: nc.tensor.transpose(pt[0:32, qs * 128:(qs + 1) * 128], rank[:, qs, :], ident[:]), reads=[brank, b_const], writes=[bpt])
                    k.op("act", lambda: nc.scalar.copy(out=selbT[0:32, :], in_=pt[0:32, :]), reads=[bpt], writes=[bselbT])

                    def S_phase(r, pi):
                        pTs_, pTw_, bps_, bpw_ = pTs[pi], pTw[pi], bpTs[pi], bpTw[pi]
                        nks = qt * 4 + 4
                        for ki in range(nks):
                            pb, bpb = bank()
                            k.op("pe", lambda: nc.tensor.matmul(pb[:], lhsT=ksT[:, ki * 128:(ki + 1) * 128], rhs=qT4[:, r, qsl], start=True, stop=False),
                                 reads=[bksT, bqT4[r]], writes=[bpb])
                            k.op("pe", lambda: nc.tensor.matmul(pb[:], lhsT=esel[:, ki * 128:(ki + 1) * 128], rhs=selbT[:, :], start=False, stop=True),
                                 reads=[bnc, bselbT], writes=[bpb])
                            k.op("act", lambda: nc.scalar.activation(out=pTs_[:, ki, :], in_=pb[:], func=AF.Exp), reads=[bpb], writes=[bps_[ki]])
                            if ki >= qt * 4:
                                qs = ki - qt * 4
                                k.op("dve", lambda: nc.vector.tensor_tensor(out=pTs_[:, ki, qs * 128:(qs + 1) * 128], in0=pTs_[:, ki, qs * 128:(qs + 1) * 128],
                                                                            in1=cam[:, 0:128], op=ALU.mult), reads=[bps_[ki], bnc], writes=[bps_[ki]])
                        kw0 = max(0, qt * 4 - 4)
                        for ki in range(kw0, qt * 4 + 4):
                            wi = ki - kw0
                            pb, bpb = bank()
                            k.op("pe", lambda: nc.tensor.matmul(pb[:], lhsT=kwT[:, ki * 128:(ki + 1) * 128], rhs=qT4[:, r, qsl], start=True, stop=True),
                                 reads=[bkwT, bqT4[r]], writes=[bpb])
                            k.op("act", lambda: nc.scalar.activation(out=pTw_[:, wi, :], in_=pb[:], func=AF.Exp), reads=[bpb], writes=[bpw_[wi]])
                            for qs in range(4):
                                qi = qt * 4 + qs
                                if ki == qi:
                                    mk = cam[:, 0:128]
                                elif ki == qi - 4:
                                    mk = cam[:, 128:256]
                                else:
                                    continue
                                k.op("dve", lambda: nc.vector.tensor_tensor(out=pTw_[:, wi, qs * 128:(qs + 1) * 128], in0=pTw_[:, wi, qs * 128:(qs + 1) * 128],
                                                                            in1=mk, op=ALU.mult), reads=[bpw_[wi], bnc], writes=[bpw_[wi]])

                    def PV_phase(r, pi):
                        pTs_, pTw_, bps_, bpw_ = pTs[pi], pTw[pi], bpTs[pi], bpTw[pi]
                        h = g * 4 + r
                        kw0 = max(0, qt * 4 - 4)
                        po, bpo = bank()
                        pw, bpw = bank()
                        pl, bpl = bank()
                        for qs in range(4):
                            qi = qt * 4 + qs
                            for ki in range(qi + 1):
                                k.op("pe", lambda: nc.tensor.matmul(po[:, qs * 128:(qs + 1) * 128], lhsT=pTs_[:, ki, qs * 128:(qs + 1) * 128], rhs=vsa[:, ki, 0:128],
                                                                    start=(ki == 0), stop=(ki == qi)), reads=[bps_[ki], bvsa], writes=[bpo])
                            for ki in range(qi + 1):
                                k.op("pe", lambda: nc.tensor.matmul(pl[:, 2 * qs:2 * qs + 1], lhsT=pTs_[:, ki, qs * 128:(qs + 1) * 128], rhs=vsa[:, ki, 128:129],
                                                                    start=(ki == 0), stop=(ki == qi)), reads=[bps_[ki], bvsa], writes=[bpl])
                            kis = list(range(max(0, qi - 4), qi + 1))
                            for ki in kis:
                                k.op("pe", lambda: nc.tensor.matmul(pw[:, qs * 128:(qs + 1) * 128], lhsT=pTw_[:, ki - kw0, qs * 128:(qs + 1) * 128], rhs=vwa[:, ki, 0:128],
                                                                    start=(ki == kis[0]), stop=(ki == kis[-1])), reads=[bpw_[ki - kw0], bvwa], writes=[bpw])
                            for ki in kis:
                                k.op("pe", lambda: nc.tensor.matmul(pl[:, 2 * qs + 1:2 * qs + 2], lhsT=pTw_[:, ki - kw0, qs * 128:(qs + 1) * 128], rhs=vwa[:, ki, 128:129],
                                                                    start=(ki == kis[0]), stop=(ki == kis[-1])), reads=[bpw_[ki - kw0], bvwa], writes=[bpl])
                        s_, bs_ = sm()
                        a_, ba_ = acc()
                        t_, bt_ = acc()
                        ab_, bab_ = accb()
                        k.op("dve", lambda: nc.vector.reciprocal(out=s_[:, 0:8], in_=pl[:, 0:8]), reads=[bpl], writes=[bs_])
                        k.op("dve", lambda: nc.vector.tensor_tensor(out=s_[:, 8:16].rearrange("p (q b) -> p q b", q=4), in0=s_[:, 0:8].rearrange("p (q b) -> p q b", q=4),
                                                                    in1=gat[:, :, 16 + h:48:16], op=ALU.mult), reads=[bs_, bgat], writes=[bs_])
                        cg = s_[:, 8:16].rearrange("p (q b) -> p q b", q=4)
                        k.op("dve", lambda: nc.vector.tensor_tensor(out=a_[:], in0=po[:, :].rearrange("p (q d) -> p q d", q=4),
                                                                    in1=cg[:, :, 0].unsqueeze(2).to_broadcast([128, 4, 128]), op=ALU.mult),
                             reads=[bpo, bs_], writes=[ba_])
                        k.op("dve", lambda: nc.vector.tensor_tensor(out=t_[:], in0=pw[:, :].rearrange("p (q d) -> p q d", q=4),
                                                                    in1=cg[:, :, 1].unsqueeze(2).to_broadcast([128, 4, 128]), op=ALU.mult),
                             reads=[bpw, bs_], writes=[bt_])
                        k.op("dve", lambda: nc.vector.tensor_tensor(out=a_[:], in0=a_[:], in1=t_[:], op=ALU.add), reads=[ba_, bt_], writes=[ba_])
                        k.op("dve", lambda: nc.vector.tensor_tensor(out=ab_[:], in0=a_[:], in1=ocmp[:, r, :, :], op=ALU.add),
                             reads=[ba_, bocmp[r][0]], writes=[bab_])
                        tb_, btb_ = tbank()
                        for qs in range(4):
                            k.op("pe", lambda: nc.tensor.transpose(tb_[:, qs * 128:(qs + 1) * 128], ab_[:, qs, :], identb[:]), reads=[bab_, bnc], writes=[btb_])
                        o_, bo_ = ots()
                        k.op("act", lambda: nc.scalar.copy(out=o_[:], in_=tb_[:]), reads=[btb_], writes=[bo_])
                        k.dma("act", OT[h * 128:(h + 1) * 128, s0 + q0:s0 + q0 + TT], o_[:], owner=bo_, reads=[bo_], writes=[k.buf("nOT", h, sq_ * 4 + qt)])

                    prev = None
                    for r in range(4):
                        pi = cnts["h"] % 2
                        cnts["h"] += 1
                        S_phase(r, pi)
                        if prev is not None:
                            PV_phase(*prev)
                        prev = (r, pi)
                    PV_phase(*prev)
        k.barrier()
        P.release(m)
        P.release(m0)
        run_outproj_from_dram(L, src, dst, OT, "nOTx", cw["out"])

    cur = 0
    import os
    if not os.environ.get("SKIP_IN"):
        phase_in_transpose(cur)
    for pi_, p_ in enumerate(plan):
        if pi_ + NAHEAD < len(plan):
            conv_phase(plan[pi_ + NAHEAD])
        if p_[0] == "ffn":
            run_ffn(p_[1], p_[2], cur, 1 - cur)
        elif p_[0] == "conv":
            run_conv(p_[1], cur, 1 - cur)
        elif p_[0] == "gla":
            run_gla(p_[1], cur, 1 - cur)
        elif p_[0] == "nsa":
            run_nsa(p_[1], cur, 1 - cur)
        cur = 1 - cur
    if not os.environ.get("SKIP_OUT"):
        phase_out_transpose(cur)
    k.barrier()
    return P


def make_inputs(P, core, nseq, x, ln_g, ln_b, ffn_w_in, ffn_w_out, conv_w_in=None, conv_w=None, conv_w_out=None,
                gla_w_in=None, gla_w_a2=None, gla_b_a=None, gla_norm_g=None, gla_w_out=None,
                positions=None, nsa_w_in=None, nsa_gate_b=None, nsa_cmp_pos=None, nsa_cmp_w1=None, nsa_cmp_w2=None, nsa_w_out=None, **rest):
    m = {}
    xs = np.ascontiguousarray(x[core * nseq:(core + 1) * nseq]).reshape(nseq * T, D)
    m["x"] = xs
    g = np.stack([ln_g, ln_b], axis=2)
    g = g.reshape(DEPTH, 3, 2, NDT, 128).transpose(4, 0, 1, 2, 3).reshape(128, -1)
    m["ln_gb"] = np.ascontiguousarray(g, dtype=np.float32)
    m["ident"] = np.eye(128, dtype=np.float32)
    for name in P.inputs:
        if name.startswith("nsa_"):
            m[name] = nsa_host_input(name, core, nseq, positions, nsa_w_in, nsa_gate_b, nsa_cmp_pos, nsa_cmp_w1, nsa_cmp_w2, nsa_w_out)
        elif name == "gla_w_in":
            m[name] = gla_w_in[0]
        elif name == "gla_w_out":
            m[name] = gla_w_out[0]
        elif name == "gla_const":
            s_ = np.arange(128)[:, None]
            t_ = np.arange(128)[None, :]
            same = (s_ // 64) == (t_ // 64)
            tri = ((s_ <= t_) & same).astype(np.float32) / 16.0
            mm = ((s_ > t_) & same).astype(np.float32) / 16.0
            m[name] = np.ascontiguousarray(np.concatenate([tri, mm], axis=1))
        elif name == "gla_wa2":
            m[name] = np.ascontiguousarray(np.concatenate([gla_w_a2[0], gla_b_a[0][None, :]], axis=0))
        elif name == "gla_ng":
            m[name] = np.ascontiguousarray(np.broadcast_to(gla_norm_g[0][None, :], (64, 512)))
        elif name == "gla_cm":
            m[name] = (np.arange(64)[:, None] <= np.arange(64)[None, :]).astype(np.float32)
        elif name == "conv_w_in":
            m[name] = conv_w_in[0]
        elif name == "conv_w_out":
            m[name] = conv_w_out[0]
        elif name == "conv_w":
            m[name] = np.ascontiguousarray(conv_w[0].reshape(3, NDT, 128).transpose(2, 0, 1).reshape(128, 3 * NDT))
        elif name.startswith("ffn_w_in_"):
            L, j = map(int, name.split("_")[-2:])
            m[name] = ffn_w_in[L, j]
        elif name.startswith("ffn_w_out_"):
            L, j = map(int, name.split("_")[-2:])
            m[name] = ffn_w_out[L, j]
    return m


def nsa_host_input(name, core, nseq, positions, nsa_w_in, nsa_gate_b, nsa_cmp_pos, nsa_cmp_w1, nsa_cmp_w2, nsa_w_out):
    bf = ml_dtypes.bfloat16
    if name[-2] == "_" and name[-1].isdigit():
        jn = int(name[-1])
        base = name[:-2]
        if base == "nsa_w_in":
            return nsa_w_in[jn]
        if base == "nsa_w_out":
            return nsa_w_out[jn]
        if base == "nsa_w1":
            return nsa_cmp_w1[jn]
        if base == "nsa_w2":
            return nsa_cmp_w2[jn]
        if base == "nsa_posT":
            return np.ascontiguousarray(nsa_cmp_pos[jn].transpose(2, 0, 1).reshape(128, 64))
        if base == "nsa_gateb":
            return np.ascontiguousarray(np.broadcast_to(nsa_gate_b[jn][None, :], (128, 48)))
    if name == "nsa_pos":
        p = np.ascontiguousarray(positions[core * nseq:(core + 1) * nseq]).reshape(1, nseq * T)
        return np.ascontiguousarray(np.broadcast_to(p, (128, nseq * T))).astype(np.int32)
    if name == "nsa_invs":
        inv = (np.float32(10000.0) ** (-np.arange(0, 128, 2, dtype=np.float32) / np.float32(128))).astype(np.float32)
        o = np.zeros((128, 2), np.float32)
        o[:64, 0] = inv
        o[64:, 0] = inv
        o[:64, 1] = -1.0
        o[64:, 1] = 1.0
        return o
    if name == "nsa_validc":
        n = np.arange(128)[:, None]
        t = np.arange(T)[None, :]
        v = ((16 * n + 31 <= t) & (n < 127)).astype(np.float32)
        return v.astype(bf)
    if name == "nsa_cam":
        kk = np.arange(128)[:, None]
        tt = np.arange(128)[None, :]
        return np.concatenate([(kk <= tt), (kk > tt)], axis=1).astype(np.float32).astype(bf)
    if name == "nsa_addc":
        t = np.arange(T)
        cur = t // 64
        blk = np.arange(32)[None, :]
        valid = blk <= cur[:, None]
        forced = (blk == 0) | (blk == cur[:, None]) | (blk == cur[:, None] - 1)
        a = np.where(valid, np.where(forced, 1000.0, 0.0), -1e30).astype(np.float32)
        return np.ascontiguousarray(a.reshape(16, 128, 32).transpose(1, 0, 2).reshape(128, 512))
    if name == "nsa_esel":
        mm = np.arange(128)[:, None]
        kk = np.arange(T)[None, :]
        return ((mm == kk // 64) & (mm < 32)).astype(np.float32).astype(bf)
    if name == "nsa_ovl":
        n = np.arange(128)[:, None]
        mblk = np.arange(32)[None, :]
        cs = n * 16
        ss = mblk * 64
        ov = ((cs < ss + 64) & (cs + 32 > ss) & (n < 127)).astype(np.float32)
        ones = (n < 127).astype(np.float32)
        return np.concatenate([ov, ones], axis=1).astype(bf)
    raise KeyError(name)


_CACHE = {}


def kernel(**inputs):
    inputs = {k_: np.asarray(v) for k_, v in inputs.items()}
    if "prog" not in _CACHE:
        _CACHE["prog"] = build_program(nseq=2)
    P = _CACHE["prog"]
    in_maps = [make_inputs(P, c, 2, **inputs) for c in range(NCORES)]
    res = run_bass_kernel_spmd(P.nc, in_maps, core_ids=list(range(NCORES)))
    outs = [np.asarray(r["out"]).reshape(2, T, D) for r in res.results]
    return np.concatenate(outs, axis=0).astype(np.float32)
```

```python
import math
import os
import numpy as np
import ml_dtypes
import concourse.bass as bass
import concourse.mybir as mybir
from concourse.bass_utils import run_bass_kernel_spmd

F32 = mybir.dt.float32
BF16 = mybir.dt.bfloat16
I32 = mybir.dt.int32
ALU = mybir.AluOpType
AF = mybir.ActivationFunctionType
AX = mybir.AxisListType

D = 2048
T = 2048
DEPTH = 4
DFF = 5632
NCORES = 8
ALPHA = (2.0 * DEPTH) ** 0.25
LN_EPS = 1e-5
TT = 512
NDT = D // 128
NHT = DFF // 128
RING_SLOT = 44 * 256
NRING = 3


PSUM_NAMES = {"pst", "opst", "pin", "py", "ps1", "ps2", "gb", "hb", "htb", "nb", "cb", "ab", "atb"}


class Buf:
    __slots__ = ("name", "writer", "readers", "dsem", "dcnt", "excl")

    def __init__(self, name):
        self.name = name
        self.writer = None
        self.readers = []
        self.dsem = None
        self.dcnt = 0
        self.excl = bool(name) and name[0] in PSUM_NAMES


class KB:
    def __init__(self, nc):
        self.nc = nc
        self.eng = {"pe": nc.tensor, "act": nc.scalar, "dve": nc.vector, "pool": nc.gpsimd, "sp": nc.sync}
        self.sem = {}
        self.cnt = {}
        self.waited = {e: {} for e in self.eng}
        self._stack = []
        for e in ("pe", "act", "dve", "pool"):
            self.sem[e] = self._enter(nc.semaphore("sem_" + e))
            self.cnt[e] = 0
        self.semid = {}
        self.bufs = {}
        self.ndma = 0

    def _enter(self, cm):
        v = cm.__enter__()
        self._stack.append(cm)
        return v

    def close(self):
        while self._stack:
            self._stack.pop().__exit__(None, None, None)

    def buf(self, *key):
        b = self.bufs.get(key)
        if b is None:
            b = Buf(key)
            self.bufs[key] = b
        return b

    def _wait(self, eng, tick):
        if tick is None:
            return
        if tick[0] == "e":
            pe, c = tick[1], tick[2]
            if pe == eng and eng == "pe":
                return
            key = pe
            sem, val = self.sem[pe], c
        else:
            b = tick[1]
            key = id(b)
            sem, val = b.dsem, 16 * b.dcnt
        w = self.waited[eng]
        if w.get(key, 0) >= val:
            return
        w[key] = val
        self.eng[eng].wait_ge(sem, val)

    def _sync(self, eng, reads, writes, same_war=False):
        for r in reads:
            self._wait(eng, r.writer)
            if r.excl:
                for t in r.readers:
                    if not (t[0] == "e" and t[1] == eng):
                        self._wait(eng, t)
        for wb in writes:
            self._wait(eng, wb.writer)
            for t in wb.readers:
                if t[0] == "e" and t[1] == eng and not same_war:
                    continue
                self._wait(eng, t)

    def op(self, eng, fn, reads=(), writes=()):
        self._sync(eng, reads, writes)
        ins = fn()
        self.cnt[eng] += 1
        ins.then_inc(self.sem[eng], 1)
        tick = ("e", eng, self.cnt[eng])
        for r in reads:
            r.readers.append(tick)
            if len(r.readers) > 24:
                r.readers = self._prune(r.readers)
        for wb in writes:
            wb.writer = tick
            wb.readers = []
        return ins

    def _prune(self, readers):
        best = {}
        out = []
        for t in readers:
            if t[0] == "e":
                if t[1] not in best or best[t[1]][2] < t[2]:
                    best[t[1]] = t
            else:
                if all(o[1] is not t[1] for o in out):
                    out.append(t)
        return out + list(best.values())

    def dma(self, q, out, in_, owner, reads=(), writes=()):
        if owner.dsem is None:
            owner.dsem = self._enter(self.nc.semaphore("dsem%d" % len(self.semid)))
            self.semid[id(owner)] = owner
        self._sync(q, reads, writes, same_war=True)
        ins = self.eng[q].dma_start(out=out, in_=in_)
        owner.dcnt += 1
        ins.then_inc(owner.dsem, 16)
        if q == "pool":
            pass
        tick = ("d", owner)
        for r in reads:
            r.readers.append(tick)
            if len(r.readers) > 24:
                r.readers = self._prune(r.readers)
        for wb in writes:
            wb.writer = tick
            wb.readers = []
        self.ndma += 1
        return ins

    def barrier(self, bufs=()):
        for e in ("pe", "act", "dve", "pool", "sp"):
            for pe in ("pe", "act", "dve", "pool"):
                if self.cnt[pe] > 0 and not (pe == e):
                    self._wait(e, ("e", pe, self.cnt[pe]))
            for b in self.semid.values():
                if b.dcnt > 0 and b.name[0] != "wbf":
                    self._wait(e, ("d", b))


class Prog:
    def __init__(self, nseq=2, layers=(0, 1, 2, 3), stop_after=None):
        self.nseq = nseq
        self.NT = nseq * T
        self.layers = layers
        self.stop_after = stop_after
        self.nc = bass.Bass("TRN2", target_bir_lowering=False)
        self.k = KB(self.nc)
        self.inputs = {}
        self._cms = []

    def dram_in(self, name, shape, dt=F32):
        t = self.nc.dram_tensor(name, list(shape), dt, kind="ExternalInput").ap()
        self.inputs[name] = (tuple(shape), dt)
        return t

    def dram_tmp(self, name, shape, dt):
        return self.nc.dram_tensor(name, list(shape), dt, kind="Internal").ap()

    def enter(self, cm):
        v = cm.__enter__()
        self._cms.append(cm)
        return v

    def sb(self, name, shape, dt):
        self._uid = getattr(self, "_uid", 0) + 1
        return self.enter(self.nc.sbuf_tensor("%s_s%d" % (name, self._uid), list(shape), dt))

    def ps(self, name, shape, dt=F32):
        self._uid = getattr(self, "_uid", 0) + 1
        return self.enter(self.nc.psum_tensor("%s_p%d" % (name, self._uid), list(shape), dt))

    def mark(self):
        return len(self._cms)

    def release(self, mark):
        while len(self._cms) > mark:
            self._cms.pop().__exit__(None, None, None)

    def convert_weight(self, name, w_ap, K, col_blocks, bw, group=None):
        k = self.k
        KT = K // 128
        nblk = len(col_blocks)
        wb = self.dram_tmp(name + "_bf", [nblk, 128, KT, bw], BF16)
        bufs = []
        wv = w_ap.rearrange("(kt p) n -> p kt n", p=128)
        b = k.buf("wbf", group if group is not None else name)
        for bi, ranges in enumerate(col_blocks):
            off = 0
            for (c0, ncol) in ranges:
                k.dma("pool", wb[bi, :, :, off:off + ncol], wv[:, :, c0:c0 + ncol], owner=b)
                off += ncol
            bufs.append(b)
        b.writer = ("d", b)
        return wb, bufs

    def ring_init(self):
        self.ring = self.sb("wring", [128, NRING, RING_SLOT], BF16)
        self.ring_bufs = [self.k.buf("ring", i) for i in range(NRING)]
        self.ring_pos = 0
        self.ring_queue = []
        self.ring_loaded = []

    def ring_plan(self, blocks):
        self.ring_queue.extend(blocks)

    def _ring_issue_one(self):
        if not self.ring_queue:
            return False
        ap, srcbuf, a, b = self.ring_queue.pop(0)
        slot = self.ring_pos % NRING
        self.ring_pos += 1
        rb = self.ring_bufs[slot]
        view = self.ring[:, slot, 0:a * b].rearrange("p (a b) -> p a b", a=a)
        self.k.dma("sp", view, ap, owner=rb, reads=[srcbuf], writes=[rb])
        self.ring_loaded.append((slot, view))
        return True

    def ring_next(self):
        while len(self.ring_loaded) < NRING - 1 and self._ring_issue_one():
            pass
        if not self.ring_loaded:
            self._ring_issue_one()
        slot, view = self.ring_loaded.pop(0)
        return self.ring_bufs[slot], view

    def ring_prefetch(self):
        while len(self.ring_loaded) < NRING - 1 and self._ring_issue_one():
            pass


def default_plan(layers=(0, 1, 2, 3)):
    plan = []
    for L in layers:
        plan.append(("ffn", L, 0))
        plan.append((("nsa", "conv", "gla")[L % 3], L))
        plan.append(("ffn", L, 1))
    return plan


def build_program(nseq=2, layers=(0, 1, 2, 3), plan=None):
    if plan is None:
        plan = default_plan(layers)
    P = Prog(nseq=nseq, layers=layers)
    nc, k = P.nc, P.k
    NT = P.NT
    NTT = NT // TT

    x_in = P.dram_in("x", [NT, D])
    out_ap = nc.dram_tensor("out", [NT, D], F32, kind="ExternalOutput").ap()
    ln_gb = P.dram_in("ln_gb", [128, DEPTH * 3 * 2 * NDT])
    ident_in = P.dram_in("ident", [128, 128])
    ffn_win = {}
    ffn_wout = {}
    for p_ in (plan or []):
        if p_[0] == "ffn":
            L, j = p_[1], p_[2]
            ffn_win[(L, j)] = P.dram_in("ffn_w_in_%d_%d" % (L, j), [D, 2 * DFF])
            ffn_wout[(L, j)] = P.dram_in("ffn_w_out_%d_%d" % (L, j), [DFF, D])
    if any(p_[0] == "conv" for p_ in plan):
        conv_win = P.dram_in("conv_w_in", [D, 3 * D])
        conv_wout = P.dram_in("conv_w_out", [D, D])
        convw_in = P.dram_in("conv_w", [128, 3 * NDT])

    if any(p_[0] == "gla" for p_ in plan):
        gla_win = P.dram_in("gla_w_in", [D, 6160])
        gla_wout = P.dram_in("gla_w_out", [D, D])
        gla_const_in = P.dram_in("gla_const", [128, 256])
        gla_wa2_in = P.dram_in("gla_wa2", [17, 1024])
        gla_ng_in = P.dram_in("gla_ng", [64, 512])
        gla_cm_in = P.dram_in("gla_cm", [64, 64])
    nsa_in = {}
    if any(p_[0] == "nsa" for p_ in plan):
        for p_ in plan:
            if p_[0] == "nsa":
                jn = p_[1] // 3
                nsa_in[jn] = dict(
                    w_in=P.dram_in("nsa_w_in_%d" % jn, [D, 5168]),
                    w_out=P.dram_in("nsa_w_out_%d" % jn, [D, D]),
                    w1=P.dram_in("nsa_w1_%d" % jn, [2, 4096, 128]),
                    w2=P.dram_in("nsa_w2_%d" % jn, [2, 128, 128]),
                    posT=P.dram_in("nsa_posT_%d" % jn, [128, 64]),
                    gate_b=P.dram_in("nsa_gateb_%d" % jn, [128, 48]),
                )
        nsa_pos_in = P.dram_in("nsa_pos", [128, NT], I32)
        nsa_invs_in = P.dram_in("nsa_invs", [128, 2])
        nsa_validc_in = P.dram_in("nsa_validc", [128, T], BF16)
        nsa_cam_in = P.dram_in("nsa_cam", [128, 256], BF16)
        nsa_addc_in = P.dram_in("nsa_addc", [128, 512])
        nsa_esel_in = P.dram_in("nsa_esel", [128, 2048], BF16)
        nsa_ovl_in = P.dram_in("nsa_ovl", [128, 33], BF16)
    XT = [P.dram_tmp("XT%d" % i, [D, NT], F32) for i in range(2)]
    XTb = [P.dram_tmp("XTb%d" % i, [D, NT], BF16) for i in range(2)]

    ident = P.sb("ident", [128, 128], F32)
    ones = P.sb("ones", [128, 128], F32)
    lngb = P.sb("lngb", [128, DEPTH * 3 * 2 * NDT], F32)
    b_const = k.buf("const")
    k.dma("sp", ident[:], ident_in[:, :], owner=b_const, writes=[b_const])
    k.dma("sp", lngb[:], ln_gb[:, :], owner=b_const, writes=[b_const])
    if any(p_[0] == "conv" for p_ in plan):
        convw = P.sb("convw", [128, 3 * NDT], F32)
        k.dma("sp", convw[:], convw_in[:, :], owner=b_const, writes=[b_const])
    b_ones = k.buf("ones")
    k.op("dve", lambda: nc.vector.memset(ones[:], 1.0), writes=[b_ones])
    onesb = P.sb("onesb", [128, 128], BF16)
    k.op("dve", lambda: nc.vector.memset(onesb[:], 1.0), writes=[b_ones])
    P.ring_init()

    conv = {}

    def conv_ffn(L, j):
        blocks_in = [[(nb * 256, 256), (DFF + nb * 256, 256)] for nb in range(DFF // 256)]
        wb_in, bufs_in = P.convert_weight("fin_%d_%d" % (L, j), ffn_win[(L, j)], D, blocks_in, 512, group="ffn%d%d" % (L, j))
        blocks_out = [[(db * 256, 256)] for db in range(D // 256)]
        wb_out, bufs_out = P.convert_weight("fout_%d_%d" % (L, j), ffn_wout[(L, j)], DFF, blocks_out, 256, group="ffn%d%d" % (L, j))
        conv[("ffn", L, j)] = (wb_in, bufs_in, wb_out, bufs_out)

    def conv_conv(L):
        blocks_in = [[(d * 128, 128), (D + d * 128, 128), (2 * D + d * 128, 128)] for d in range(NDT)]
        wb_in, bufs_in = P.convert_weight("cin_%d" % L, conv_win, D, blocks_in, 384, group="conv")
        blocks_out = [[(db * 256, 256)] for db in range(D // 256)]
        wb_out, bufs_out = P.convert_weight("cout_%d" % L, conv_wout, D, blocks_out, 256, group="conv")
        conv[("conv", L)] = (wb_in, bufs_in, wb_out, bufs_out)

    def conv_gla(L):
        cwd = {}
        cwd["a"] = P.convert_weight("ga_%d" % L, gla_win, D, [[(6144, 16)]], 16, group="gla")
        cwd["fm"] = P.convert_weight("gfm_%d" % L, gla_win, D, [[(i * 256, 256), (1024 + i * 256, 256)] for i in range(4)], 512, group="gla")
        tmb = [[(1024 + i * 512, 512)] for i in range(2)] + [[(2048 + i * 512, 512)] for i in range(4)] + [[(4096 + i * 512, 512)] for i in range(4)]
        cwd["tm"] = P.convert_weight("gtm_%d" % L, gla_win, D, tmb, 512, group="gla")
        cwd["out"] = P.convert_weight("gout_%d" % L, gla_wout, D, [[(db * 256, 256)] for db in range(D // 256)], 256, group="gla")
        conv[("gla", L)] = cwd

    def conv_nsa(L):
        jn = L // 3
        w = nsa_in[jn]["w_in"]
        cwd = {}
        tiles = [h * 128 for h in range(16)] + [2048 + g * 128 for g in range(4)] + [3072 + g * 128 for g in range(4)] + [4096 + g * 128 for g in range(4)]
        cwd["rope"] = P.convert_weight("nrope_%d" % L, w, D, [[(c0, 128), (c0 + 64, 64), (c0, 64)] for c0 in tiles], 256, group="nsa%d" % L)
        cwd["vc"] = P.convert_weight("nvc_%d" % L, w, D, [[(2560, 512)]], 512, group="nsa%d" % L)
        cwd["tm"] = P.convert_weight("ntm_%d" % L, w, D, [[(3584, 512)], [(4608, 512)]], 512, group="nsa%d" % L)
        cwd["gl"] = P.convert_weight("ngl_%d" % L, w, D, [[(5120, 48)]], 48, group="nsa%d" % L)
        cwd["out"] = P.convert_weight("nout_%d" % L, nsa_in[jn]["w_out"], D, [[(db * 256, 256)] for db in range(D // 256)], 256, group="nsa%d" % L)
        conv[("nsa", L)] = cwd

    def conv_phase(p_):
        if p_[0] == "nsa":
            conv_nsa(p_[1])
        elif p_[0] == "ffn":
            conv_ffn(p_[1], p_[2])
        elif p_[0] == "conv":
            conv_conv(p_[1])
        elif p_[0] == "gla":
            conv_gla(p_[1])

    bsmall = k.buf("wbf", "small")
    small = {}
    for p_ in plan:
        if p_[0] == "nsa":
            jn = p_[1] // 3
            d_ = {}
            d_["w1"] = P.dram_tmp("nsa_w1b_%d" % jn, [2, 4096, 128], BF16)
            d_["w2"] = P.dram_tmp("nsa_w2b_%d" % jn, [2, 128, 128], BF16)
            d_["posT"] = P.dram_tmp("nsa_posTb_%d" % jn, [128, 64], BF16)
            for kv in range(2):
                k.dma("pool", d_["w1"][kv].rearrange("(a b) j -> a b j", a=128), nsa_in[jn]["w1"][kv].rearrange("(a b) j -> a b j", a=128), owner=bsmall)
                k.dma("pool", d_["w2"][kv], nsa_in[jn]["w2"][kv], owner=bsmall)
            k.dma("pool", d_["posT"][:, :], nsa_in[jn]["posT"][:, :], owner=bsmall)
            small[("nsa", jn)] = d_
        elif p_[0] == "gla":
            d_ = {"wa2": P.dram_tmp("gla_wa2b", [17, 1024], BF16)}
            k.dma("pool", d_["wa2"][:, :], gla_wa2_in[:, :], owner=bsmall)
            small["gla"] = d_
    bsmall.writer = ("d", bsmall)
    NAHEAD = 3
    for p_ in plan[:NAHEAD]:
        conv_phase(p_)

    def phase_in_transpose(dst):
        m = P.mark()
        xin = [P.sb("xin%d" % i, [128, D], F32) for i in range(2)]
        stg = [P.sb("stg%d" % i, [128, NDT, TT], F32) for i in range(2)]
        stgb = [P.sb("stgb%d" % i, [128, NDT, TT], BF16) for i in range(2)]
        pst = [P.ps("pst%d" % i, [128, 512], F32) for i in range(4)]
        bx = [k.buf("xin", i) for i in range(2)]
        bs = [k.buf("stg", i) for i in range(2)]
        bsb = [k.buf("stgb", i) for i in range(2)]
        bp = [k.buf("pst", i) for i in range(4)]
        cnt = 0
        pcnt = 0
        for tt in range(NTT):
            s = tt % 2
            for sub in range(TT // 128):
                t0 = tt * TT + sub * 128
                xi = cnt % 2
                cnt += 1
                k.dma("sp", xin[xi][:], x_in[t0:t0 + 128, :], owner=bx[xi], writes=[bx[xi]])
                for g in range(NDT // 4):
                    pb = pcnt % 4
                    pcnt += 1
                    for q in range(4):
                        dt_ = g * 4 + q
                        k.op("pe", lambda: nc.tensor.transpose(pst[pb][:, q * 128:(q + 1) * 128],
                                                               xin[xi][:, dt_ * 128:(dt_ + 1) * 128], ident[:]),
                             reads=[bx[xi], b_const], writes=[bp[pb]])
                    dst_v = stg[s][:, g * 4:(g + 1) * 4, sub * 128:(sub + 1) * 128]
                    dst_b = stgb[s][:, g * 4:(g + 1) * 4, sub * 128:(sub + 1) * 128]
                    src_v = pst[pb][:, :].rearrange("p (q t) -> p q t", q=4)
                    k.op("dve", lambda: nc.vector.tensor_copy(out=dst_v, in_=src_v), reads=[bp[pb]], writes=[bs[s]])
                    if os.environ.get("NO_BF16") != "1":
                        for q in range(4):
                            k.op("act", lambda: nc.scalar.copy(out=stgb[s][:, g * 4 + q, sub * 128:(sub + 1) * 128],
                                                               in_=pst[pb][:, q * 128:(q + 1) * 128]), reads=[bp[pb]], writes=[bsb[s]])
            k.dma("sp", XT[dst].rearrange("(dt p) t -> p dt t", p=128)[:, :, tt * TT:(tt + 1) * TT], stg[s][:],
                  owner=bs[s], reads=[bs[s]], writes=[k.buf("XT", dst, tt)])
            if not os.environ.get("NO_BF16"):
                k.dma("sp", XTb[dst].rearrange("(dt p) t -> p dt t", p=128)[:, :, tt * TT:(tt + 1) * TT], stgb[s][:],
                      owner=bsb[s], reads=[bsb[s]], writes=[k.buf("XTb", dst, tt)])
        k.barrier()
        P.release(m)

    def phase_out_transpose(src):
        m = P.mark()
        xin = [P.sb("oin%d" % i, [128, NDT, TT], F32) for i in range(2)]
        stg = [P.sb("ostg%d" % i, [128, D], F32) for i in range(2)]
        pst = [P.ps("opst%d" % i, [128, 512], F32) for i in range(4)]
        bx = [k.buf("oin", i) for i in range(2)]
        bs = [k.buf("ostg", i) for i in range(2)]
        bp = [k.buf("opst", i) for i in range(4)]
        cnt = 0
        pcnt = 0
        for tt in range(NTT):
            s = tt % 2
            k.dma("sp", xin[s][:], XT[src].rearrange("(dt p) t -> p dt t", p=128)[:, :, tt * TT:(tt + 1) * TT],
                  owner=bx[s], reads=[k.buf("XT", src, tt)], writes=[bx[s]])
            for sub in range(TT // 128):
                t0 = tt * TT + sub * 128
                si = cnt % 2
                cnt += 1
                for g in range(NDT // 4):
                    pb = pcnt % 4
                    pcnt += 1
                    for q in range(4):
                        dt_ = g * 4 + q
                        k.op("pe", lambda: nc.tensor.transpose(pst[pb][:, q * 128:(q + 1) * 128],
                                                               xin[s][:, dt_, sub * 128:(sub + 1) * 128], ident[:]),
                             reads=[bx[s], b_const], writes=[bp[pb]])
                    dst_v = stg[si][:, g * 512:(g + 1) * 512]
                    if g % 2 == 0:
                        k.op("dve", lambda: nc.vector.tensor_copy(out=dst_v, in_=pst[pb][:, :]), reads=[bp[pb]], writes=[bs[si]])
                    else:
                        k.op("act", lambda: nc.scalar.copy(out=dst_v, in_=pst[pb][:, :]), reads=[bp[pb]], writes=[bs[si]])
                k.dma("sp", out_ap[t0:t0 + 128, :], stg[si][:], owner=bs[si], reads=[bs[si]], writes=[k.buf("out", t0)])
        k.barrier()
        P.release(m)

    def ln_cols(L, j):
        base = ((L * 3 + j) * 2) * NDT
        return base, base + NDT

    def phase_rowlocal(L, lnj, src, dst, KT, c_res, wout, stage_blocks, stage_alloc, stage_run, need_xT=True):
        wb_out, bufs_out = wout
        NBO = D // 256
        blocks = []
        for tt in range(NTT):
            blocks.extend(stage_blocks(tt))
            for db in range(NBO):
                blocks.append((wb_out[db], bufs_out[db], KT, 256))
        P.ring_plan(blocks)
        m = P.mark()
        C = {}
        nz = 1 if need_xT else 2
        zs = [P.sb("z%d" % i, [128, NDT, TT], F32) for i in range(nz)]
        xT = P.sb("xT", [128, NDT, TT], BF16) if need_xT else None
        hT = P.sb("hT", [128, KT, TT], BF16)
        zsq = [P.sb("zsq%d" % i, [128, TT], F32) for i in range(2)]
        spl = [[P.sb("spl%d_%d" % (i, j), [128, TT], BF16) for j in range(4)] for i in range(2)]
        bspl = [[k.buf("spl", i, j) for j in range(4)] for i in range(2)]
        zb = [P.sb("zb%d" % i, [128, TT], BF16) for i in range(2)]
        mean = P.sb("mean", [128, TT], F32)
        rstd = P.sb("rstd", [128, TT], F32)
        tmpv = P.sb("tmpv", [128, TT], F32)
        pin = [P.ps("pin%d" % i, [128, TT]) for i in range(4)]
        py = [P.ps("py%d" % i, [128, TT]) for i in range(2)]
        ps1 = P.ps("ps1", [128, TT])
        ps2 = P.ps("ps2", [128, TT])
        bzs = [[k.buf("z", i, d) for d in range(NDT)] for i in range(nz)]
        bxT = k.buf("xT")
        bh = [k.buf("hT", n) for n in range(KT)]
        bzsq = [k.buf("zsq", i) for i in range(2)]
        bzb = [k.buf("zb", i) for i in range(2)]
        bzls = [[k.buf("zl", j, i) for i in range(4)] for j in range(nz)]
        bzalls = [k.buf("zstore", j) for j in range(nz)]
        bmean, brstd, btmpv = k.buf("mean"), k.buf("rstd"), k.buf("tmpv")
        bpin = [k.buf("pin", i) for i in range(4)]
        bpy = [k.buf("py", i) for i in range(2)]
        bps1, bps2 = k.buf("ps1"), k.buf("ps2")
        gcol, bcol = ln_cols(L, lnj)
        eps_p = LN_EPS / (ALPHA * ALPHA)
        XTs = XT[src].rearrange("(dt p) t -> p dt t", p=128)
        XTd = XT[dst].rearrange("(dt p) t -> p dt t", p=128)
        XTbs = XTb[src].rearrange("(dt p) t -> p dt t", p=128)
        XTbd = XTb[dst].rearrange("(dt p) t -> p dt t", p=128)
        C.update(xT=xT, bxT=bxT, hT=hT, bh=bh, pin=pin, bpin=bpin, rot=0)
        stage_alloc(C)
        dq = []

        def pump(n):
            while n > 0 and dq:
                dq.pop(0)[1]()
                n -= 1

        def flush_upto(tile):
            while dq and dq[0][0] <= tile:
                dq.pop(0)[1]()

        C["pump"] = pump

        def load_xT(tt):
            if need_xT:
                k.dma("sp", xT[:], XTbs[:, :, tt * TT:(tt + 1) * TT], owner=bxT,
                      reads=[k.buf("XTb", src, tt)], writes=[bxT])

        def load_z(tt):
            flush_upto(tt - nz)
            z, bz, bzl = zs[tt % nz], bzs[tt % nz], bzls[tt % nz]
            for g4 in range(4):
                k.dma("sp", z[:, g4 * 4:(g4 + 1) * 4, :], XTs[:, g4 * 4:(g4 + 1) * 4, tt * TT:(tt + 1) * TT], owner=bzl[g4],
                      reads=[k.buf("XT", src, tt)], writes=bz[g4 * 4:(g4 + 1) * 4])

        def make_epilogue(tt):
            tsl = slice(tt * TT, (tt + 1) * TT)
            z, bz, bzall = zs[tt % nz], bzs[tt % nz], bzalls[tt % nz]
            items = []

            def add(fn):
                items.append((tt, fn))

            add(lambda: k.op("dve", lambda: nc.vector.tensor_scalar(out=mean[:], in0=ps1[:], scalar1=1.0 / D, scalar2=None, op0=ALU.mult),
                             reads=[bps1], writes=[bmean]))
            add(lambda: k.op("dve", lambda: nc.vector.tensor_tensor(out=tmpv[:], in0=mean[:], in1=mean[:], op=ALU.mult),
                             reads=[bmean], writes=[btmpv]))
            add(lambda: k.op("dve", lambda: nc.vector.scalar_tensor_tensor(out=tmpv[:], in0=ps2[:], scalar=1.0 / D, in1=tmpv[:],
                                                                           op0=ALU.mult, op1=ALU.subtract),
                             reads=[bps2, btmpv], writes=[btmpv]))
            add(lambda: k.op("dve", lambda: nc.vector.tensor_scalar(out=tmpv[:], in0=tmpv[:], scalar1=eps_p, scalar2=None, op0=ALU.add),
                             reads=[btmpv], writes=[btmpv]))
            add(lambda: k.op("act", lambda: nc.scalar.activation(out=rstd[:], in_=tmpv[:], func=AF.Sqrt),
                             reads=[btmpv], writes=[brstd]))
            add(lambda: k.op("dve", lambda: nc.vector.reciprocal(out=rstd[:], in_=rstd[:]), reads=[brstd], writes=[brstd]))
            for d in range(NDT):
                zi = d % 2
                add(lambda d=d: k.op("dve", lambda: nc.vector.tensor_tensor(out=z[:, d, :], in0=z[:, d, :], in1=mean[:], op=ALU.subtract),
                                     reads=[bz[d], bmean], writes=[bz[d]]))
                add(lambda d=d: k.op("dve", lambda: nc.vector.tensor_tensor(out=z[:, d, :], in0=z[:, d, :], in1=rstd[:], op=ALU.mult),
                                     reads=[bz[d], brstd], writes=[bz[d]]))
                add(lambda d=d: k.op("act", lambda: nc.scalar.activation(out=z[:, d, :], in_=z[:, d, :], func=AF.Identity,
                                                                         scale=lngb[:, gcol + d:gcol + d + 1], bias=lngb[:, bcol + d:bcol + d + 1]),
                                     reads=[bz[d], b_const], writes=[bz[d]]))
                add(lambda d=d, zi=zi: k.op("act", lambda: nc.scalar.copy(out=zb[zi][:], in_=z[:, d, :]), reads=[bz[d]], writes=[bzb[zi]]))
                add(lambda d=d, zi=zi: k.dma("act", XTbd[:, d, tsl], zb[zi][:], owner=bzb[zi], reads=[bzb[zi]], writes=[k.buf("XTb", dst, tt)]))
            add(lambda: k.dma("act", XTd[:, :, tsl], z[:], owner=bzall, reads=bz, writes=[k.buf("XT", dst, tt)]))
            return items

        def stats_mm(pd, pyi):
            k.op("pe", lambda: nc.tensor.matmul(ps1[:], lhsT=onesb[:], rhs=spl[pyi][0][:], start=(pd == 0), stop=False),
                 reads=[b_ones, bspl[pyi][0]], writes=[bps1])
            k.op("pe", lambda: nc.tensor.matmul(ps1[:], lhsT=onesb[:], rhs=spl[pyi][1][:], start=False, stop=(pd == NDT - 1)),
                 reads=[b_ones, bspl[pyi][1]], writes=[bps1])
            k.op("pe", lambda: nc.tensor.matmul(ps2[:], lhsT=onesb[:], rhs=spl[pyi][2][:], start=(pd == 0), stop=False),
                 reads=[b_ones, bspl[pyi][2]], writes=[bps2])
            k.op("pe", lambda: nc.tensor.matmul(ps2[:], lhsT=onesb[:], rhs=spl[pyi][3][:], start=False, stop=(pd == NDT - 1)),
                 reads=[b_ones, bspl[pyi][3]], writes=[bps2])

        C["load_z"] = load_z
        load_xT(0)
        cy = 0
        for tt in range(NTT):
            C["z_loaded"] = False
            z, bz = zs[tt % nz], bzs[tt % nz]
            stage_run(tt, C)
            if not C["z_loaded"]:
                load_z(tt)
            if tt + 1 < NTT:
                load_xT(tt + 1)
            pend = None
            for db in range(NBO):
                wbuf, wv = P.ring_next()
                for jd in range(2):
                    d = db * 2 + jd
                    yi = cy % 2
                    cy += 1
                    for kk in range(KT):
                        k.op("pe", lambda: nc.tensor.matmul(py[yi][:], lhsT=wv[:, kk, jd * 128:(jd + 1) * 128],
                                                            rhs=hT[:, kk, :], start=(kk == 0), stop=(kk == KT - 1)),
                             reads=[wbuf, bh[kk]], writes=[bpy[yi]])
                    k.op("dve", lambda: nc.vector.scalar_tensor_tensor(out=z[:, d, :], in0=py[yi][:], scalar=c_res,
                                                                       in1=z[:, d, :], op0=ALU.mult, op1=ALU.add),
                         reads=[bpy[yi], bz[d]], writes=[bz[d]])
                    k.op("act", lambda: nc.scalar.activation(out=zsq[yi][:], in_=z[:, d, :], func=AF.Square),
                         reads=[bz[d]], writes=[bzsq[yi]])
                    k.op("act", lambda: nc.scalar.copy(out=spl[yi][0][:], in_=z[:, d, :]), reads=[bz[d]], writes=[bspl[yi][0]])
                    k.op("dve", lambda: nc.vector.tensor_tensor(out=spl[yi][1][:], in0=z[:, d, :], in1=spl[yi][0][:], op=ALU.subtract),
                         reads=[bz[d], bspl[yi][0]], writes=[bspl[yi][1]])
                    k.op("act", lambda: nc.scalar.copy(out=spl[yi][2][:], in_=zsq[yi][:]), reads=[bzsq[yi]], writes=[bspl[yi][2]])
                    k.op("dve", lambda: nc.vector.tensor_tensor(out=spl[yi][3][:], in0=zsq[yi][:], in1=spl[yi][2][:], op=ALU.subtract),
                         reads=[bzsq[yi], bspl[yi][2]], writes=[bspl[yi][3]])
                    if pend is not None:
                        pd, pyi = pend
                        stats_mm(pd, pyi)
                    pend = (d, yi)
                    if not need_xT:
                        pump(6)
            pd, pyi = pend
            stats_mm(pd, pyi)
            P.ring_prefetch()
            dq.extend(make_epilogue(tt))
            if tt == NTT - 1:
                flush_upto(tt)
        k.barrier()
        P.release(m)

    def run_ffn(L, j, src, dst):
        wb_in, bufs_in, wb_out, bufs_out = conv[("ffn", L, j)]
        NBI = DFF // 256

        def blocks(tt):
            return [(wb_in[nb], bufs_in[nb], 16, 512) for nb in range(NBI)]

        def alloc(C):
            C["sg"] = [P.sb("sg%d" % i, [128, TT], F32) for i in range(2)]
            C["bsg"] = [k.buf("sg", i) for i in range(2)]
            C["cg"] = 0

        def run(tt, C):
            xT, bxT, hT, bh, pin, bpin, sg, bsg = (C[n] for n in ("xT", "bxT", "hT", "bh", "pin", "bpin", "sg", "bsg"))
            for nb in range(NBI):
                wbuf, wv = P.ring_next()
                if nb == 8:
                    C["load_z"](tt)
                    C["z_loaded"] = True
                for jn in range(2):
                    n = nb * 2 + jn
                    pi = C["cg"] % 2
                    C["cg"] += 1
                    pgb, pub = pin[pi], pin[2 + pi]
                    for kk in range(NDT):
                        k.op("pe", lambda: nc.tensor.matmul(pgb[:], lhsT=wv[:, kk, jn * 128:(jn + 1) * 128],
                                                            rhs=xT[:, kk, :], start=(kk == 0), stop=(kk == NDT - 1)),
                             reads=[wbuf, bxT], writes=[bpin[pi]])
                    for kk in range(NDT):
                        k.op("pe", lambda: nc.tensor.matmul(pub[:], lhsT=wv[:, kk, 256 + jn * 128:256 + (jn + 1) * 128],
                                                            rhs=xT[:, kk, :], start=(kk == 0), stop=(kk == NDT - 1)),
                             reads=[wbuf, bxT], writes=[bpin[2 + pi]])
                    k.op("act", lambda: nc.scalar.activation(out=sg[pi][:], in_=pgb[:], func=AF.Silu),
                         reads=[bpin[pi]], writes=[bsg[pi]])
                    k.op("dve", lambda: nc.vector.tensor_tensor(out=hT[:, n, :], in0=pub[:], in1=sg[pi][:], op=ALU.mult),
                         reads=[bpin[2 + pi], bsg[pi]], writes=[bh[n]])
                    C["pump"](8)

        phase_rowlocal(L, 0 if j == 0 else 2, src, dst, NHT, 0.5 / ALPHA, (wb_out, bufs_out), blocks, alloc, run)

    def run_conv(L, src, dst):
        wb_in, bufs_in, wb_out, bufs_out = conv[("conv", L)]

        def blocks(tt):
            return [(wb_in[d], bufs_in[d], 16, 384) for d in range(NDT)]

        def alloc(C):
            C["csb"] = [P.sb("csb%d" % i, [128, TT], F32) for i in range(2)]
            C["u"] = [P.sb("u%d" % i, [128, TT + 2], F32) for i in range(2)]
            C["yy"] = [P.sb("yy%d" % i, [128, TT], F32) for i in range(2)]
            C["uh"] = P.sb("uh", [128, NDT, 2], F32)
            C["bcsb"] = [k.buf("csb", i) for i in range(2)]
            C["bu"] = [k.buf("u", i) for i in range(2)]
            C["byy"] = [k.buf("yy", i) for i in range(2)]
            C["buh"] = [k.buf("uh", d) for d in range(NDT)]
            C["ci"] = 0

        def run(tt, C):
            xT, bxT, hT, bh, pin, bpin = (C[n] for n in ("xT", "bxT", "hT", "bh", "pin", "bpin"))
            seq_start = (tt % (T // TT) == 0)
            for d in range(NDT):
                wbuf, wv = P.ring_next()
                if d == 11:
                    C["load_z"](tt)
                    C["z_loaded"] = True
                i2 = C["ci"] % 2
                C["ci"] += 1
                banks = []
                for which in (1, 2, 0):
                    r = C["rot"] % 4
                    C["rot"] += 1
                    for kk in range(NDT):
                        k.op("pe", lambda: nc.tensor.matmul(pin[r][:], lhsT=wv[:, kk, which * 128:(which + 1) * 128],
                                                            rhs=xT[:, kk, :], start=(kk == 0), stop=(kk == NDT - 1)),
                             reads=[wbuf, bxT], writes=[bpin[r]])
                    banks.append(r)
                rc, rh, rb = banks
                csb, u, yy = C["csb"][i2], C["u"][i2], C["yy"][i2]
                bcsb, bu, byy, buh = C["bcsb"][i2], C["bu"][i2], C["byy"][i2], C["buh"][d]
                uh = C["uh"]
                k.op("act", lambda: nc.scalar.copy(out=csb[:], in_=pin[rc][:]), reads=[bpin[rc]], writes=[bcsb])
                k.op("dve", lambda: nc.vector.tensor_tensor(out=u[:, 2:TT + 2], in0=pin[rh][:], in1=csb[:], op=ALU.mult),
                     reads=[bpin[rh], bcsb], writes=[bu])
                if seq_start:
                    k.op("dve", lambda: nc.vector.memset(u[:, 0:2], 0.0), reads=[], writes=[bu])
                else:
                    k.op("dve", lambda: nc.vector.tensor_copy(out=u[:, 0:2], in_=uh[:, d, :]), reads=[buh], writes=[bu])
                k.op("dve", lambda: nc.vector.tensor_copy(out=uh[:, d, :], in_=u[:, TT:TT + 2]), reads=[bu], writes=[buh])
                k.op("dve", lambda: nc.vector.tensor_scalar(out=yy[:], in0=u[:, 2:TT + 2], scalar1=convw[:, 2 * NDT + d:2 * NDT + d + 1],
                                                            scalar2=None, op0=ALU.mult),
                     reads=[bu, b_const], writes=[byy])
                k.op("dve", lambda: nc.vector.scalar_tensor_tensor(out=yy[:], in0=u[:, 1:TT + 1], scalar=convw[:, NDT + d:NDT + d + 1],
                                                                   in1=yy[:], op0=ALU.mult, op1=ALU.add),
                     reads=[bu, byy, b_const], writes=[byy])
                k.op("dve", lambda: nc.vector.scalar_tensor_tensor(out=yy[:], in0=u[:, 0:TT], scalar=convw[:, d:d + 1],
                                                                   in1=yy[:], op0=ALU.mult, op1=ALU.add),
                     reads=[bu, byy, b_const], writes=[byy])
                k.op("dve", lambda: nc.vector.tensor_tensor(out=hT[:, d, :], in0=pin[rb][:], in1=yy[:], op=ALU.mult),
                     reads=[bpin[rb], byy], writes=[bh[d]])
                C["pump"](9)

        phase_rowlocal(L, 1, src, dst, NDT, 1.0 / ALPHA, (wb_out, bufs_out), blocks, alloc, run)

    def run_outproj_from_dram(L, src, dst, OT, otname, wout):
        OTv = OT.rearrange("(dt p) t -> p dt t", p=128)

        def blocks(tt):
            return []

        def alloc(C):
            C["bhl"] = k.buf("hload")

        def run(tt, C):
            k.dma("sp", C["hT"][:, :, :], OTv[:, :, tt * TT:(tt + 1) * TT], owner=C["bhl"],
                  reads=[k.buf(otname, tt)], writes=C["bh"])

        phase_rowlocal(L, 1, src, dst, NDT, 1.0 / ALPHA, wout, blocks, alloc, run, need_xT=False)

    def run_gla(L, src, dst):
        cw = conv[("gla", L)]
        NCH = NT // 64
        QD = P.dram_tmp("gla_QD", [1024, NT], BF16)
        KD = P.dram_tmp("gla_KD", [1024, NT], BF16)
        KL = P.dram_tmp("gla_KL", [NT, 1024], BF16)
        VV = P.dram_tmp("gla_V", [NT, 2048], BF16)
        RS = P.dram_tmp("gla_RS", [NT, 2048], F32)
        OT = P.dram_tmp("gla_OT", [D, NT], BF16)
        m0 = P.mark()
        dec = P.sb("gdec", [128, 8, NCH], F32)
        bdec = k.buf("gdec")
        gc = P.sb("gconst", [128, 256], F32)
        wa2 = P.sb("gwa2", [32, 1024], BF16)
        gng = P.sb("gng", [64, 512], F32)
        cmask = P.sb("gcm", [64, 64], F32)
        identb = P.sb("gidb", [128, 128], BF16)
        bgc = k.buf("gconst")
        k.dma("sp", gc[:], gla_const_in[:, :], owner=bgc, writes=[bgc])
        bgc2 = k.buf("gconst2")
        k.dma("sp", wa2[0:17, :], small["gla"]["wa2"][:, :], owner=bgc2, reads=[bsmall], writes=[bgc2])
        k.dma("sp", gng[:], gla_ng_in[:, :], owner=bgc, writes=[bgc])
        k.dma("sp", cmask[:], gla_cm_in[:, :], owner=bgc, writes=[bgc])
        k.op("act", lambda: nc.scalar.copy(out=identb[:], in_=ident[:]), reads=[b_const], writes=[bgc])
        tri16 = gc[:, 0:128]
        m16 = gc[:, 128:256]

        wfm, bfm = cw["fm"]
        wa, ba = cw["a"]
        wtm, btm = cw["tm"]
        blocks = []
        for tt in range(NTT):
            blocks.append((wa[0], ba[0], 16, 16))
            for i in range(4):
                blocks.append((wfm[i], bfm[i], 16, 512))
            for i in range(10):
                blocks.append((wtm[i], btm[i], 16, 512))
        P.ring_plan(blocks)
        m = P.mark()
        xT = P.sb("xT", [128, NDT, TT], BF16)
        bxT = k.buf("xT")
        gk = P.sb("gk", [128, 4, 1024], F32)
        bgk = k.buf("gk")
        aT = P.sb("aT", [32, TT], BF16)
        baT = k.buf("aT")
        k.op("dve", lambda: nc.vector.memset(aT[:], 1.0), writes=[baT])
        tmpe = [P.sb("tmpe%d" % i, [128, TT], F32) for i in range(6)]
        btmpe = [k.buf("tmpe", i) for i in range(6)]
        qd = P.sb("qd", [128, 8, TT], BF16)
        kd = P.sb("kd", [128, 8, TT], BF16)
        klst = P.sb("klst", [128, 4, 1024], BF16)
        vst = P.sb("vst", [128, 4, 2048], BF16)
        rst = [P.sb("rst%d" % i, [128, 4, 512], F32) for i in range(2)]
        bqd, bkd, bklst, bvst = k.buf("qd"), k.buf("kd"), k.buf("klst"), k.buf("vst")
        brst = [k.buf("rst", i) for i in range(2)]
        banks = [P.ps("gb%d" % i, [128, 512]) for i in range(8)]
        bbanks = [k.buf("gb", i) for i in range(8)]
        st = {"r": 0, "e": 0}

        def bank():
            i = st["r"] % 8
            st["r"] += 1
            return banks[i], bbanks[i]

        def tmp():
            i = st["e"] % 6
            st["e"] += 1
            return tmpe[i], btmpe[i]

        XTbs = XTb[src].rearrange("(dt p) t -> p dt t", p=128)
        for tt in range(NTT):
            tsl = slice(tt * TT, (tt + 1) * TT)
            k.dma("sp", xT[:], XTbs[:, :, tsl], owner=bxT, reads=[k.buf("XTb", src, tt)], writes=[bxT])
            wbuf, wv = P.ring_next()
            pb, bpb = bank()
            for kk in range(NDT):
                k.op("pe", lambda: nc.tensor.matmul(pb[0:16, :], lhsT=wv[:, kk, 0:16], rhs=xT[:, kk, :],
                                                    start=(kk == 0), stop=(kk == NDT - 1)), reads=[wbuf, bxT], writes=[bpb])
            k.op("act", lambda: nc.scalar.copy(out=aT[0:16, :], in_=pb[0:16, :]), reads=[bpb], writes=[baT])
            for sub in range(4):
                for half in range(2):
                    pb, bpb = bank()
                    k.op("pe", lambda: nc.tensor.matmul(pb[:], lhsT=aT[0:17, sub * 128:(sub + 1) * 128],
                                                        rhs=wa2[0:17, half * 512:(half + 1) * 512], start=True, stop=True),
                         reads=[baT, bgc2], writes=[bpb])
                    k.op("act", lambda: nc.scalar.activation(out=gk[:, sub, half * 512:(half + 1) * 512], in_=pb[:], func=AF.Sigmoid),
                         reads=[bpb], writes=[bgk])
            k.op("act", lambda: nc.scalar.activation(out=gk[:], in_=gk[:], func=AF.Ln), reads=[bgk], writes=[bgk])
            for i in range(4):
                wbuf, wv = P.ring_next()
                for jj in range(2):
                    dt_ = 2 * i + jj
                    pb, bpb = bank()
                    for sub in range(4):
                        k.op("pe", lambda: nc.tensor.matmul(pb[:, sub * 128:(sub + 1) * 128], lhsT=gk[:, sub, dt_ * 128:(dt_ + 1) * 128],
                                                            rhs=tri16, start=True, stop=True), reads=[bgk, bgc], writes=[bpb])
                    ebq, bebq = tmp()
                    ebk, bebk = tmp()
                    k.op("act", lambda: nc.scalar.activation(out=ebq[:], in_=pb[:], func=AF.Exp), reads=[bpb], writes=[bebq])
                    k.op("act", lambda: nc.scalar.activation(out=ebk[:], in_=pb[:], func=AF.Exp, scale=-1.0), reads=[bpb], writes=[bebk])
                    k.op("act", lambda: nc.scalar.activation(out=dec[:, dt_, tt * 8:(tt + 1) * 8],
                                                             in_=pb[:, :].rearrange("p (c s) -> p c s", s=64)[:, :, 63], func=AF.Exp),
                         reads=[bpb], writes=[bdec])
                    pq, bpq = bank()
                    for kk in range(NDT):
                        k.op("pe", lambda: nc.tensor.matmul(pq[:], lhsT=wv[:, kk, jj * 128:(jj + 1) * 128], rhs=xT[:, kk, :],
                                                            start=(kk == 0), stop=(kk == NDT - 1)), reads=[wbuf, bxT], writes=[bpq])
                    k.op("dve", lambda: nc.vector.scalar_tensor_tensor(out=qd[:, dt_, :], in0=pq[:], scalar=1.0 / 16.0, in1=ebq[:],
                                                                       op0=ALU.mult, op1=ALU.mult), reads=[bpq, bebq], writes=[bqd])
                    pk, bpk = bank()
                    for kk in range(NDT):
                        k.op("pe", lambda: nc.tensor.matmul(pk[:], lhsT=wv[:, kk, 256 + jj * 128:256 + (jj + 1) * 128], rhs=xT[:, kk, :],
                                                            start=(kk == 0), stop=(kk == NDT - 1)), reads=[wbuf, bxT], writes=[bpk])
                    k.op("dve", lambda: nc.vector.tensor_tensor(out=kd[:, dt_, :], in0=pk[:], in1=ebk[:], op=ALU.mult),
                         reads=[bpk, bebk], writes=[bkd])
            k.dma("act", QD.rearrange("(dt p) t -> p dt t", p=128)[:, :, tsl], qd[:], owner=bqd, reads=[bqd], writes=[k.buf("gQD", tt)])
            k.dma("act", KD.rearrange("(dt p) t -> p dt t", p=128)[:, :, tsl], kd[:], owner=bkd, reads=[bkd], writes=[k.buf("gKD", tt)])
            for blk in range(2):
                wbuf, wv = P.ring_next()
                for sub in range(4):
                    pb, bpb = bank()
                    k.op("pe", lambda: nc.tensor.matmul(pb[:], lhsT=m16, rhs=gk[:, sub, blk * 512:(blk + 1) * 512], start=True, stop=True),
                         reads=[bgk, bgc], writes=[bpb])
                    kle, bkle = tmp()
                    k.op("act", lambda: nc.scalar.activation(out=kle[:], in_=pb[:], func=AF.Exp), reads=[bpb], writes=[bkle])
                    pk, bpk = bank()
                    for kk in range(NDT):
                        k.op("pe", lambda: nc.tensor.matmul(pk[:], lhsT=xT[:, kk, sub * 128:(sub + 1) * 128], rhs=wv[:, kk, :],
                                                            start=(kk == 0), stop=(kk == NDT - 1)), reads=[wbuf, bxT], writes=[bpk])
                    k.op("dve", lambda: nc.vector.tensor_tensor(out=klst[:, sub, blk * 512:(blk + 1) * 512], in0=pk[:], in1=kle[:], op=ALU.mult),
                         reads=[bpk, bkle], writes=[bklst])
            k.dma("act", KL[tsl, :].rearrange("(s p) d -> p s d", p=128), klst[:], owner=bklst, reads=[bklst], writes=[k.buf("gKL", tt)])
            for blk in range(4):
                wbuf, wv = P.ring_next()
                for sub in range(4):
                    pk, bpk = bank()
                    for kk in range(NDT):
                        k.op("pe", lambda: nc.tensor.matmul(pk[:], lhsT=xT[:, kk, sub * 128:(sub + 1) * 128], rhs=wv[:, kk, :],
                                                            start=(kk == 0), stop=(kk == NDT - 1)), reads=[wbuf, bxT], writes=[bpk])
                    k.op("act", lambda: nc.scalar.copy(out=vst[:, sub, blk * 512:(blk + 1) * 512], in_=pk[:]), reads=[bpk], writes=[bvst])
            k.dma("act", VV[tsl, :].rearrange("(s p) d -> p s d", p=128), vst[:], owner=bvst, reads=[bvst], writes=[k.buf("gV", tt)])
            for blk in range(4):
                wbuf, wv = P.ring_next()
                ri = blk % 2
                for sub in range(4):
                    pk, bpk = bank()
                    for kk in range(NDT):
                        k.op("pe", lambda: nc.tensor.matmul(pk[:], lhsT=xT[:, kk, sub * 128:(sub + 1) * 128], rhs=wv[:, kk, :],
                                                            start=(kk == 0), stop=(kk == NDT - 1)), reads=[wbuf, bxT], writes=[bpk])
                    k.op("act", lambda: nc.scalar.activation(out=rst[ri][:, sub, :], in_=pk[:], func=AF.Silu), reads=[bpk], writes=[brst[ri]])
                k.dma("act", RS[tsl, blk * 512:(blk + 1) * 512].rearrange("(s p) d -> p s d", p=128), rst[ri][:], owner=brst[ri],
                      reads=[brst[ri]], writes=[k.buf("gRS", tt, blk)])
        k.barrier()
        P.release(m)

        m = P.mark()
        TB = 256
        qdl = P.sb("qdl", [128, 8, TB], BF16)
        kdl = P.sb("kdl", [128, 8, TB], BF16)
        kll = P.sb("kll", [64, 4, 1024], BF16)
        vl = P.sb("vl", [64, 4, 2048], BF16)
        rsl = [P.sb("rsl%d" % i, [64, 2048], F32) for i in range(2)]
        S = P.sb("gS", [128, 8, 512], F32)
        Sb = P.sb("gSb", [128, 8, 512], BF16)
        atm = [P.sb("atm%d" % i, [64, 64], BF16) for i in range(2)]
        og = [P.sb("og%d" % i, [64, 512], F32) for i in range(2)]
        ogb = [P.sb("ogb%d" % i, [64, 2048], BF16) for i in range(2)]
        sq = P.sb("gsq", [64, 512], F32)
        ms = [P.sb("gms%d" % i, [64, 1], F32) for i in range(4)]
        otst = [P.sb("otst%d" % i, [128, NDT, TB], BF16) for i in range(2)]
        bqdl, bkdl, bkll, bvl = k.buf("qdl"), k.buf("kdl"), k.buf("kll"), k.buf("vl")
        brsl = [k.buf("rsl", i) for i in range(2)]
        bS = [k.buf("gS", i) for i in range(8)]
        bSb = [k.buf("gSb", i) for i in range(8)]
        batm = [k.buf("atm", i) for i in range(2)]
        bog = [k.buf("og", i) for i in range(2)]
        bogb = [k.buf("ogb", i) for i in range(2)]
        bsq = k.buf("gsq")
        bms = [k.buf("gms", i) for i in range(4)]
        botst = [k.buf("otst", i) for i in range(2)]
        banks = [P.ps("hb%d" % i, [128, 512]) for i in range(6)]
        bbanks = [k.buf("hb", i) for i in range(6)]
        tb = [P.ps("htb%d" % i, [128, 512], BF16) for i in range(2)]
        btb = [k.buf("htb", i) for i in range(2)]
        st = {"r": 0, "a": 0, "m": 0, "t": 0}

        def bank():
            i = st["r"] % 6
            st["r"] += 1
            return banks[i], bbanks[i]

        for tb_ in range(NT // TB):
            tsl = slice(tb_ * TB, (tb_ + 1) * TB)
            tt = (tb_ * TB) // TT
            oi = tb_ % 2
            if tb_ % (T // TB) == 0:
                for i in range(8):
                    k.op("dve", lambda: nc.vector.memset(S[:, i, :], 0.0), writes=[bS[i]])
                    k.op("dve", lambda: nc.vector.memset(Sb[:, i, :], 0.0), writes=[bSb[i]])
            k.dma("sp", qdl[:], QD.rearrange("(dt p) t -> p dt t", p=128)[:, :, tsl], owner=bqdl, reads=[k.buf("gQD", tt)], writes=[bqdl])
            k.dma("sp", kdl[:], KD.rearrange("(dt p) t -> p dt t", p=128)[:, :, tsl], owner=bkdl, reads=[k.buf("gKD", tt)], writes=[bkdl])
            k.dma("sp", kll[:], KL[tsl, :].rearrange("(n c) d -> c n d", c=64), owner=bkll, reads=[k.buf("gKL", tt)], writes=[bkll])
            k.dma("sp", vl[:], VV[tsl, :].rearrange("(n c) d -> c n d", c=64), owner=bvl, reads=[k.buf("gV", tt)], writes=[bvl])
            for n in range(TB // 64):
                ch = tb_ * (TB // 64) + n
                csl = slice(n * 64, (n + 1) * 64)
                ri = ch % 2
                k.dma("sp", rsl[ri][:], RS[ch * 64:(ch + 1) * 64, :], owner=brsl[ri],
                      reads=[k.buf("gRS", tt, b_) for b_ in range(4)], writes=[brsl[ri]])
                for h in range(4):
                    pa, bpa = bank()
                    for dt2 in range(2):
                        k.op("pe", lambda: nc.tensor.matmul(pa[0:64, 0:64], lhsT=kdl[:, h * 2 + dt2, csl], rhs=qdl[:, h * 2 + dt2, csl],
                                                            start=(dt2 == 0), stop=(dt2 == 1)), reads=[bkdl, bqdl], writes=[bpa])
                    ai = st["a"] % 2
                    st["a"] += 1
                    k.op("dve", lambda: nc.vector.tensor_tensor(out=atm[ai][:], in0=pa[0:64, 0:64], in1=cmask[:], op=ALU.mult),
                         reads=[bpa, bgc], writes=[batm[ai]])
                    po, bpo = bank()
                    for dt2 in range(2):
                        k.op("pe", lambda: nc.tensor.matmul(po[0:64, :], lhsT=qdl[:, h * 2 + dt2, csl], rhs=Sb[:, h * 2 + dt2, :],
                                                            start=(dt2 == 0), stop=False), reads=[bqdl, bSb[h * 2 + dt2]], writes=[bpo])
                    k.op("pe", lambda: nc.tensor.matmul(po[0:64, :], lhsT=atm[ai][:], rhs=vl[:, n, h * 512:(h + 1) * 512],
                                                        start=False, stop=True), reads=[batm[ai], bvl], writes=[bpo])
                    for dt2 in range(2):
                        si = h * 2 + dt2
                        psn, bpsn = bank()
                        k.op("pe", lambda: nc.tensor.matmul(psn[:], lhsT=kll[:, n, si * 128:(si + 1) * 128], rhs=vl[:, n, h * 512:(h + 1) * 512],
                                                            start=True, stop=True), reads=[bkll, bvl], writes=[bpsn])
                        k.op("dve", lambda: nc.vector.scalar_tensor_tensor(out=S[:, si, :], in0=S[:, si, :], scalar=dec[:, si, ch:ch + 1],
                                                                           in1=psn[:], op0=ALU.mult, op1=ALU.add),
                             reads=[bS[si], bdec, bpsn], writes=[bS[si]])
                        k.op("act", lambda: nc.scalar.copy(out=Sb[:, si, :], in_=S[:, si, :]), reads=[bS[si]], writes=[bSb[si]])
                    mi = st["m"] % 4
                    st["m"] += 1
                    k.op("act", lambda: nc.scalar.activation(out=sq[:], in_=po[0:64, :], func=AF.Square, accum_out=ms[mi][:]),
                         reads=[bpo], writes=[bsq, bms[mi]])
                    k.op("dve", lambda: nc.vector.tensor_scalar(out=ms[mi][:], in0=ms[mi][:], scalar1=1.0 / 512.0, scalar2=LN_EPS,
                                                                op0=ALU.mult, op1=ALU.add), reads=[bms[mi]], writes=[bms[mi]])
                    k.op("act", lambda: nc.scalar.activation(out=ms[mi][:], in_=ms[mi][:], func=AF.Sqrt), reads=[bms[mi]], writes=[bms[mi]])
                    k.op("dve", lambda: nc.vector.reciprocal(out=ms[mi][:], in_=ms[mi][:]), reads=[bms[mi]], writes=[bms[mi]])
                    gi = st["m"] % 2
                    k.op("dve", lambda: nc.vector.scalar_tensor_tensor(out=og[gi][:], in0=po[0:64, :], scalar=ms[mi][:, 0:1], in1=gng[:],
                                                                       op0=ALU.mult, op1=ALU.mult), reads=[bpo, bms[mi], bgc], writes=[bog[gi]])
                    k.op("dve", lambda: nc.vector.tensor_tensor(out=ogb[ri][:, h * 512:(h + 1) * 512], in0=og[gi][:],
                                                                in1=rsl[ri][:, h * 512:(h + 1) * 512], op=ALU.mult),
                         reads=[bog[gi], brsl[ri]], writes=[bogb[ri]])
                for g4 in range(4):
                    ti = st["t"] % 2
                    st["t"] += 1
                    for q in range(4):
                        dt_ = g4 * 4 + q
                        k.op("pe", lambda: nc.tensor.transpose(tb[ti][:, q * 64:(q + 1) * 64], ogb[ri][:, dt_ * 128:(dt_ + 1) * 128], identb[0:64, 0:64]),
                             reads=[bogb[ri], bgc], writes=[btb[ti]])
                    k.op("act", lambda: nc.scalar.copy(out=otst[oi][:, g4 * 4:(g4 + 1) * 4, csl],
                                                       in_=tb[ti][:, 0:256].rearrange("p (q c) -> p q c", q=4)),
                         reads=[btb[ti]], writes=[botst[oi]])
            k.dma("act", OT.rearrange("(dt p) t -> p dt t", p=128)[:, :, tsl], otst[oi][:], owner=botst[oi], reads=[botst[oi]],
                  writes=[k.buf("gOT", tt)])
        k.barrier()
        P.release(m)
        P.release(m0)
        run_outproj_from_dram(L, src, dst, OT, "gOT", cw["out"])

    def run_nsa(L, src, dst):
        jn = L // 3
        cw = conv[("nsa", L)]
        nin = nsa_in[jn]
        QT = P.dram_tmp("nsa_QT%d" % L, [2048, NT], BF16)
        KCT = P.dram_tmp("nsa_KCT%d" % L, [512, NT], BF16)
        KST = P.dram_tmp("nsa_KST%d" % L, [512, NT], BF16)
        KWT = P.dram_tmp("nsa_KWT%d" % L, [512, NT], BF16)
        VCT = P.dram_tmp("nsa_VCT%d" % L, [512, NT], BF16)
        VS = P.dram_tmp("nsa_VS%d" % L, [NT, 512], BF16)
        VW = P.dram_tmp("nsa_VW%d" % L, [NT, 512], BF16)
        GT = P.dram_tmp("nsa_GT%d" % L, [NT, 48], F32)
        OT = P.dram_tmp("nsa_OT%d" % L, [D, NT], BF16)
        SCALE = 128.0 ** -0.5
        m0 = P.mark()
        bnc = k.buf("nconst")
        bnc2 = k.buf("nconst2")
        invs = P.sb("n_invs", [128, 2], F32)
        gb = P.sb("n_gb", [128, 48], F32)
        validc = P.sb("n_validc", [128, T], BF16)
        cam = P.sb("n_cam", [128, 256], BF16)
        addc = P.sb("n_addc", [128, 16 * 32], F32)
        esel = P.sb("n_esel", [128, 16 * 128], BF16)
        identb = P.sb("n_idb", [128, 128], BF16)
        kcmp = P.sb("n_kcmp", [128, nseq * 4, 128], BF16)
        vaug = P.sb("n_vaug", [128, nseq * 4, 161], BF16)
        bkcmp, bvaug = k.buf("n_kcmp"), k.buf("n_vaug")
        k.dma("sp", invs[:], nsa_invs_in[:, :], owner=bnc, writes=[bnc])
        k.dma("sp", gb[:], nin["gate_b"][:, :], owner=bnc, writes=[bnc])
        k.dma("sp", validc[:], nsa_validc_in[:, :], owner=bnc, writes=[bnc])
        k.dma("sp", cam[:], nsa_cam_in[:, :], owner=bnc, writes=[bnc])
        k.dma("sp", addc[:], nsa_addc_in[:, :], owner=bnc, writes=[bnc])
        k.dma("sp", esel[:], nsa_esel_in[:, :], owner=bnc, writes=[bnc])
        for sg in range(nseq * 4):
            k.dma("sp", vaug[:, sg, 128:161], nsa_ovl_in[:, :], owner=bnc, writes=[bnc])
        k.op("act", lambda: nc.scalar.copy(out=identb[:], in_=ident[:]), reads=[b_const], writes=[bnc])

        class Rot:
            def __init__(self, items, bufs):
                self.items, self.bufs, self.i = items, bufs, 0

            def __call__(self):
                j = self.i % len(self.items)
                self.i += 1
                return self.items[j], self.bufs[j]

        wr, br_ = cw["rope"]
        wvc, bvc = cw["vc"]
        wtm, btm = cw["tm"]
        wgl, bgl = cw["gl"]
        blocks = []
        for tt in range(NTT):
            for i in range(28):
                blocks.append((wr[i], br_[i], 16, 256))
            blocks.append((wvc[0], bvc[0], 16, 512))
            blocks.append((wtm[0], btm[0], 16, 512))
            blocks.append((wtm[1], btm[1], 16, 512))
            blocks.append((wgl[0], bgl[0], 16, 48))
        P.ring_plan(blocks)
        m = P.mark()
        xT = P.sb("xT", [128, NDT, TT], BF16)
        bxT = k.buf("xT")
        posi = P.sb("posi", [128, TT], I32)
        ang = P.sb("ang", [128, TT], F32)
        kk_ = P.sb("kk_", [128, TT], F32)
        rr = P.sb("rr", [128, TT], F32)
        cos2 = P.sb("cos2", [128, TT], F32)
        sin2 = P.sb("sin2", [128, TT], F32)
        bposi, bang, bkk, brr, bcos, bsin = (k.buf(n) for n in ("posi", "ang", "kk_", "rr", "cos2", "sin2"))
        t1r = Rot([P.sb("t1_%d" % i, [128, TT], F32) for i in range(2)], [k.buf("t1", i) for i in range(2)])
        t2r = Rot([P.sb("t2_%d" % i, [128, TT], F32) for i in range(2)], [k.buf("t2", i) for i in range(2)])
        str_ = Rot([P.sb("rst_%d" % i, [128, TT], BF16) for i in range(4)], [k.buf("rst_", i) for i in range(4)])
        vst = Rot([P.sb("nvst%d" % i, [128, 4, 512], BF16) for i in range(2)], [k.buf("nvst", i) for i in range(2)])
        gst = P.sb("gst", [128, 4, 48], F32)
        gtmp = P.sb("gtmp", [128, 48], F32)
        bgst, bgtmp = k.buf("gst"), k.buf("gtmp")
        bank = Rot([P.ps("nb%d" % i, [128, 512]) for i in range(8)], [k.buf("nb", i) for i in range(8)])
        MAGIC = 12582912.0
        C1 = 6.28125
        C2 = 2.0 * math.pi - 6.28125
        XTbs = XTb[src].rearrange("(dt p) t -> p dt t", p=128)

        def sincos(dst_t, bdst, shift, signed):
            k.op("dve", lambda: nc.vector.tensor_scalar(out=rr[:], in0=ang[:], scalar1=shift, scalar2=None, op0=ALU.add),
                 reads=[bang], writes=[brr])
            k.op("dve", lambda: nc.vector.tensor_scalar(out=kk_[:], in0=rr[:], scalar1=1.0 / (2.0 * math.pi), scalar2=MAGIC,
                                                        op0=ALU.mult, op1=ALU.add), reads=[brr], writes=[bkk])
            k.op("dve", lambda: nc.vector.tensor_scalar(out=kk_[:], in0=kk_[:], scalar1=-MAGIC, scalar2=None, op0=ALU.add),
                 reads=[bkk], writes=[bkk])
            k.op("dve", lambda: nc.vector.scalar_tensor_tensor(out=rr[:], in0=kk_[:], scalar=-C1, in1=rr[:], op0=ALU.mult, op1=ALU.add),
                 reads=[bkk, brr], writes=[brr])
            k.op("dve", lambda: nc.vector.scalar_tensor_tensor(out=rr[:], in0=kk_[:], scalar=-C2, in1=rr[:], op0=ALU.mult, op1=ALU.add),
                 reads=[bkk, brr], writes=[brr])
            k.op("dve", lambda: nc.vector.tensor_scalar(out=rr[:], in0=rr[:], scalar1=3.1415925, scalar2=-3.1415925,
                                                        op0=ALU.min, op1=ALU.max), reads=[brr], writes=[brr])
            if signed:
                k.op("act", lambda: nc.scalar.activation(out=dst_t[:], in_=rr[:], func=AF.Sin, scale=invs[:, 1:2]),
                     reads=[brr, bnc], writes=[bdst])
            else:
                k.op("act", lambda: nc.scalar.activation(out=dst_t[:], in_=rr[:], func=AF.Sin), reads=[brr], writes=[bdst])

        for tt in range(NTT):
            tsl = slice(tt * TT, (tt + 1) * TT)
            k.dma("sp", xT[:], XTbs[:, :, tsl], owner=bxT, reads=[k.buf("XTb", src, tt)], writes=[bxT])
            k.dma("sp", posi[:], nsa_pos_in[:, tsl], owner=bposi, writes=[bposi])
            k.op("dve", lambda: nc.vector.tensor_copy(out=ang[:], in_=posi[:]), reads=[bposi], writes=[bang])
            k.op("dve", lambda: nc.vector.tensor_scalar(out=ang[:], in0=ang[:], scalar1=invs[:, 0:1], scalar2=None, op0=ALU.mult),
                 reads=[bang, bnc], writes=[bang])
            sincos(cos2, bcos, math.pi / 2.0, False)
            sincos(sin2, bsin, 0.0, False)
            for i in range(28):
                wbuf, wv = P.ring_next()
                sc = SCALE if i < 16 else 1.0
                p1, bp1 = bank()
                for kk in range(NDT):
                    k.op("pe", lambda: nc.tensor.matmul(p1[:], lhsT=wv[:, kk, 0:128], rhs=xT[:, kk, :], start=(kk == 0), stop=(kk == NDT - 1)),
                         reads=[wbuf, bxT], writes=[bp1])
                t1, bt1 = t1r()
                t2, bt2 = t2r()
                so, bso = str_()
                lo, hi = slice(0, 64), slice(64, 128)
                k.op("dve", lambda: nc.vector.scalar_tensor_tensor(out=t1[lo, :], in0=p1[lo, :], scalar=sc, in1=cos2[lo, :], op0=ALU.mult, op1=ALU.mult),
                     reads=[bp1, bcos], writes=[bt1])
                k.op("dve", lambda: nc.vector.scalar_tensor_tensor(out=t2[lo, :], in0=p1[hi, :], scalar=sc, in1=sin2[hi, :], op0=ALU.mult, op1=ALU.mult),
                     reads=[bp1, bsin], writes=[bt2])
                k.op("dve", lambda: nc.vector.tensor_tensor(out=so[lo, :], in0=t1[lo, :], in1=t2[lo, :], op=ALU.subtract), reads=[bt1, bt2], writes=[bso])
                k.op("dve", lambda: nc.vector.scalar_tensor_tensor(out=t1[hi, :], in0=p1[hi, :], scalar=sc, in1=cos2[hi, :], op0=ALU.mult, op1=ALU.mult),
                     reads=[bp1, bcos], writes=[bt1])
                k.op("dve", lambda: nc.vector.scalar_tensor_tensor(out=t2[hi, :], in0=p1[lo, :], scalar=sc, in1=sin2[lo, :], op0=ALU.mult, op1=ALU.mult),
                     reads=[bp1, bsin], writes=[bt2])
                k.op("dve", lambda: nc.vector.tensor_tensor(out=so[hi, :], in0=t1[hi, :], in1=t2[hi, :], op=ALU.add), reads=[bt1, bt2], writes=[bso])
                if i < 16:
                    dst_ap, dname = QT[i * 128:(i + 1) * 128, tsl], ("nQT", i, tt)
                elif i < 20:
                    dst_ap, dname = KCT[(i - 16) * 128:(i - 15) * 128, tsl], ("nKCT", i - 16, tt)
                elif i < 24:
                    dst_ap, dname = KST[(i - 20) * 128:(i - 19) * 128, tsl], ("nKST", i - 20, tt)
                else:
                    dst_ap, dname = KWT[(i - 24) * 128:(i - 23) * 128, tsl], ("nKWT", i - 24, tt)
                k.dma("act", dst_ap, so[:], owner=bso, reads=[bso], writes=[k.buf(*dname)])
            wbuf, wv = P.ring_next()
            for g in range(4):
                p1, bp1 = bank()
                for kk in range(NDT):
                    k.op("pe", lambda: nc.tensor.matmul(p1[:], lhsT=wv[:, kk, g * 128:(g + 1) * 128], rhs=xT[:, kk, :],
                                                        start=(kk == 0), stop=(kk == NDT - 1)), reads=[wbuf, bxT], writes=[bp1])
                so, bso = str_()
                k.op("act", lambda: nc.scalar.copy(out=so[:], in_=p1[:]), reads=[bp1], writes=[bso])
                k.dma("act", VCT[g * 128:(g + 1) * 128, tsl], so[:], owner=bso, reads=[bso], writes=[k.buf("nVCT", g, tt)])
            for which, DST, nm in ((0, VS, "nVS"), (1, VW, "nVW")):
                wbuf, wv = P.ring_next()
                vs_, bvs_ = vst()
                for sub in range(4):
                    p1, bp1 = bank()
                    for kk in range(NDT):
                        k.op("pe", lambda: nc.tensor.matmul(p1[:], lhsT=xT[:, kk, sub * 128:(sub + 1) * 128], rhs=wv[:, kk, :],
                                                            start=(kk == 0), stop=(kk == NDT - 1)), reads=[wbuf, bxT], writes=[bp1])
                    k.op("act", lambda: nc.scalar.copy(out=vs_[:, sub, :], in_=p1[:]), reads=[bp1], writes=[bvs_])
                k.dma("act", DST[tsl, :].rearrange("(s p) d -> p s d", p=128), vs_[:], owner=bvs_, reads=[bvs_], writes=[k.buf(nm, tt)])
            wbuf, wv = P.ring_next()
            for sub in range(4):
                p1, bp1 = bank()
                for kk in range(NDT):
                    k.op("pe", lambda: nc.tensor.matmul(p1[:, 0:48], lhsT=xT[:, kk, sub * 128:(sub + 1) * 128], rhs=wv[:, kk, 0:48],
                                                        start=(kk == 0), stop=(kk == NDT - 1)), reads=[wbuf, bxT], writes=[bp1])
                k.op("dve", lambda: nc.vector.tensor_tensor(out=gtmp[:], in0=p1[:, 0:48], in1=gb[:], op=ALU.add), reads=[bp1, bnc], writes=[bgtmp])
                k.op("act", lambda: nc.scalar.activation(out=gst[:, sub, :], in_=gtmp[:], func=AF.Sigmoid), reads=[bgtmp], writes=[bgst])
            k.dma("act", GT[tsl, :].rearrange("(s p) d -> p s d", p=128), gst[:], owner=bgst, reads=[bgst], writes=[k.buf("nGT", tt)])
        k.barrier()
        P.release(m)

        m = P.mark()
        w1 = P.sb("n_w1", [128, 2, 32, 128], BF16)
        w2 = P.sb("n_w2", [128, 2, 128], BF16)
        posT = P.sb("n_posT", [128, 64], BF16)
        sm_ = small[("nsa", jn)]
        for kv in range(2):
            k.dma("sp", w1[:, kv, :, :], sm_["w1"][kv].rearrange("(l p) j -> p l j", p=128), owner=bnc2, reads=[bsmall], writes=[bnc2])
            k.dma("sp", w2[:, kv, :], sm_["w2"][kv], owner=bnc2, reads=[bsmall], writes=[bnc2])
        k.dma("sp", posT[:], sm_["posT"][:, :], owner=bnc2, reads=[bsmall], writes=[bnc2])
        kct = P.sb("kct", [128, T], BF16)
        bkct = k.buf("kct")
        biasv = P.sb("biasv", [128, 2], F32)
        bbias = k.buf("biasv")
        xg = P.sb("xg", [128, 128], F32)
        x2 = P.sb("x2g", [128, 128], F32)
        gT = P.sb("gTg", [128, 128], BF16)
        bxg, bx2, bgT = k.buf("xg"), k.buf("x2g"), k.buf("gTg")
        bank = Rot([P.ps("cb%d" % i, [128, 512]) for i in range(4)], [k.buf("cb", i) for i in range(4)])
        for kv in range(2):
            pb, bpb = bank()
            for l in range(32):
                k.op("pe", lambda: nc.tensor.matmul(pb[:, 0:1], lhsT=w1[:, kv, l, :], rhs=posT[:, kv * 32 + l:kv * 32 + l + 1],
                                                    start=(l == 0), stop=(l == 31)), reads=[bnc2], writes=[bpb])
            k.op("dve", lambda: nc.vector.tensor_copy(out=biasv[:, kv:kv + 1], in_=pb[:, 0:1]), reads=[bpb], writes=[bbias])
        for sq_ in range(nseq):
            for g in range(4):
                sg = sq_ * 4 + g
                for kv in range(2):
                    SRC = KCT if kv == 0 else VCT
                    k.dma("sp", kct[:], SRC[g * 128:(g + 1) * 128, sq_ * T:(sq_ + 1) * T], owner=bkct,
                          reads=[k.buf("nKCT" if kv == 0 else "nVCT", g, tt_) for tt_ in range(sq_ * 4, sq_ * 4 + 4)], writes=[bkct])
                    pb, bpb = bank()
                    for l in range(32):
                        k.op("pe", lambda: nc.tensor.matmul(pb[:, 0:127], lhsT=w1[:, kv, l, :], rhs=kct[:, l:l + 16 * 126 + 1:16],
                                                            start=(l == 0), stop=(l == 31)), reads=[bnc2, bkct], writes=[bpb])
                    k.op("act", lambda: nc.scalar.activation(out=xg[:, 0:127], in_=pb[:, 0:127], func=AF.Identity, bias=biasv[:, kv:kv + 1]),
                         reads=[bpb, bbias], writes=[bxg])
                    k.op("dve", lambda: nc.vector.tensor_tensor(out=x2[:, 0:127], in0=xg[:, 0:127], in1=xg[:, 0:127], op=ALU.mult),
                         reads=[bxg], writes=[bx2])
                    k.op("dve", lambda: nc.vector.tensor_scalar(out=x2[:, 0:127], in0=x2[:, 0:127], scalar1=0.044715, scalar2=1.0,
                                                                op0=ALU.mult, op1=ALU.add), reads=[bx2], writes=[bx2])
                    k.op("dve", lambda: nc.vector.tensor_tensor(out=x2[:, 0:127], in0=x2[:, 0:127], in1=xg[:, 0:127], op=ALU.mult),
                         reads=[bx2, bxg], writes=[bx2])
                    k.op("act", lambda: nc.scalar.activation(out=x2[:, 0:127], in_=x2[:, 0:127], func=AF.Tanh, scale=math.sqrt(2.0 / math.pi)),
                         reads=[bx2], writes=[bx2])
                    k.op("dve", lambda: nc.vector.tensor_scalar(out=x2[:, 0:127], in0=x2[:, 0:127], scalar1=1.0, scalar2=0.5,
                                                                op0=ALU.add, op1=ALU.mult), reads=[bx2], writes=[bx2])
                    k.op("dve", lambda: nc.vector.tensor_tensor(out=gT[:, 0:127], in0=x2[:, 0:127], in1=xg[:, 0:127], op=ALU.mult),
                         reads=[bx2, bxg], writes=[bgT])
                    pc, bpc = bank()
                    if kv == 0:
                        k.op("pe", lambda: nc.tensor.matmul(pc[:, 0:127], lhsT=w2[:, 0, :], rhs=gT[:, 0:127], start=True, stop=True),
                             reads=[bnc2, bgT], writes=[bpc])
                        k.op("act", lambda: nc.scalar.copy(out=kcmp[:, sg, 0:127], in_=pc[:, 0:127]), reads=[bpc], writes=[bkcmp])
                    else:
                        k.op("pe", lambda: nc.tensor.matmul(pc[0:127, 0:128], lhsT=gT[:, 0:127], rhs=w2[:, 1, :], start=True, stop=True),
                             reads=[bnc2, bgT], writes=[bpc])
                        k.op("act", lambda: nc.scalar.copy(out=vaug[0:127, sg, 0:128], in_=pc[0:127, 0:128]), reads=[bpc, bnc], writes=[bvaug])
        k.barrier()
        P.release(m)

        m = P.mark()
        ksT = P.sb("ksT", [128, T], BF16)
        kwT = P.sb("kwT", [128, T], BF16)
        vsa = P.sb("vsa", [128, 16, 129], BF16)
        vwa = P.sb("vwa", [128, 16, 129], BF16)
        qT4 = P.sb("qT4", [128, 4, T], BF16)
        bksT, bkwT, bvsa, bvwa = k.buf("ksT"), k.buf("kwT"), k.buf("vsa"), k.buf("vwa")
        bqT4 = [k.buf("qT4", r) for r in range(4)]
        k.op("dve", lambda: nc.vector.memset(vsa[:, :, 128:129], 1.0), writes=[bvsa])
        k.op("dve", lambda: nc.vector.memset(vwa[:, :, 128:129], 1.0), writes=[bvwa])
        pTs = [P.sb("pTs%d" % i, [128, 16, TT], BF16) for i in range(2)]
        pTw = [P.sb("pTw%d" % i, [128, 8, TT], BF16) for i in range(2)]
        bpTs = [[k.buf("pTs", i, j) for j in range(16)] for i in range(2)]
        bpTw = [[k.buf("pTw", i, j) for j in range(8)] for i in range(2)]
        ec = Rot([P.sb("ec%d" % i, [128, TT], BF16) for i in range(2)], [k.buf("ec", i) for i in range(2)])
        ocmp2 = [P.sb("ocmp%d" % i, [128, 4, 4, 128], F32) for i in range(2)]
        bocmp2 = [[[k.buf("ocmp", i, r, q) for q in range(4)] for r in range(4)] for i in range(2)]
        imp2 = [P.sb("imp%d" % i, [128, 4, 32], F32) for i in range(2)]
        bimp2 = [[k.buf("imp", i, q) for q in range(4)] for i in range(2)]
        cnts = {"qt": 0, "h": 0}
        cmp3 = P.sb("cmp3", [128, 4, 32, 32], BF16)
        bcmp3 = k.buf("cmp3")
        rank = P.sb("rank", [128, 4, 32], F32)
        brank = k.buf("rank")
        selbT2 = [P.sb("selbT%d" % i, [128, TT], BF16) for i in range(2)]
        bselbT2 = [k.buf("selbT", i) for i in range(2)]
        for i in range(2):
            k.op("dve", lambda: nc.vector.memset(selbT2[i][:], 0.0), writes=[bselbT2[i]])
        impt = P.sb("impt", [128, 4, 32], F32)
        bimpt = k.buf("impt")
        gat2 = [P.sb("gat%d" % i, [128, 4, 48], F32) for i in range(2)]
        bgat2 = [k.buf("gat", i) for i in range(2)]
        sm = Rot([P.sb("sm%d" % i, [128, 16], F32) for i in range(4)], [k.buf("sm", i) for i in range(4)])
        acc = Rot([P.sb("acc%d" % i, [128, 4, 128], F32) for i in range(2)], [k.buf("acc", i) for i in range(2)])
        accb = Rot([P.sb("accb%d" % i, [128, 4, 128], BF16) for i in range(2)], [k.buf("accb", i) for i in range(2)])
        ots = Rot([P.sb("ots%d" % i, [128, TT], BF16) for i in range(2)], [k.buf("ots", i) for i in range(2)])
        bank = Rot([P.ps("ab%d" % i, [128, 512]) for i in range(3)], [k.buf("ab", i) for i in range(3)])
        pvbank = Rot([P.ps("ab%d" % i, [128, 512]) for i in range(3, 6)], [k.buf("ab", i) for i in range(3, 6)])
        tbank = Rot([P.ps("atb%d" % i, [128, 512], BF16) for i in range(2)], [k.buf("atb", i) for i in range(2)])

        for sq_ in range(nseq):
            s0 = sq_ * T
            for g in range(4):
                sg = sq_ * 4 + g
                tts = range(sq_ * 4, sq_ * 4 + 4)
                k.dma("sp", ksT[:], KST[g * 128:(g + 1) * 128, s0:s0 + T], owner=bksT, reads=[k.buf("nKST", g, t_) for t_ in tts], writes=[bksT])
                k.dma("sp", kwT[:], KWT[g * 128:(g + 1) * 128, s0:s0 + T], owner=bkwT, reads=[k.buf("nKWT", g, t_) for t_ in tts], writes=[bkwT])
                k.dma("sp", vsa[:, :, 0:128], VS[s0:s0 + T, g * 128:(g + 1) * 128].rearrange("(kt p) d -> p kt d", p=128), owner=bvsa,
                      reads=[k.buf("nVS", t_) for t_ in tts], writes=[bvsa])
                k.dma("sp", vwa[:, :, 0:128], VW[s0:s0 + T, g * 128:(g + 1) * 128].rearrange("(kt p) d -> p kt d", p=128), owner=bvwa,
                      reads=[k.buf("nVW", t_) for t_ in tts], writes=[bvwa])
                for r in range(4):
                    h = g * 4 + r
                    k.dma("sp", qT4[:, r, :], QT[h * 128:(h + 1) * 128, s0:s0 + T], owner=bqT4[r],
                          reads=[k.buf("nQT", h, t_) for t_ in tts], writes=[bqT4[r]])
                pvq = []
                for qt in range(4):
                    q0 = qt * TT
                    qsl = slice(q0, q0 + TT)
                    gat, bgat = gat2[cnts["qt"] % 2], bgat2[cnts["qt"] % 2]
                    k.dma("sp", gat[:], GT[s0 + q0:s0 + q0 + TT, :].rearrange("(s p) d -> p s d", p=128), owner=bgat,
                          reads=[k.buf("nGT", sq_ * 4 + qt)], writes=[bgat])
                    oi = cnts["qt"] % 2
                    cnts["qt"] += 1
                    ocmp, bocmp, imp, bimp = ocmp2[oi], bocmp2[oi], imp2[oi], bimp2[oi]
                    for r in range(4):
                        h = g * 4 + r
                        pb, bpb = bank()
                        k.op("pe", lambda: nc.tensor.matmul(pb[0:127, :], lhsT=kcmp[:, sg, 0:127], rhs=qT4[:, r, qsl], start=True, stop=True),
                             reads=[bkcmp, bqT4[r]], writes=[bpb])
                        e_, be_ = ec()
                        k.op("act", lambda: nc.scalar.activation(out=e_[0:127, :], in_=pb[0:127, :], func=AF.Exp), reads=[bpb], writes=[be_])
                        k.op("dve", lambda: nc.vector.tensor_tensor(out=e_[0:127, :], in0=e_[0:127, :], in1=validc[0:127, qsl], op=ALU.mult),
                             reads=[be_, bnc], writes=[be_])
                        po, bpo = bank()
                        pil, bpil = bank()
                        for qs in range(4):
                            k.op("pe", lambda: nc.tensor.matmul(po[:, qs * 128:(qs + 1) * 128], lhsT=e_[0:127, qs * 128:(qs + 1) * 128],
                                                                rhs=vaug[0:127, sg, 0:128], start=True, stop=True), reads=[be_, bvaug], writes=[bpo])
                            k.op("pe", lambda: nc.tensor.matmul(pil[:, qs * 33:(qs + 1) * 33], lhsT=e_[0:127, qs * 128:(qs + 1) * 128],
                                                                rhs=vaug[0:127, sg, 128:161], start=True, stop=True), reads=[be_, bvaug], writes=[bpil])
                        s_, bs_ = sm()
                        ilv = pil[:, 0:132].rearrange("p (q c) -> p q c", q=4)
                        k.op("dve", lambda: nc.vector.tensor_scalar(out=s_[:, 0:4], in0=ilv[:, :, 32], scalar1=1e-30, scalar2=None, op0=ALU.add),
                             reads=[bpil], writes=[bs_])
                        k.op("dve", lambda: nc.vector.reciprocal(out=s_[:, 4:8], in_=s_[:, 0:4]), reads=[bs_], writes=[bs_])
                        k.op("dve", lambda: nc.vector.tensor_tensor(out=s_[:, 8:12], in0=s_[:, 4:8], in1=gat[:, :, h], op=ALU.mult),
                             reads=[bs_, bgat], writes=[bs_])
                        k.op("dve", lambda: nc.vector.tensor_tensor(out=ocmp[:, r, :, :], in0=po[:, :].rearrange("p (q d) -> p q d", q=4),
                                                                    in1=s_[:, 8:12].unsqueeze(2).to_broadcast([128, 4, 128]), op=ALU.mult),
                             reads=[bpo, bs_], writes=[bocmp[r][0]])
                        if r == 0:
                            k.op("dve", lambda: nc.vector.tensor_tensor(out=imp[:, :, :], in0=ilv[:, :, 0:32],
                                                                        in1=s_[:, 4:8].unsqueeze(2).to_broadcast([128, 4, 32]), op=ALU.mult),
                                 reads=[bpil, bs_], writes=[bimp[0]])
                        else:
                            k.op("dve", lambda: nc.vector.tensor_tensor(out=impt[:, :, :], in0=ilv[:, :, 0:32],
                                                                        in1=s_[:, 4:8].unsqueeze(2).to_broadcast([128, 4, 32]), op=ALU.mult),
                                 reads=[bpil, bs_], writes=[bimpt])
                            k.op("dve", lambda: nc.vector.tensor_tensor(out=imp[:, :, :], in0=imp[:, :, :], in1=impt[:, :, :], op=ALU.add),
                                 reads=[bimp[0], bimpt], writes=[bimp[0]])
                    selbT, bselbT = selbT2[oi], bselbT2[oi]
                    need_sel = (qt * TT + TT - 1) // 64 + 1 > 16
                    if need_sel:
                        k.op("dve", lambda: nc.vector.tensor_tensor(out=imp[:, :, :], in0=imp[:, :, :],
                                                                    in1=addc[:, qt * 128:(qt + 1) * 128].rearrange("p (q m) -> p q m", q=4), op=ALU.add),
                             reads=[bimp[0], bnc], writes=[bimp[0]])
                        k.op("dve", lambda: nc.vector.tensor_tensor(out=cmp3[:], in0=imp[:, :, :].unsqueeze(2).to_broadcast([128, 4, 32, 32]),
                                                                    in1=imp[:, :, :].unsqueeze(3).to_broadcast([128, 4, 32, 32]), op=ALU.is_gt),
                             reads=[bimp[0]], writes=[bcmp3])
                        k.op("dve", lambda: nc.vector.reduce_sum(out=rank[:], in_=cmp3[:], axis=AX.X), reads=[bcmp3], writes=[brank])
                        k.op("dve", lambda: nc.vector.tensor_scalar(out=rank[:], in0=rank[:], scalar1=15.5, scalar2=-30000.0,
                                                                    op0=ALU.is_gt, op1=ALU.mult), reads=[brank], writes=[brank])
                        pt, bpt = bank()
                        for qs in range(4):
                            k.op("pe", lambda: nc.tensor.transpose(pt[0:32, qs * 128:(qs + 1) * 128], rank[:, qs, :], ident[:]), reads=[brank, b_const], writes=[bpt])
                        k.op("act", lambda: nc.scalar.copy(out=selbT[0:32, :], in_=pt[0:32, :]), reads=[bpt], writes=[bselbT])

                    def S_phase(r, pi):
                        pTs_, pTw_, bps_, bpw_ = pTs[pi], pTw[pi], bpTs[pi], bpTw[pi]
                        nks = qt * 4 + 4
                        for ki in range(nks):
                            pb, bpb = bank()
                            k.op("pe", lambda: nc.tensor.matmul(pb[:], lhsT=ksT[:, ki * 128:(ki + 1) * 128], rhs=qT4[:, r, qsl], start=True, stop=not need_sel),
                                 reads=[bksT, bqT4[r]], writes=[bpb])
                            if need_sel:
                                k.op("pe", lambda: nc.tensor.matmul(pb[:], lhsT=esel[:, ki * 128:(ki + 1) * 128], rhs=selbT[:, :], start=False, stop=True),
                                     reads=[bnc, bselbT], writes=[bpb])
                            k.op("act", lambda: nc.scalar.activation(out=pTs_[:, ki, :], in_=pb[:], func=AF.Exp), reads=[bpb], writes=[bps_[ki]])
                            if ki >= qt * 4:
                                qs = ki - qt * 4
                                k.op("dve", lambda: nc.vector.tensor_tensor(out=pTs_[:, ki, qs * 128:(qs + 1) * 128], in0=pTs_[:, ki, qs * 128:(qs + 1) * 128],
                                                                            in1=cam[:, 0:128], op=ALU.mult), reads=[bps_[ki], bnc], writes=[bps_[ki]])
                            if pvq:
                                pvq.pop(0)()
                        kw0 = max(0, qt * 4 - 4)
                        for ki in range(kw0, qt * 4 + 4):
                            wi = ki - kw0
                            if pvq:
                                pvq.pop(0)()
                            pb, bpb = bank()
                            k.op("pe", lambda: nc.tensor.matmul(pb[:], lhsT=kwT[:, ki * 128:(ki + 1) * 128], rhs=qT4[:, r, qsl], start=True, stop=True),
                                 reads=[bkwT, bqT4[r]], writes=[bpb])
                            k.op("act", lambda: nc.scalar.activation(out=pTw_[:, wi, :], in_=pb[:], func=AF.Exp), reads=[bpb], writes=[bpw_[wi]])
                            for qs in range(4):
                                qi = qt * 4 + qs
                                if ki == qi:
                                    mk = cam[:, 0:128]
                                elif ki == qi - 4:
                                    mk = cam[:, 128:256]
                                else:
                                    continue
                                k.op("dve", lambda: nc.vector.tensor_tensor(out=pTw_[:, wi, qs * 128:(qs + 1) * 128], in0=pTw_[:, wi, qs * 128:(qs + 1) * 128],
                                                                            in1=mk, op=ALU.mult), reads=[bpw_[wi], bnc], writes=[bpw_[wi]])

                    def PV_chunks(r, pi, qt=qt, g=g, ocmp=ocmp, bocmp=bocmp, gat=gat, bgat=bgat, q0=q0):
                        pTs_, pTw_, bps_, bpw_ = pTs[pi], pTw[pi], bpTs[pi], bpTw[pi]
                        h = g * 4 + r
                        kw0 = max(0, qt * 4 - 4)
                        st_ = {}
                        chunks = []

                        def c_alloc():
                            st_["po"], st_["bpo"] = pvbank()
                            st_["pw"], st_["bpw"] = pvbank()
                            st_["pl"], st_["bpl"] = pvbank()

                        def c_sel(qs):
                            if qs == 0:
                                c_alloc()
                            po, bpo, pl, bpl = st_["po"], st_["bpo"], st_["pl"], st_["bpl"]
                            qi = qt * 4 + qs
                            for ki in range(qi + 1):
                                k.op("pe", lambda: nc.tensor.matmul(po[:, qs * 128:(qs + 1) * 128], lhsT=pTs_[:, ki, qs * 128:(qs + 1) * 128], rhs=vsa[:, ki, 0:128],
                                                                    start=(ki == 0), stop=(ki == qi)), reads=[bps_[ki], bvsa], writes=[bpo])
                            for ki in range(qi + 1):
                                k.op("pe", lambda: nc.tensor.matmul(pl[:, 2 * qs:2 * qs + 1], lhsT=pTs_[:, ki, qs * 128:(qs + 1) * 128], rhs=vsa[:, ki, 128:129],
                                                                    start=(ki == 0), stop=(ki == qi)), reads=[bps_[ki], bvsa], writes=[bpl])

                        def c_win(qs):
                            pw, bpw, pl, bpl = st_["pw"], st_["bpw"], st_["pl"], st_["bpl"]
                            qi = qt * 4 + qs
                            kis = list(range(max(0, qi - 4), qi + 1))
                            for ki in kis:
                                k.op("pe", lambda: nc.tensor.matmul(pw[:, qs * 128:(qs + 1) * 128], lhsT=pTw_[:, ki - kw0, qs * 128:(qs + 1) * 128], rhs=vwa[:, ki, 0:128],
                                                                    start=(ki == kis[0]), stop=(ki == kis[-1])), reads=[bpw_[ki - kw0], bvwa], writes=[bpw])
                            for ki in kis:
                                k.op("pe", lambda: nc.tensor.matmul(pl[:, 2 * qs + 1:2 * qs + 2], lhsT=pTw_[:, ki - kw0, qs * 128:(qs + 1) * 128], rhs=vwa[:, ki, 128:129],
                                                                    start=(ki == kis[0]), stop=(ki == kis[-1])), reads=[bpw_[ki - kw0], bvwa], writes=[bpl])

                        def c_fin():
                            po, bpo, pw, bpw, pl, bpl = st_["po"], st_["bpo"], st_["pw"], st_["bpw"], st_["pl"], st_["bpl"]
                            s_, bs_ = sm()
                            a_, ba_ = acc()
                            t_, bt_ = acc()
                            ab_, bab_ = accb()
                            k.op("dve", lambda: nc.vector.reciprocal(out=s_[:, 0:8], in_=pl[:, 0:8]), reads=[bpl], writes=[bs_])
                            k.op("dve", lambda: nc.vector.tensor_tensor(out=s_[:, 8:16].rearrange("p (q b) -> p q b", q=4), in0=s_[:, 0:8].rearrange("p (q b) -> p q b", q=4),
                                                                        in1=gat[:, :, 16 + h:48:16], op=ALU.mult), reads=[bs_, bgat], writes=[bs_])
                            cg = s_[:, 8:16].rearrange("p (q b) -> p q b", q=4)
                            k.op("dve", lambda: nc.vector.tensor_tensor(out=a_[:], in0=po[:, :].rearrange("p (q d) -> p q d", q=4),
                                                                        in1=cg[:, :, 0].unsqueeze(2).to_broadcast([128, 4, 128]), op=ALU.mult),
                                 reads=[bpo, bs_], writes=[ba_])
                            k.op("dve", lambda: nc.vector.tensor_tensor(out=t_[:], in0=pw[:, :].rearrange("p (q d) -> p q d", q=4),
                                                                        in1=cg[:, :, 1].unsqueeze(2).to_broadcast([128, 4, 128]), op=ALU.mult),
                                 reads=[bpw, bs_], writes=[bt_])
                            k.op("dve", lambda: nc.vector.tensor_tensor(out=a_[:], in0=a_[:], in1=t_[:], op=ALU.add), reads=[ba_, bt_], writes=[ba_])
                            k.op("dve", lambda: nc.vector.tensor_tensor(out=ab_[:], in0=a_[:], in1=ocmp[:, r, :, :], op=ALU.add),
                                 reads=[ba_, bocmp[r][0]], writes=[bab_])
                            st_["ab"], st_["bab"] = ab_, bab_

                        def c_out():
                            ab_, bab_ = st_["ab"], st_["bab"]
                            tb_, btb_ = tbank()
                            for qs in range(4):
                                k.op("pe", lambda: nc.tensor.transpose(tb_[:, qs * 128:(qs + 1) * 128], ab_[:, qs, :], identb[:]), reads=[bab_, bnc], writes=[btb_])
                            o_, bo_ = ots()
                            k.op("act", lambda: nc.scalar.copy(out=o_[:], in_=tb_[:]), reads=[btb_], writes=[bo_])
                            k.dma("act", OT[h * 128:(h + 1) * 128, s0 + q0:s0 + q0 + TT], o_[:], owner=bo_, reads=[bo_], writes=[k.buf("nOT", h, sq_ * 4 + qt)])

                        for qs in range(4):
                            chunks.append(lambda qs=qs: c_sel(qs))
                            chunks.append(lambda qs=qs: c_win(qs))
                        chunks.append(c_fin)
                        chunks.append(c_out)
                        return chunks

                    for r in range(4):
                        pi = cnts["h"] % 2
                        cnts["h"] += 1
                        S_phase(r, pi)
                        while pvq:
                            pvq.pop(0)()
                        pvq.extend(PV_chunks(r, pi))
                    if qt == 3:
                        while pvq:
                            pvq.pop(0)()
        k.barrier()
        P.release(m)
        P.release(m0)
        run_outproj_from_dram(L, src, dst, OT, "nOTx", cw["out"])

    cur = 0
    import os
    if not os.environ.get("SKIP_IN"):
        phase_in_transpose(cur)
    for pi_, p_ in enumerate(plan):
        if pi_ + NAHEAD < len(plan):
            conv_phase(plan[pi_ + NAHEAD])
        if p_[0] == "ffn":
            run_ffn(p_[1], p_[2], cur, 1 - cur)
        elif p_[0] == "conv":
            run_conv(p_[1], cur, 1 - cur)
        elif p_[0] == "gla":
            run_gla(p_[1], cur, 1 - cur)
        elif p_[0] == "nsa":
            run_nsa(p_[1], cur, 1 - cur)
        cur = 1 - cur
    if not os.environ.get("SKIP_OUT"):
        phase_out_transpose(cur)
    k.barrier()
    return P


def make_inputs(P, core, nseq, x, ln_g, ln_b, ffn_w_in, ffn_w_out, conv_w_in=None, conv_w=None, conv_w_out=None,
                gla_w_in=None, gla_w_a2=None, gla_b_a=None, gla_norm_g=None, gla_w_out=None,
                positions=None, nsa_w_in=None, nsa_gate_b=None, nsa_cmp_pos=None, nsa_cmp_w1=None, nsa_cmp_w2=None, nsa_w_out=None, **rest):
    m = {}
    xs = np.ascontiguousarray(x[core * nseq:(core + 1) * nseq]).reshape(nseq * T, D)
    m["x"] = xs
    g = np.stack([ln_g, ln_b], axis=2)
    g = g.reshape(DEPTH, 3, 2, NDT, 128).transpose(4, 0, 1, 2, 3).reshape(128, -1)
    m["ln_gb"] = np.ascontiguousarray(g, dtype=np.float32)
    m["ident"] = np.eye(128, dtype=np.float32)
    for name in P.inputs:
        if name.startswith("nsa_"):
            m[name] = nsa_host_input(name, core, nseq, positions, nsa_w_in, nsa_gate_b, nsa_cmp_pos, nsa_cmp_w1, nsa_cmp_w2, nsa_w_out)
        elif name == "gla_w_in":
            m[name] = gla_w_in[0]
        elif name == "gla_w_out":
            m[name] = gla_w_out[0]
        elif name == "gla_const":
            s_ = np.arange(128)[:, None]
            t_ = np.arange(128)[None, :]
            same = (s_ // 64) == (t_ // 64)
            tri = ((s_ <= t_) & same).astype(np.float32) / 16.0
            mm = ((s_ > t_) & same).astype(np.float32) / 16.0
            m[name] = np.ascontiguousarray(np.concatenate([tri, mm], axis=1))
        elif name == "gla_wa2":
            m[name] = np.ascontiguousarray(np.concatenate([gla_w_a2[0], gla_b_a[0][None, :]], axis=0))
        elif name == "gla_ng":
            m[name] = np.ascontiguousarray(np.broadcast_to(gla_norm_g[0][None, :], (64, 512)))
        elif name == "gla_cm":
            m[name] = (np.arange(64)[:, None] <= np.arange(64)[None, :]).astype(np.float32)
        elif name == "conv_w_in":
            m[name] = conv_w_in[0]
        elif name == "conv_w_out":
            m[name] = conv_w_out[0]
        elif name == "conv_w":
            m[name] = np.ascontiguousarray(conv_w[0].reshape(3, NDT, 128).transpose(2, 0, 1).reshape(128, 3 * NDT))
        elif name.startswith("ffn_w_in_"):
            L, j = map(int, name.split("_")[-2:])
            m[name] = ffn_w_in[L, j]
        elif name.startswith("ffn_w_out_"):
            L, j = map(int, name.split("_")[-2:])
            m[name] = ffn_w_out[L, j]
    return m


def nsa_host_input(name, core, nseq, positions, nsa_w_in, nsa_gate_b, nsa_cmp_pos, nsa_cmp_w1, nsa_cmp_w2, nsa_w_out):
    bf = ml_dtypes.bfloat16
    if name[-2] == "_" and name[-1].isdigit():
        jn = int(name[-1])
        base = name[:-2]
        if base == "nsa_w_in":
            return nsa_w_in[jn]
        if base == "nsa_w_out":
            return nsa_w_out[jn]
        if base == "nsa_w1":
            return nsa_cmp_w1[jn]
        if base == "nsa_w2":
            return nsa_cmp_w2[jn]
        if base == "nsa_posT":
            return np.ascontiguousarray(nsa_cmp_pos[jn].transpose(2, 0, 1).reshape(128, 64))
        if base == "nsa_gateb":
            return np.ascontiguousarray(np.broadcast_to(nsa_gate_b[jn][None, :], (128, 48)))
    if name == "nsa_pos":
        p = np.ascontiguousarray(positions[core * nseq:(core + 1) * nseq]).reshape(1, nseq * T)
        return np.ascontiguousarray(np.broadcast_to(p, (128, nseq * T))).astype(np.int32)
    if name == "nsa_invs":
        inv = (np.float32(10000.0) ** (-np.arange(0, 128, 2, dtype=np.float32) / np.float32(128))).astype(np.float32)
        o = np.zeros((128, 2), np.float32)
        o[:64, 0] = inv
        o[64:, 0] = inv
        o[:64, 1] = -1.0
        o[64:, 1] = 1.0
        return o
    if name == "nsa_validc":
        n = np.arange(128)[:, None]
        t = np.arange(T)[None, :]
        v = ((16 * n + 31 <= t) & (n < 127)).astype(np.float32)
        return v.astype(bf)
    if name == "nsa_cam":
        kk = np.arange(128)[:, None]
        tt = np.arange(128)[None, :]
        return np.concatenate([(kk <= tt), (kk > tt)], axis=1).astype(np.float32).astype(bf)
    if name == "nsa_addc":
        t = np.arange(T)
        cur = t // 64
        blk = np.arange(32)[None, :]
        valid = blk <= cur[:, None]
        forced = (blk == 0) | (blk == cur[:, None]) | (blk == cur[:, None] - 1)
        a = np.where(valid, np.where(forced, 1000.0, 0.0), -1e30).astype(np.float32)
        return np.ascontiguousarray(a.reshape(16, 128, 32).transpose(1, 0, 2).reshape(128, 512))
    if name == "nsa_esel":
        mm = np.arange(128)[:, None]
        kk = np.arange(T)[None, :]
        return ((mm == kk // 64) & (mm < 32)).astype(np.float32).astype(bf)
    if name == "nsa_ovl":
        n = np.arange(128)[:, None]
        mblk = np.arange(32)[None, :]
        cs = n * 16
        ss = mblk * 64
        ov = ((cs < ss + 64) & (cs + 32 > ss) & (n < 127)).astype(np.float32)
        ones = (n < 127).astype(np.float32)
        return np.concatenate([ov, ones], axis=1).astype(bf)
    raise KeyError(name)


_CACHE = {}


def kernel(**inputs):
    inputs = {k_: np.asarray(v) for k_, v in inputs.items()}
    if "prog" not in _CACHE:
        _CACHE["prog"] = build_program(nseq=2)
    P = _CACHE["prog"]
    in_maps = [make_inputs(P, c, 2, **inputs) for c in range(NCORES)]
    res = run_bass_kernel_spmd(P.nc, in_maps, core_ids=list(range(NCORES)))
    outs = [np.asarray(r["out"]).reshape(2, T, D) for r in res.results]
    return np.concatenate(outs, axis=0).astype(np.float32)
```

```python
import math
import os
import numpy as np
import ml_dtypes
import concourse.bass as bass
import concourse.mybir as mybir
from concourse.bass_utils import run_bass_kernel_spmd

F32 = mybir.dt.float32
BF16 = mybir.dt.bfloat16
I32 = mybir.dt.int32
ALU = mybir.AluOpType
AF = mybir.ActivationFunctionType
AX = mybir.AxisListType

D = 2048
T = 2048
DEPTH = 4
DFF = 5632
NCORES = 8
ALPHA = (2.0 * DEPTH) ** 0.25
LN_EPS = 1e-5
TT = 512
NDT = D // 128
NHT = DFF // 128
RING_SLOT = 44 * 256
NRING = 3


PSUM_NAMES = {"pst", "opst", "pin", "py", "ps1", "ps2", "gb", "hb", "htb", "nb", "cb", "ab", "atb"}


class Buf:
    __slots__ = ("name", "writer", "readers", "dsem", "dcnt", "excl")

    def __init__(self, name):
        self.name = name
        self.writer = None
        self.readers = []
        self.dsem = None
        self.dcnt = 0
        self.excl = bool(name) and name[0] in PSUM_NAMES


class KB:
    def __init__(self, nc):
        self.nc = nc
        self.eng = {"pe": nc.tensor, "act": nc.scalar, "dve": nc.vector, "pool": nc.gpsimd, "sp": nc.sync}
        self.sem = {}
        self.cnt = {}
        self.waited = {e: {} for e in self.eng}
        self._stack = []
        for e in ("pe", "act", "dve", "pool"):
            self.sem[e] = self._enter(nc.semaphore("sem_" + e))
            self.cnt[e] = 0
        self.semid = {}
        self.bufs = {}
        self.ndma = 0

    def _enter(self, cm):
        v = cm.__enter__()
        self._stack.append(cm)
        return v

    def close(self):
        while self._stack:
            self._stack.pop().__exit__(None, None, None)

    def buf(self, *key):
        b = self.bufs.get(key)
        if b is None:
            b = Buf(key)
            self.bufs[key] = b
        return b

    def _wait(self, eng, tick):
        if tick is None:
            return
        if tick[0] == "e":
            pe, c = tick[1], tick[2]
            if pe == eng and eng == "pe":
                return
            key = pe
            sem, val = self.sem[pe], c
        else:
            b = tick[1]
            key = id(b)
            sem, val = b.dsem, 16 * b.dcnt
        w = self.waited[eng]
        if w.get(key, 0) >= val:
            return
        w[key] = val
        self.eng[eng].wait_ge(sem, val)

    def _sync(self, eng, reads, writes, same_war=False):
        for r in reads:
            self._wait(eng, r.writer)
            if r.excl:
                for t in r.readers:
                    if not (t[0] == "e" and t[1] == eng):
                        self._wait(eng, t)
        for wb in writes:
            self._wait(eng, wb.writer)
            for t in wb.readers:
                if t[0] == "e" and t[1] == eng and not same_war:
                    continue
                self._wait(eng, t)

    def op(self, eng, fn, reads=(), writes=()):
        self._sync(eng, reads, writes)
        ins = fn()
        self.cnt[eng] += 1
        ins.then_inc(self.sem[eng], 1)
        tick = ("e", eng, self.cnt[eng])
        for r in reads:
            r.readers.append(tick)
            if len(r.readers) > 24:
                r.readers = self._prune(r.readers)
        for wb in writes:
            wb.writer = tick
            wb.readers = []
        return ins

    def _prune(self, readers):
        best = {}
        out = []
        for t in readers:
            if t[0] == "e":
                if t[1] not in best or best[t[1]][2] < t[2]:
                    best[t[1]] = t
            else:
                if all(o[1] is not t[1] for o in out):
                    out.append(t)
        return out + list(best.values())

    def dma(self, q, out, in_, owner, reads=(), writes=()):
        if owner.dsem is None:
            owner.dsem = self._enter(self.nc.semaphore("dsem%d" % len(self.semid)))
            self.semid[id(owner)] = owner
        self._sync(q, reads, writes, same_war=True)
        ins = self.eng[q].dma_start(out=out, in_=in_)
        owner.dcnt += 1
        ins.then_inc(owner.dsem, 16)
        if q == "pool":
            pass
        tick = ("d", owner)
        for r in reads:
            r.readers.append(tick)
            if len(r.readers) > 24:
                r.readers = self._prune(r.readers)
        for wb in writes:
            wb.writer = tick
            wb.readers = []
        self.ndma += 1
        return ins

    def barrier(self, bufs=()):
        for e in ("pe", "act", "dve", "pool", "sp"):
            for pe in ("pe", "act", "dve", "pool"):
                if self.cnt[pe] > 0 and not (pe == e):
                    self._wait(e, ("e", pe, self.cnt[pe]))
            for b in self.semid.values():
                if b.dcnt > 0 and b.name[0] != "wbf":
                    self._wait(e, ("d", b))


class Prog:
    def __init__(self, nseq=2, layers=(0, 1, 2, 3), stop_after=None):
        self.nseq = nseq
        self.NT = nseq * T
        self.layers = layers
        self.stop_after = stop_after
        self.nc = bass.Bass("TRN2", target_bir_lowering=False)
        self.k = KB(self.nc)
        self.inputs = {}
        self._cms = []

    def dram_in(self, name, shape, dt=F32):
        t = self.nc.dram_tensor(name, list(shape), dt, kind="ExternalInput").ap()
        self.inputs[name] = (tuple(shape), dt)
        return t

    def dram_tmp(self, name, shape, dt):
        return self.nc.dram_tensor(name, list(shape), dt, kind="Internal").ap()

    def enter(self, cm):
        v = cm.__enter__()
        self._cms.append(cm)
        return v

    def sb(self, name, shape, dt):
        self._uid = getattr(self, "_uid", 0) + 1
        return self.enter(self.nc.sbuf_tensor("%s_s%d" % (name, self._uid), list(shape), dt))

    def ps(self, name, shape, dt=F32):
        self._uid = getattr(self, "_uid", 0) + 1
        return self.enter(self.nc.psum_tensor("%s_p%d" % (name, self._uid), list(shape), dt))

    def mark(self):
        return len(self._cms)

    def release(self, mark):
        while len(self._cms) > mark:
            self._cms.pop().__exit__(None, None, None)

    def convert_weight(self, name, w_ap, K, col_blocks, bw, group=None):
        k = self.k
        KT = K // 128
        nblk = len(col_blocks)
        wb = self.dram_tmp(name + "_bf", [nblk, 128, KT, bw], BF16)
        bufs = []
        wv = w_ap.rearrange("(kt p) n -> p kt n", p=128)
        b = k.buf("wbf", group if group is not None else name)
        for bi, ranges in enumerate(col_blocks):
            off = 0
            for (c0, ncol) in ranges:
                k.dma("pool", wb[bi, :, :, off:off + ncol], wv[:, :, c0:c0 + ncol], owner=b)
                off += ncol
            bufs.append(b)
        b.writer = ("d", b)
        return wb, bufs

    def ring_init(self):
        self.ring = self.sb("wring", [128, NRING, RING_SLOT], BF16)
        self.ring_bufs = [self.k.buf("ring", i) for i in range(NRING)]
        self.ring_pos = 0
        self.ring_queue = []
        self.ring_loaded = []

    def ring_plan(self, blocks):
        self.ring_queue.extend(blocks)

    def _ring_issue_one(self):
        if not self.ring_queue:
            return False
        ap, srcbuf, a, b = self.ring_queue.pop(0)
        slot = self.ring_pos % NRING
        self.ring_pos += 1
        rb = self.ring_bufs[slot]
        view = self.ring[:, slot, 0:a * b].rearrange("p (a b) -> p a b", a=a)
        self.k.dma("sp", view, ap, owner=rb, reads=[srcbuf], writes=[rb])
        self.ring_loaded.append((slot, view))
        return True

    def ring_next(self):
        while len(self.ring_loaded) < NRING - 1 and self._ring_issue_one():
            pass
        if not self.ring_loaded:
            self._ring_issue_one()
        slot, view = self.ring_loaded.pop(0)
        return self.ring_bufs[slot], view

    def ring_prefetch(self):
        while len(self.ring_loaded) < NRING - 1 and self._ring_issue_one():
            pass


def default_plan(layers=(0, 1, 2, 3)):
    plan = []
    for L in layers:
        plan.append(("ffn", L, 0))
        plan.append((("nsa", "conv", "gla")[L % 3], L))
        plan.append(("ffn", L, 1))
    return plan


def build_program(nseq=2, layers=(0, 1, 2, 3), plan=None):
    if plan is None:
        plan = default_plan(layers)
    P = Prog(nseq=nseq, layers=layers)
    nc, k = P.nc, P.k
    NT = P.NT
    NTT = NT // TT

    x_in = P.dram_in("x", [NT, D])
    out_ap = nc.dram_tensor("out", [NT, D], F32, kind="ExternalOutput").ap()
    ln_gb = P.dram_in("ln_gb", [128, DEPTH * 3 * 2 * NDT])
    ident_in = P.dram_in("ident", [128, 128])
    ffn_win = {}
    ffn_wout = {}
    for p_ in (plan or []):
        if p_[0] == "ffn":
            L, j = p_[1], p_[2]
            ffn_win[(L, j)] = P.dram_in("ffn_w_in_%d_%d" % (L, j), [D, 2 * DFF])
            ffn_wout[(L, j)] = P.dram_in("ffn_w_out_%d_%d" % (L, j), [DFF, D])
    if any(p_[0] == "conv" for p_ in plan):
        conv_win = P.dram_in("conv_w_in", [D, 3 * D])
        conv_wout = P.dram_in("conv_w_out", [D, D])
        convw_in = P.dram_in("conv_w", [128, 3 * NDT])

    if any(p_[0] == "gla" for p_ in plan):
        gla_win = P.dram_in("gla_w_in", [D, 6160])
        gla_wout = P.dram_in("gla_w_out", [D, D])
        gla_const_in = P.dram_in("gla_const", [128, 256])
        gla_wa2_in = P.dram_in("gla_wa2", [17, 1024])
        gla_ng_in = P.dram_in("gla_ng", [64, 512])
        gla_cm_in = P.dram_in("gla_cm", [64, 64])
    nsa_in = {}
    if any(p_[0] == "nsa" for p_ in plan):
        for p_ in plan:
            if p_[0] == "nsa":
                jn = p_[1] // 3
                nsa_in[jn] = dict(
                    w_in=P.dram_in("nsa_w_in_%d" % jn, [D, 5168]),
                    w_out=P.dram_in("nsa_w_out_%d" % jn, [D, D]),
                    w1=P.dram_in("nsa_w1_%d" % jn, [2, 4096, 128]),
                    w2=P.dram_in("nsa_w2_%d" % jn, [2, 128, 128]),
                    posT=P.dram_in("nsa_posT_%d" % jn, [128, 64]),
                    gate_b=P.dram_in("nsa_gateb_%d" % jn, [128, 48]),
                )
        nsa_pos_in = P.dram_in("nsa_pos", [128, NT], I32)
        nsa_invs_in = P.dram_in("nsa_invs", [128, 2])
        nsa_validc_in = P.dram_in("nsa_validc", [128, T], BF16)
        nsa_cam_in = P.dram_in("nsa_cam", [128, 256], BF16)
        nsa_addc_in = P.dram_in("nsa_addc", [128, 512])
        nsa_esel_in = P.dram_in("nsa_esel", [128, 2048], BF16)
        nsa_ovl_in = P.dram_in("nsa_ovl", [128, 33], BF16)
    XT = [P.dram_tmp("XT%d" % i, [D, NT], F32) for i in range(2)]
    XTb = [P.dram_tmp("XTb%d" % i, [D, NT], BF16) for i in range(2)]

    ident = P.sb("ident", [128, 128], F32)
    ones = P.sb("ones", [128, 128], F32)
    lngb = P.sb("lngb", [128, DEPTH * 3 * 2 * NDT], F32)
    b_const = k.buf("const")
    k.dma("sp", ident[:], ident_in[:, :], owner=b_const, writes=[b_const])
    k.dma("sp", lngb[:], ln_gb[:, :], owner=b_const, writes=[b_const])
    if any(p_[0] == "conv" for p_ in plan):
        convw = P.sb("convw", [128, 3 * NDT], F32)
        k.dma("sp", convw[:], convw_in[:, :], owner=b_const, writes=[b_const])
    b_ones = k.buf("ones")
    k.op("dve", lambda: nc.vector.memset(ones[:], 1.0), writes=[b_ones])
    onesb = P.sb("onesb", [128, 128], BF16)
    k.op("dve", lambda: nc.vector.memset(onesb[:], 1.0), writes=[b_ones])
    P.ring_init()

    conv = {}

    def conv_ffn(L, j):
        blocks_in = [[(nb * 256, 256), (DFF + nb * 256, 256)] for nb in range(DFF // 256)]
        if (L, j) == plan_first_ffn:
            wb_a, bufs_a = P.convert_weight("fin_%d_%d_a" % (L, j), ffn_win[(L, j)], D, blocks_in[:11], 512, group="ffn%d%d_ia" % (L, j))
            wb_b, bufs_b = P.convert_weight("fin_%d_%d_b" % (L, j), ffn_win[(L, j)], D, blocks_in[11:], 512, group="ffn%d%d_ib" % (L, j))
            wb_in = [wb_a[i] for i in range(11)] + [wb_b[i] for i in range(11)]
            bufs_in = bufs_a + bufs_b
        else:
            wb_in, bufs_in = P.convert_weight("fin_%d_%d" % (L, j), ffn_win[(L, j)], D, blocks_in, 512, group="ffn%d%d_i" % (L, j))
        blocks_out = [[(db * 256, 256)] for db in range(D // 256)]
        wb_out, bufs_out = P.convert_weight("fout_%d_%d" % (L, j), ffn_wout[(L, j)], DFF, blocks_out, 256, group="ffn%d%d_o" % (L, j))
        conv[("ffn", L, j)] = (wb_in, bufs_in, wb_out, bufs_out)

    def conv_conv(L):
        blocks_in = [[(d * 128, 128), (D + d * 128, 128), (2 * D + d * 128, 128)] for d in range(NDT)]
        wb_in, bufs_in = P.convert_weight("cin_%d" % L, conv_win, D, blocks_in, 384, group="conv")
        blocks_out = [[(db * 256, 256)] for db in range(D // 256)]
        wb_out, bufs_out = P.convert_weight("cout_%d" % L, conv_wout, D, blocks_out, 256, group="conv")
        conv[("conv", L)] = (wb_in, bufs_in, wb_out, bufs_out)

    def conv_gla(L):
        cwd = {}
        cwd["a"] = P.convert_weight("ga_%d" % L, gla_win, D, [[(6144, 16)]], 16, group="gla")
        cwd["fm"] = P.convert_weight("gfm_%d" % L, gla_win, D, [[(i * 256, 256), (1024 + i * 256, 256)] for i in range(4)], 512, group="gla")
        tmb = [[(1024 + i * 512, 512)] for i in range(2)] + [[(2048 + i * 512, 512)] for i in range(4)] + [[(4096 + i * 512, 512)] for i in range(4)]
        cwd["tm"] = P.convert_weight("gtm_%d" % L, gla_win, D, tmb, 512, group="gla")
        cwd["out"] = P.convert_weight("gout_%d" % L, gla_wout, D, [[(db * 256, 256)] for db in range(D // 256)], 256, group="gla")
        conv[("gla", L)] = cwd

    def conv_nsa(L):
        jn = L // 3
        w = nsa_in[jn]["w_in"]
        cwd = {}
        tiles = [h * 128 for h in range(16)] + [2048 + g * 128 for g in range(4)] + [3072 + g * 128 for g in range(4)] + [4096 + g * 128 for g in range(4)]
        cwd["rope"] = P.convert_weight("nrope_%d" % L, w, D, [[(c0, 128), (c0 + 64, 64), (c0, 64)] for c0 in tiles], 256, group="nsa%d" % L)
        cwd["vc"] = P.convert_weight("nvc_%d" % L, w, D, [[(2560, 512)]], 512, group="nsa%d" % L)
        cwd["tm"] = P.convert_weight("ntm_%d" % L, w, D, [[(3584, 512)], [(4608, 512)]], 512, group="nsa%d" % L)
        cwd["gl"] = P.convert_weight("ngl_%d" % L, w, D, [[(5120, 48)]], 48, group="nsa%d" % L)
        cwd["out"] = P.convert_weight("nout_%d" % L, nsa_in[jn]["w_out"], D, [[(db * 256, 256)] for db in range(D // 256)], 256, group="nsa%d" % L)
        conv[("nsa", L)] = cwd

    plan_first_ffn = next(((p_[1], p_[2]) for p_ in plan if p_[0] == "ffn"), None)

    def conv_phase(p_):
        if p_[0] == "nsa":
            conv_nsa(p_[1])
        elif p_[0] == "ffn":
            conv_ffn(p_[1], p_[2])
        elif p_[0] == "conv":
            conv_conv(p_[1])
        elif p_[0] == "gla":
            conv_gla(p_[1])

    bsmall = k.buf("wbf", "small")
    small = {}
    for p_ in plan:
        if p_[0] == "nsa":
            jn = p_[1] // 3
            d_ = {}
            d_["w1"] = P.dram_tmp("nsa_w1b_%d" % jn, [2, 4096, 128], BF16)
            d_["w2"] = P.dram_tmp("nsa_w2b_%d" % jn, [2, 128, 128], BF16)
            d_["posT"] = P.dram_tmp("nsa_posTb_%d" % jn, [128, 64], BF16)
            for kv in range(2):
                k.dma("pool", d_["w1"][kv].rearrange("(a b) j -> a b j", a=128), nsa_in[jn]["w1"][kv].rearrange("(a b) j -> a b j", a=128), owner=bsmall)
                k.dma("pool", d_["w2"][kv], nsa_in[jn]["w2"][kv], owner=bsmall)
            k.dma("pool", d_["posT"][:, :], nsa_in[jn]["posT"][:, :], owner=bsmall)
            small[("nsa", jn)] = d_
        elif p_[0] == "gla":
            d_ = {"wa2": P.dram_tmp("gla_wa2b", [17, 1024], BF16)}
            k.dma("pool", d_["wa2"][:, :], gla_wa2_in[:, :], owner=bsmall)
            small["gla"] = d_
    bsmall.writer = ("d", bsmall)
    NAHEAD = 3
    for p_ in plan[:NAHEAD]:
        conv_phase(p_)

    def phase_in_transpose(dst):
        m = P.mark()
        xin = [P.sb("xin%d" % i, [128, D], F32) for i in range(2)]
        stg = [P.sb("stg%d" % i, [128, NDT, TT], F32) for i in range(2)]
        stgb = [P.sb("stgb%d" % i, [128, NDT, TT], BF16) for i in range(2)]
        pst = [P.ps("pst%d" % i, [128, 512], F32) for i in range(4)]
        bx = [k.buf("xin", i) for i in range(2)]
        bs = [k.buf("stg", i) for i in range(2)]
        bsb = [k.buf("stgb", i) for i in range(2)]
        bp = [k.buf("pst", i) for i in range(4)]
        cnt = 0
        pcnt = 0
        for tt in range(NTT):
            s = tt % 2
            for sub in range(TT // 128):
                t0 = tt * TT + sub * 128
                xi = cnt % 2
                cnt += 1
                k.dma("sp", xin[xi][:], x_in[t0:t0 + 128, :], owner=bx[xi], writes=[bx[xi]])
                for g in range(NDT // 4):
                    pb = pcnt % 4
                    pcnt += 1
                    for q in range(4):
                        dt_ = g * 4 + q
                        k.op("pe", lambda: nc.tensor.transpose(pst[pb][:, q * 128:(q + 1) * 128],
                                                               xin[xi][:, dt_ * 128:(dt_ + 1) * 128], ident[:]),
                             reads=[bx[xi], b_const], writes=[bp[pb]])
                    dst_v = stg[s][:, g * 4:(g + 1) * 4, sub * 128:(sub + 1) * 128]
                    dst_b = stgb[s][:, g * 4:(g + 1) * 4, sub * 128:(sub + 1) * 128]
                    src_v = pst[pb][:, :].rearrange("p (q t) -> p q t", q=4)
                    k.op("dve", lambda: nc.vector.tensor_copy(out=dst_v, in_=src_v), reads=[bp[pb]], writes=[bs[s]])
                    if os.environ.get("NO_BF16") != "1":
                        for q in range(4):
                            k.op("act", lambda: nc.scalar.copy(out=stgb[s][:, g * 4 + q, sub * 128:(sub + 1) * 128],
                                                               in_=pst[pb][:, q * 128:(q + 1) * 128]), reads=[bp[pb]], writes=[bsb[s]])
            k.dma("sp", XT[dst].rearrange("(dt p) t -> p dt t", p=128)[:, :, tt * TT:(tt + 1) * TT], stg[s][:],
                  owner=bs[s], reads=[bs[s]], writes=[k.buf("XT", dst, tt)])
            if not os.environ.get("NO_BF16"):
                k.dma("sp", XTb[dst].rearrange("(dt p) t -> p dt t", p=128)[:, :, tt * TT:(tt + 1) * TT], stgb[s][:],
                      owner=bsb[s], reads=[bsb[s]], writes=[k.buf("XTb", dst, tt)])
        k.barrier()
        P.release(m)

    def phase_out_transpose(src):
        m = P.mark()
        xin = [P.sb("oin%d" % i, [128, NDT, TT], F32) for i in range(2)]
        stg = [P.sb("ostg%d" % i, [128, D], F32) for i in range(2)]
        pst = [P.ps("opst%d" % i, [128, 512], F32) for i in range(4)]
        bx = [k.buf("oin", i) for i in range(2)]
        bs = [k.buf("ostg", i) for i in range(2)]
        bp = [k.buf("opst", i) for i in range(4)]
        cnt = 0
        pcnt = 0
        for tt in range(NTT):
            s = tt % 2
            k.dma("sp", xin[s][:], XT[src].rearrange("(dt p) t -> p dt t", p=128)[:, :, tt * TT:(tt + 1) * TT],
                  owner=bx[s], reads=[k.buf("XT", src, tt)], writes=[bx[s]])
            for sub in range(TT // 128):
                t0 = tt * TT + sub * 128
                si = cnt % 2
                cnt += 1
                for g in range(NDT // 4):
                    pb = pcnt % 4
                    pcnt += 1
                    for q in range(4):
                        dt_ = g * 4 + q
                        k.op("pe", lambda: nc.tensor.transpose(pst[pb][:, q * 128:(q + 1) * 128],
                                                               xin[s][:, dt_, sub * 128:(sub + 1) * 128], ident[:]),
                             reads=[bx[s], b_const], writes=[bp[pb]])
                    dst_v = stg[si][:, g * 512:(g + 1) * 512]
                    if g % 2 == 0:
                        k.op("dve", lambda: nc.vector.tensor_copy(out=dst_v, in_=pst[pb][:, :]), reads=[bp[pb]], writes=[bs[si]])
                    else:
                        k.op("act", lambda: nc.scalar.copy(out=dst_v, in_=pst[pb][:, :]), reads=[bp[pb]], writes=[bs[si]])
                k.dma("sp", out_ap[t0:t0 + 128, :], stg[si][:], owner=bs[si], reads=[bs[si]], writes=[k.buf("out", t0)])
        k.barrier()
        P.release(m)

    def ln_cols(L, j):
        base = ((L * 3 + j) * 2) * NDT
        return base, base + NDT

    def phase_rowlocal(L, lnj, src, dst, KT, c_res, wout, stage_blocks, stage_alloc, stage_run, need_xT=True):
        wb_out, bufs_out = wout
        NBO = D // 256
        blocks = []
        for tt in range(NTT):
            blocks.extend(stage_blocks(tt))
            for db in range(NBO):
                blocks.append((wb_out[db], bufs_out[db], KT, 256))
        P.ring_plan(blocks)
        m = P.mark()
        C = {}
        nz = 1 if need_xT else 2
        zs = [P.sb("z%d" % i, [128, NDT, TT], F32) for i in range(nz)]
        xT = P.sb("xT", [128, NDT, TT], BF16) if need_xT else None
        hT = P.sb("hT", [128, KT, TT], BF16)
        zsq = [P.sb("zsq%d" % i, [128, TT], F32) for i in range(2)]
        spl = [[P.sb("spl%d_%d" % (i, j), [128, TT], BF16) for j in range(4)] for i in range(2)]
        bspl = [[k.buf("spl", i, j) for j in range(4)] for i in range(2)]
        zb = [P.sb("zb%d" % i, [128, TT], BF16) for i in range(2)]
        mean = P.sb("mean", [128, TT], F32)
        rstd = P.sb("rstd", [128, TT], F32)
        tmpv = P.sb("tmpv", [128, TT], F32)
        pin = [P.ps("pin%d" % i, [128, TT]) for i in range(4)]
        py = [P.ps("py%d" % i, [128, TT]) for i in range(2)]
        ps1 = P.ps("ps1", [128, TT])
        ps2 = P.ps("ps2", [128, TT])
        bzs = [[k.buf("z", i, d) for d in range(NDT)] for i in range(nz)]
        bxT = k.buf("xT")
        bh = [k.buf("hT", n) for n in range(KT)]
        bzsq = [k.buf("zsq", i) for i in range(2)]
        bzb = [k.buf("zb", i) for i in range(2)]
        bzls = [[k.buf("zl", j, i) for i in range(4)] for j in range(nz)]
        bzalls = [k.buf("zstore", j) for j in range(nz)]
        bmean, brstd, btmpv = k.buf("mean"), k.buf("rstd"), k.buf("tmpv")
        bpin = [k.buf("pin", i) for i in range(4)]
        bpy = [k.buf("py", i) for i in range(2)]
        bps1, bps2 = k.buf("ps1"), k.buf("ps2")
        gcol, bcol = ln_cols(L, lnj)
        eps_p = LN_EPS / (ALPHA * ALPHA)
        XTs = XT[src].rearrange("(dt p) t -> p dt t", p=128)
        XTd = XT[dst].rearrange("(dt p) t -> p dt t", p=128)
        XTbs = XTb[src].rearrange("(dt p) t -> p dt t", p=128)
        XTbd = XTb[dst].rearrange("(dt p) t -> p dt t", p=128)
        C.update(xT=xT, bxT=bxT, hT=hT, bh=bh, pin=pin, bpin=bpin, rot=0)
        stage_alloc(C)
        dq = []

        def pump(n):
            while n > 0 and dq:
                dq.pop(0)[1]()
                n -= 1

        def flush_upto(tile):
            while dq and dq[0][0] <= tile:
                dq.pop(0)[1]()

        C["pump"] = pump

        def load_xT(tt):
            if need_xT:
                k.dma("sp", xT[:], XTbs[:, :, tt * TT:(tt + 1) * TT], owner=bxT,
                      reads=[k.buf("XTb", src, tt)], writes=[bxT])

        def load_z(tt):
            flush_upto(tt - nz)
            z, bz, bzl = zs[tt % nz], bzs[tt % nz], bzls[tt % nz]
            for g4 in range(4):
                k.dma("sp", z[:, g4 * 4:(g4 + 1) * 4, :], XTs[:, g4 * 4:(g4 + 1) * 4, tt * TT:(tt + 1) * TT], owner=bzl[g4],
                      reads=[k.buf("XT", src, tt)], writes=bz[g4 * 4:(g4 + 1) * 4])

        def make_epilogue(tt):
            tsl = slice(tt * TT, (tt + 1) * TT)
            z, bz, bzall = zs[tt % nz], bzs[tt % nz], bzalls[tt % nz]
            items = []

            def add(fn):
                items.append((tt, fn))

            add(lambda: k.op("dve", lambda: nc.vector.tensor_scalar(out=mean[:], in0=ps1[:], scalar1=1.0 / D, scalar2=None, op0=ALU.mult),
                             reads=[bps1], writes=[bmean]))
            add(lambda: k.op("dve", lambda: nc.vector.tensor_tensor(out=tmpv[:], in0=mean[:], in1=mean[:], op=ALU.mult),
                             reads=[bmean], writes=[btmpv]))
            add(lambda: k.op("dve", lambda: nc.vector.scalar_tensor_tensor(out=tmpv[:], in0=ps2[:], scalar=1.0 / D, in1=tmpv[:],
                                                                           op0=ALU.mult, op1=ALU.subtract),
                             reads=[bps2, btmpv], writes=[btmpv]))
            add(lambda: k.op("dve", lambda: nc.vector.tensor_scalar(out=tmpv[:], in0=tmpv[:], scalar1=eps_p, scalar2=None, op0=ALU.add),
                             reads=[btmpv], writes=[btmpv]))
            add(lambda: k.op("act", lambda: nc.scalar.activation(out=rstd[:], in_=tmpv[:], func=AF.Sqrt),
                             reads=[btmpv], writes=[brstd]))
            add(lambda: k.op("dve", lambda: nc.vector.reciprocal(out=rstd[:], in_=rstd[:]), reads=[brstd], writes=[brstd]))
            for d in range(NDT):
                zi = d % 2
                add(lambda d=d: k.op("dve", lambda: nc.vector.tensor_tensor(out=z[:, d, :], in0=z[:, d, :], in1=mean[:], op=ALU.subtract),
                                     reads=[bz[d], bmean], writes=[bz[d]]))
                add(lambda d=d: k.op("dve", lambda: nc.vector.tensor_tensor(out=z[:, d, :], in0=z[:, d, :], in1=rstd[:], op=ALU.mult),
                                     reads=[bz[d], brstd], writes=[bz[d]]))
                add(lambda d=d: k.op("act", lambda: nc.scalar.activation(out=z[:, d, :], in_=z[:, d, :], func=AF.Identity,
                                                                         scale=lngb[:, gcol + d:gcol + d + 1], bias=lngb[:, bcol + d:bcol + d + 1]),
                                     reads=[bz[d], b_const], writes=[bz[d]]))
                add(lambda d=d, zi=zi: k.op("act", lambda: nc.scalar.copy(out=zb[zi][:], in_=z[:, d, :]), reads=[bz[d]], writes=[bzb[zi]]))
                add(lambda d=d, zi=zi: k.dma("act", XTbd[:, d, tsl], zb[zi][:], owner=bzb[zi], reads=[bzb[zi]], writes=[k.buf("XTb", dst, tt)]))
            add(lambda: k.dma("act", XTd[:, :, tsl], z[:], owner=bzall, reads=bz, writes=[k.buf("XT", dst, tt)]))
            return items

        def stats_mm(pd, pyi):
            k.op("pe", lambda: nc.tensor.matmul(ps1[:], lhsT=onesb[:], rhs=spl[pyi][0][:], start=(pd == 0), stop=False),
                 reads=[b_ones, bspl[pyi][0]], writes=[bps1])
            k.op("pe", lambda: nc.tensor.matmul(ps1[:], lhsT=onesb[:], rhs=spl[pyi][1][:], start=False, stop=(pd == NDT - 1)),
                 reads=[b_ones, bspl[pyi][1]], writes=[bps1])
            k.op("pe", lambda: nc.tensor.matmul(ps2[:], lhsT=onesb[:], rhs=spl[pyi][2][:], start=(pd == 0), stop=False),
                 reads=[b_ones, bspl[pyi][2]], writes=[bps2])
            k.op("pe", lambda: nc.tensor.matmul(ps2[:], lhsT=onesb[:], rhs=spl[pyi][3][:], start=False, stop=(pd == NDT - 1)),
                 reads=[b_ones, bspl[pyi][3]], writes=[bps2])

        C["load_z"] = load_z
        load_xT(0)
        cy = 0
        for tt in range(NTT):
            C["z_loaded"] = False
            z, bz = zs[tt % nz], bzs[tt % nz]
            stage_run(tt, C)
            if not C["z_loaded"]:
                load_z(tt)
            if tt + 1 < NTT:
                load_xT(tt + 1)
            pend = None
            for db in range(NBO):
                wbuf, wv = P.ring_next()
                for jd in range(2):
                    d = db * 2 + jd
                    yi = cy % 2
                    cy += 1
                    for kk in range(KT):
                        k.op("pe", lambda: nc.tensor.matmul(py[yi][:], lhsT=wv[:, kk, jd * 128:(jd + 1) * 128],
                                                            rhs=hT[:, kk, :], start=(kk == 0), stop=(kk == KT - 1)),
                             reads=[wbuf, bh[kk]], writes=[bpy[yi]])
                    k.op("dve", lambda: nc.vector.scalar_tensor_tensor(out=z[:, d, :], in0=py[yi][:], scalar=c_res,
                                                                       in1=z[:, d, :], op0=ALU.mult, op1=ALU.add),
                         reads=[bpy[yi], bz[d]], writes=[bz[d]])
                    k.op("act", lambda: nc.scalar.activation(out=zsq[yi][:], in_=z[:, d, :], func=AF.Square),
                         reads=[bz[d]], writes=[bzsq[yi]])
                    k.op("act", lambda: nc.scalar.copy(out=spl[yi][0][:], in_=z[:, d, :]), reads=[bz[d]], writes=[bspl[yi][0]])
                    k.op("dve", lambda: nc.vector.tensor_tensor(out=spl[yi][1][:], in0=z[:, d, :], in1=spl[yi][0][:], op=ALU.subtract),
                         reads=[bz[d], bspl[yi][0]], writes=[bspl[yi][1]])
                    k.op("act", lambda: nc.scalar.copy(out=spl[yi][2][:], in_=zsq[yi][:]), reads=[bzsq[yi]], writes=[bspl[yi][2]])
                    k.op("dve", lambda: nc.vector.tensor_tensor(out=spl[yi][3][:], in0=zsq[yi][:], in1=spl[yi][2][:], op=ALU.subtract),
                         reads=[bzsq[yi], bspl[yi][2]], writes=[bspl[yi][3]])
                    if pend is not None:
                        pd, pyi = pend
                        stats_mm(pd, pyi)
                    pend = (d, yi)
                    if not need_xT:
                        pump(6)
            pd, pyi = pend
            stats_mm(pd, pyi)
            P.ring_prefetch()
            dq.extend(make_epilogue(tt))
            if tt == NTT - 1:
                flush_upto(tt)
        k.barrier()
        P.release(m)

    def run_ffn(L, j, src, dst):
        wb_in, bufs_in, wb_out, bufs_out = conv[("ffn", L, j)]
        NBI = DFF // 256

        def blocks(tt):
            return [(wb_in[nb], bufs_in[nb], 16, 512) for nb in range(NBI)]

        def alloc(C):
            C["sg"] = [P.sb("sg%d" % i, [128, TT], F32) for i in range(2)]
            C["bsg"] = [k.buf("sg", i) for i in range(2)]
            C["cg"] = 0

        def run(tt, C):
            xT, bxT, hT, bh, pin, bpin, sg, bsg = (C[n] for n in ("xT", "bxT", "hT", "bh", "pin", "bpin", "sg", "bsg"))
            for nb in range(NBI):
                wbuf, wv = P.ring_next()
                if nb == 8:
                    C["load_z"](tt)
                    C["z_loaded"] = True
                for jn in range(2):
                    n = nb * 2 + jn
                    pi = C["cg"] % 2
                    C["cg"] += 1
                    pgb, pub = pin[pi], pin[2 + pi]
                    for kk in range(NDT):
                        k.op("pe", lambda: nc.tensor.matmul(pgb[:], lhsT=wv[:, kk, jn * 128:(jn + 1) * 128],
                                                            rhs=xT[:, kk, :], start=(kk == 0), stop=(kk == NDT - 1)),
                             reads=[wbuf, bxT], writes=[bpin[pi]])
                    for kk in range(NDT):
                        k.op("pe", lambda: nc.tensor.matmul(pub[:], lhsT=wv[:, kk, 256 + jn * 128:256 + (jn + 1) * 128],
                                                            rhs=xT[:, kk, :], start=(kk == 0), stop=(kk == NDT - 1)),
                             reads=[wbuf, bxT], writes=[bpin[2 + pi]])
                    k.op("act", lambda: nc.scalar.activation(out=sg[pi][:], in_=pgb[:], func=AF.Silu),
                         reads=[bpin[pi]], writes=[bsg[pi]])
                    k.op("dve", lambda: nc.vector.tensor_tensor(out=hT[:, n, :], in0=pub[:], in1=sg[pi][:], op=ALU.mult),
                         reads=[bpin[2 + pi], bsg[pi]], writes=[bh[n]])
                    C["pump"](8)

        phase_rowlocal(L, 0 if j == 0 else 2, src, dst, NHT, 0.5 / ALPHA, (wb_out, bufs_out), blocks, alloc, run)

    def run_conv(L, src, dst):
        wb_in, bufs_in, wb_out, bufs_out = conv[("conv", L)]

        def blocks(tt):
            return [(wb_in[d], bufs_in[d], 16, 384) for d in range(NDT)]

        def alloc(C):
            C["csb"] = [P.sb("csb%d" % i, [128, TT], F32) for i in range(2)]
            C["u"] = [P.sb("u%d" % i, [128, TT + 2], F32) for i in range(2)]
            C["yy"] = [P.sb("yy%d" % i, [128, TT], F32) for i in range(2)]
            C["uh"] = P.sb("uh", [128, NDT, 2], F32)
            C["bcsb"] = [k.buf("csb", i) for i in range(2)]
            C["bu"] = [k.buf("u", i) for i in range(2)]
            C["byy"] = [k.buf("yy", i) for i in range(2)]
            C["buh"] = [k.buf("uh", d) for d in range(NDT)]
            C["ci"] = 0

        def run(tt, C):
            xT, bxT, hT, bh, pin, bpin = (C[n] for n in ("xT", "bxT", "hT", "bh", "pin", "bpin"))
            seq_start = (tt % (T // TT) == 0)
            for d in range(NDT):
                wbuf, wv = P.ring_next()
                if d == 11:
                    C["load_z"](tt)
                    C["z_loaded"] = True
                i2 = C["ci"] % 2
                C["ci"] += 1
                banks = []
                for which in (1, 2, 0):
                    r = C["rot"] % 4
                    C["rot"] += 1
                    for kk in range(NDT):
                        k.op("pe", lambda: nc.tensor.matmul(pin[r][:], lhsT=wv[:, kk, which * 128:(which + 1) * 128],
                                                            rhs=xT[:, kk, :], start=(kk == 0), stop=(kk == NDT - 1)),
                             reads=[wbuf, bxT], writes=[bpin[r]])
                    banks.append(r)
                rc, rh, rb = banks
                csb, u, yy = C["csb"][i2], C["u"][i2], C["yy"][i2]
                bcsb, bu, byy, buh = C["bcsb"][i2], C["bu"][i2], C["byy"][i2], C["buh"][d]
                uh = C["uh"]
                k.op("act", lambda: nc.scalar.copy(out=csb[:], in_=pin[rc][:]), reads=[bpin[rc]], writes=[bcsb])
                k.op("dve", lambda: nc.vector.tensor_tensor(out=u[:, 2:TT + 2], in0=pin[rh][:], in1=csb[:], op=ALU.mult),
                     reads=[bpin[rh], bcsb], writes=[bu])
                if seq_start:
                    k.op("dve", lambda: nc.vector.memset(u[:, 0:2], 0.0), reads=[], writes=[bu])
                else:
                    k.op("dve", lambda: nc.vector.tensor_copy(out=u[:, 0:2], in_=uh[:, d, :]), reads=[buh], writes=[bu])
                k.op("dve", lambda: nc.vector.tensor_copy(out=uh[:, d, :], in_=u[:, TT:TT + 2]), reads=[bu], writes=[buh])
                k.op("dve", lambda: nc.vector.tensor_scalar(out=yy[:], in0=u[:, 2:TT + 2], scalar1=convw[:, 2 * NDT + d:2 * NDT + d + 1],
                                                            scalar2=None, op0=ALU.mult),
                     reads=[bu, b_const], writes=[byy])
                k.op("dve", lambda: nc.vector.scalar_tensor_tensor(out=yy[:], in0=u[:, 1:TT + 1], scalar=convw[:, NDT + d:NDT + d + 1],
                                                                   in1=yy[:], op0=ALU.mult, op1=ALU.add),
                     reads=[bu, byy, b_const], writes=[byy])
                k.op("dve", lambda: nc.vector.scalar_tensor_tensor(out=yy[:], in0=u[:, 0:TT], scalar=convw[:, d:d + 1],
                                                                   in1=yy[:], op0=ALU.mult, op1=ALU.add),
                     reads=[bu, byy, b_const], writes=[byy])
                k.op("dve", lambda: nc.vector.tensor_tensor(out=hT[:, d, :], in0=pin[rb][:], in1=yy[:], op=ALU.mult),
                     reads=[bpin[rb], byy], writes=[bh[d]])
                C["pump"](9)

        phase_rowlocal(L, 1, src, dst, NDT, 1.0 / ALPHA, (wb_out, bufs_out), blocks, alloc, run)

    def run_outproj_from_dram(L, src, dst, OT, otname, wout):
        OTv = OT.rearrange("(dt p) t -> p dt t", p=128)

        def blocks(tt):
            return []

        def alloc(C):
            C["bhl"] = k.buf("hload")

        def run(tt, C):
            k.dma("sp", C["hT"][:, :, :], OTv[:, :, tt * TT:(tt + 1) * TT], owner=C["bhl"],
                  reads=[k.buf(otname, tt)], writes=C["bh"])

        phase_rowlocal(L, 1, src, dst, NDT, 1.0 / ALPHA, wout, blocks, alloc, run, need_xT=False)

    def run_gla(L, src, dst):
        cw = conv[("gla", L)]
        NCH = NT // 64
        QD = P.dram_tmp("gla_QD", [1024, NT], BF16)
        KD = P.dram_tmp("gla_KD", [1024, NT], BF16)
        KL = P.dram_tmp("gla_KL", [NT, 1024], BF16)
        VV = P.dram_tmp("gla_V", [NT, 2048], BF16)
        RS = P.dram_tmp("gla_RS", [NT, 2048], F32)
        OT = P.dram_tmp("gla_OT", [D, NT], BF16)
        m0 = P.mark()
        dec = P.sb("gdec", [128, 8, NCH], F32)
        bdec = k.buf("gdec")
        gc = P.sb("gconst", [128, 256], F32)
        wa2 = P.sb("gwa2", [32, 1024], BF16)
        gng = P.sb("gng", [64, 512], F32)
        cmask = P.sb("gcm", [64, 64], F32)
        identb = P.sb("gidb", [128, 128], BF16)
        bgc = k.buf("gconst")
        k.dma("sp", gc[:], gla_const_in[:, :], owner=bgc, writes=[bgc])
        bgc2 = k.buf("gconst2")
        k.dma("sp", wa2[0:17, :], small["gla"]["wa2"][:, :], owner=bgc2, reads=[bsmall], writes=[bgc2])
        k.dma("sp", gng[:], gla_ng_in[:, :], owner=bgc, writes=[bgc])
        k.dma("sp", cmask[:], gla_cm_in[:, :], owner=bgc, writes=[bgc])
        k.op("act", lambda: nc.scalar.copy(out=identb[:], in_=ident[:]), reads=[b_const], writes=[bgc])
        tri16 = gc[:, 0:128]
        m16 = gc[:, 128:256]

        wfm, bfm = cw["fm"]
        wa, ba = cw["a"]
        wtm, btm = cw["tm"]
        blocks = []
        for tt in range(NTT):
            blocks.append((wa[0], ba[0], 16, 16))
            for i in range(4):
                blocks.append((wfm[i], bfm[i], 16, 512))
            for i in range(10):
                blocks.append((wtm[i], btm[i], 16, 512))
        P.ring_plan(blocks)
        m = P.mark()
        xT = P.sb("xT", [128, NDT, TT], BF16)
        bxT = k.buf("xT")
        gk = P.sb("gk", [128, 4, 1024], F32)
        bgk = k.buf("gk")
        aT = P.sb("aT", [32, TT], BF16)
        baT = k.buf("aT")
        k.op("dve", lambda: nc.vector.memset(aT[:], 1.0), writes=[baT])
        tmpe = [P.sb("tmpe%d" % i, [128, TT], F32) for i in range(6)]
        btmpe = [k.buf("tmpe", i) for i in range(6)]
        qd = P.sb("qd", [128, 8, TT], BF16)
        kd = P.sb("kd", [128, 8, TT], BF16)
        klst = P.sb("klst", [128, 4, 1024], BF16)
        vst = P.sb("vst", [128, 4, 2048], BF16)
        rst = [P.sb("rst%d" % i, [128, 4, 512], F32) for i in range(2)]
        bqd, bkd, bklst, bvst = k.buf("qd"), k.buf("kd"), k.buf("klst"), k.buf("vst")
        brst = [k.buf("rst", i) for i in range(2)]
        banks = [P.ps("gb%d" % i, [128, 512]) for i in range(8)]
        bbanks = [k.buf("gb", i) for i in range(8)]
        st = {"r": 0, "e": 0}

        def bank():
            i = st["r"] % 8
            st["r"] += 1
            return banks[i], bbanks[i]

        def tmp():
            i = st["e"] % 6
            st["e"] += 1
            return tmpe[i], btmpe[i]

        XTbs = XTb[src].rearrange("(dt p) t -> p dt t", p=128)
        for tt in range(NTT):
            tsl = slice(tt * TT, (tt + 1) * TT)
            k.dma("sp", xT[:], XTbs[:, :, tsl], owner=bxT, reads=[k.buf("XTb", src, tt)], writes=[bxT])
            wbuf, wv = P.ring_next()
            pb, bpb = bank()
            for kk in range(NDT):
                k.op("pe", lambda: nc.tensor.matmul(pb[0:16, :], lhsT=wv[:, kk, 0:16], rhs=xT[:, kk, :],
                                                    start=(kk == 0), stop=(kk == NDT - 1)), reads=[wbuf, bxT], writes=[bpb])
            k.op("act", lambda: nc.scalar.copy(out=aT[0:16, :], in_=pb[0:16, :]), reads=[bpb], writes=[baT])
            for sub in range(4):
                for half in range(2):
                    pb, bpb = bank()
                    k.op("pe", lambda: nc.tensor.matmul(pb[:], lhsT=aT[0:17, sub * 128:(sub + 1) * 128],
                                                        rhs=wa2[0:17, half * 512:(half + 1) * 512], start=True, stop=True),
                         reads=[baT, bgc2], writes=[bpb])
                    k.op("act", lambda: nc.scalar.activation(out=gk[:, sub, half * 512:(half + 1) * 512], in_=pb[:], func=AF.Sigmoid),
                         reads=[bpb], writes=[bgk])
            k.op("act", lambda: nc.scalar.activation(out=gk[:], in_=gk[:], func=AF.Ln), reads=[bgk], writes=[bgk])
            for i in range(4):
                wbuf, wv = P.ring_next()
                for jj in range(2):
                    dt_ = 2 * i + jj
                    pb, bpb = bank()
                    for sub in range(4):
                        k.op("pe", lambda: nc.tensor.matmul(pb[:, sub * 128:(sub + 1) * 128], lhsT=gk[:, sub, dt_ * 128:(dt_ + 1) * 128],
                                                            rhs=tri16, start=True, stop=True), reads=[bgk, bgc], writes=[bpb])
                    ebq, bebq = tmp()
                    ebk, bebk = tmp()
                    k.op("act", lambda: nc.scalar.activation(out=ebq[:], in_=pb[:], func=AF.Exp), reads=[bpb], writes=[bebq])
                    k.op("act", lambda: nc.scalar.activation(out=ebk[:], in_=pb[:], func=AF.Exp, scale=-1.0), reads=[bpb], writes=[bebk])
                    k.op("act", lambda: nc.scalar.activation(out=dec[:, dt_, tt * 8:(tt + 1) * 8],
                                                             in_=pb[:, :].rearrange("p (c s) -> p c s", s=64)[:, :, 63], func=AF.Exp),
                         reads=[bpb], writes=[bdec])
                    pq, bpq = bank()
                    for kk in range(NDT):
                        k.op("pe", lambda: nc.tensor.matmul(pq[:], lhsT=wv[:, kk, jj * 128:(jj + 1) * 128], rhs=xT[:, kk, :],
                                                            start=(kk == 0), stop=(kk == NDT - 1)), reads=[wbuf, bxT], writes=[bpq])
                    k.op("dve", lambda: nc.vector.scalar_tensor_tensor(out=qd[:, dt_, :], in0=pq[:], scalar=1.0 / 16.0, in1=ebq[:],
                                                                       op0=ALU.mult, op1=ALU.mult), reads=[bpq, bebq], writes=[bqd])
                    pk, bpk = bank()
                    for kk in range(NDT):
                        k.op("pe", lambda: nc.tensor.matmul(pk[:], lhsT=wv[:, kk, 256 + jj * 128:256 + (jj + 1) * 128], rhs=xT[:, kk, :],
                                                            start=(kk == 0), stop=(kk == NDT - 1)), reads=[wbuf, bxT], writes=[bpk])
                    k.op("dve", lambda: nc.vector.tensor_tensor(out=kd[:, dt_, :], in0=pk[:], in1=ebk[:], op=ALU.mult),
                         reads=[bpk, bebk], writes=[bkd])
            k.dma("act", QD.rearrange("(dt p) t -> p dt t", p=128)[:, :, tsl], qd[:], owner=bqd, reads=[bqd], writes=[k.buf("gQD", tt)])
            k.dma("act", KD.rearrange("(dt p) t -> p dt t", p=128)[:, :, tsl], kd[:], owner=bkd, reads=[bkd], writes=[k.buf("gKD", tt)])
            for blk in range(2):
                wbuf, wv = P.ring_next()
                for sub in range(4):
                    pb, bpb = bank()
                    k.op("pe", lambda: nc.tensor.matmul(pb[:], lhsT=m16, rhs=gk[:, sub, blk * 512:(blk + 1) * 512], start=True, stop=True),
                         reads=[bgk, bgc], writes=[bpb])
                    kle, bkle = tmp()
                    k.op("act", lambda: nc.scalar.activation(out=kle[:], in_=pb[:], func=AF.Exp), reads=[bpb], writes=[bkle])
                    pk, bpk = bank()
                    for kk in range(NDT):
                        k.op("pe", lambda: nc.tensor.matmul(pk[:], lhsT=xT[:, kk, sub * 128:(sub + 1) * 128], rhs=wv[:, kk, :],
                                                            start=(kk == 0), stop=(kk == NDT - 1)), reads=[wbuf, bxT], writes=[bpk])
                    k.op("dve", lambda: nc.vector.tensor_tensor(out=klst[:, sub, blk * 512:(blk + 1) * 512], in0=pk[:], in1=kle[:], op=ALU.mult),
                         reads=[bpk, bkle], writes=[bklst])
            k.dma("act", KL[tsl, :].rearrange("(s p) d -> p s d", p=128), klst[:], owner=bklst, reads=[bklst], writes=[k.buf("gKL", tt)])
            for blk in range(4):
                wbuf, wv = P.ring_next()
                for sub in range(4):
                    pk, bpk = bank()
                    for kk in range(NDT):
                        k.op("pe", lambda: nc.tensor.matmul(pk[:], lhsT=xT[:, kk, sub * 128:(sub + 1) * 128], rhs=wv[:, kk, :],
                                                            start=(kk == 0), stop=(kk == NDT - 1)), reads=[wbuf, bxT], writes=[bpk])
                    k.op("act", lambda: nc.scalar.copy(out=vst[:, sub, blk * 512:(blk + 1) * 512], in_=pk[:]), reads=[bpk], writes=[bvst])
            k.dma("act", VV[tsl, :].rearrange("(s p) d -> p s d", p=128), vst[:], owner=bvst, reads=[bvst], writes=[k.buf("gV", tt)])
            for blk in range(4):
                wbuf, wv = P.ring_next()
                ri = blk % 2
                for sub in range(4):
                    pk, bpk = bank()
                    for kk in range(NDT):
                        k.op("pe", lambda: nc.tensor.matmul(pk[:], lhsT=xT[:, kk, sub * 128:(sub + 1) * 128], rhs=wv[:, kk, :],
                                                            start=(kk == 0), stop=(kk == NDT - 1)), reads=[wbuf, bxT], writes=[bpk])
                    k.op("act", lambda: nc.scalar.activation(out=rst[ri][:, sub, :], in_=pk[:], func=AF.Silu), reads=[bpk], writes=[brst[ri]])
                k.dma("act", RS[tsl, blk * 512:(blk + 1) * 512].rearrange("(s p) d -> p s d", p=128), rst[ri][:], owner=brst[ri],
                      reads=[brst[ri]], writes=[k.buf("gRS", tt, blk)])
        k.barrier()
        P.release(m)

        m = P.mark()
        TB = 256
        qdl = P.sb("qdl", [128, 8, TB], BF16)
        kdl = P.sb("kdl", [128, 8, TB], BF16)
        kll = P.sb("kll", [64, 4, 1024], BF16)
        vl = P.sb("vl", [64, 4, 2048], BF16)
        rsl = [P.sb("rsl%d" % i, [64, 2048], F32) for i in range(2)]
        S = P.sb("gS", [128, 8, 512], F32)
        Sb = P.sb("gSb", [128, 8, 512], BF16)
        atm = [P.sb("atm%d" % i, [64, 64], BF16) for i in range(2)]
        og = [P.sb("og%d" % i, [64, 512], F32) for i in range(2)]
        ogb = [P.sb("ogb%d" % i, [64, 2048], BF16) for i in range(2)]
        sq = P.sb("gsq", [64, 512], F32)
        ms = [P.sb("gms%d" % i, [64, 1], F32) for i in range(4)]
        otst = [P.sb("otst%d" % i, [128, NDT, TB], BF16) for i in range(2)]
        bqdl, bkdl, bkll, bvl = k.buf("qdl"), k.buf("kdl"), k.buf("kll"), k.buf("vl")
        brsl = [k.buf("rsl", i) for i in range(2)]
        bS = [k.buf("gS", i) for i in range(8)]
        bSb = [k.buf("gSb", i) for i in range(8)]
        batm = [k.buf("atm", i) for i in range(2)]
        bog = [k.buf("og", i) for i in range(2)]
        bogb = [k.buf("ogb", i) for i in range(2)]
        bsq = k.buf("gsq")
        bms = [k.buf("gms", i) for i in range(4)]
        botst = [k.buf("otst", i) for i in range(2)]
        banks = [P.ps("hb%d" % i, [128, 512]) for i in range(6)]
        bbanks = [k.buf("hb", i) for i in range(6)]
        tb = [P.ps("htb%d" % i, [128, 512], BF16) for i in range(2)]
        btb = [k.buf("htb", i) for i in range(2)]
        st = {"r": 0, "a": 0, "m": 0, "t": 0}

        def bank():
            i = st["r"] % 6
            st["r"] += 1
            return banks[i], bbanks[i]

        for tb_ in range(NT // TB):
            tsl = slice(tb_ * TB, (tb_ + 1) * TB)
            tt = (tb_ * TB) // TT
            oi = tb_ % 2
            if tb_ % (T // TB) == 0:
                for i in range(8):
                    k.op("dve", lambda: nc.vector.memset(S[:, i, :], 0.0), writes=[bS[i]])
                    k.op("dve", lambda: nc.vector.memset(Sb[:, i, :], 0.0), writes=[bSb[i]])
            k.dma("sp", qdl[:], QD.rearrange("(dt p) t -> p dt t", p=128)[:, :, tsl], owner=bqdl, reads=[k.buf("gQD", tt)], writes=[bqdl])
            k.dma("sp", kdl[:], KD.rearrange("(dt p) t -> p dt t", p=128)[:, :, tsl], owner=bkdl, reads=[k.buf("gKD", tt)], writes=[bkdl])
            k.dma("sp", kll[:], KL[tsl, :].rearrange("(n c) d -> c n d", c=64), owner=bkll, reads=[k.buf("gKL", tt)], writes=[bkll])
            k.dma("sp", vl[:], VV[tsl, :].rearrange("(n c) d -> c n d", c=64), owner=bvl, reads=[k.buf("gV", tt)], writes=[bvl])
            for n in range(TB // 64):
                ch = tb_ * (TB // 64) + n
                csl = slice(n * 64, (n + 1) * 64)
                ri = ch % 2
                k.dma("sp", rsl[ri][:], RS[ch * 64:(ch + 1) * 64, :], owner=brsl[ri],
                      reads=[k.buf("gRS", tt, b_) for b_ in range(4)], writes=[brsl[ri]])
                for h in range(4):
                    pa, bpa = bank()
                    for dt2 in range(2):
                        k.op("pe", lambda: nc.tensor.matmul(pa[0:64, 0:64], lhsT=kdl[:, h * 2 + dt2, csl], rhs=qdl[:, h * 2 + dt2, csl],
                                                            start=(dt2 == 0), stop=(dt2 == 1)), reads=[bkdl, bqdl], writes=[bpa])
                    ai = st["a"] % 2
                    st["a"] += 1
                    k.op("dve", lambda: nc.vector.tensor_tensor(out=atm[ai][:], in0=pa[0:64, 0:64], in1=cmask[:], op=ALU.mult),
                         reads=[bpa, bgc], writes=[batm[ai]])
                    po, bpo = bank()
                    for dt2 in range(2):
                        k.op("pe", lambda: nc.tensor.matmul(po[0:64, :], lhsT=qdl[:, h * 2 + dt2, csl], rhs=Sb[:, h * 2 + dt2, :],
                                                            start=(dt2 == 0), stop=False), reads=[bqdl, bSb[h * 2 + dt2]], writes=[bpo])
                    k.op("pe", lambda: nc.tensor.matmul(po[0:64, :], lhsT=atm[ai][:], rhs=vl[:, n, h * 512:(h + 1) * 512],
                                                        start=False, stop=True), reads=[batm[ai], bvl], writes=[bpo])
                    for dt2 in range(2):
                        si = h * 2 + dt2
                        psn, bpsn = bank()
                        k.op("pe", lambda: nc.tensor.matmul(psn[:], lhsT=kll[:, n, si * 128:(si + 1) * 128], rhs=vl[:, n, h * 512:(h + 1) * 512],
                                                            start=True, stop=True), reads=[bkll, bvl], writes=[bpsn])
                        k.op("dve", lambda: nc.vector.scalar_tensor_tensor(out=S[:, si, :], in0=S[:, si, :], scalar=dec[:, si, ch:ch + 1],
                                                                           in1=psn[:], op0=ALU.mult, op1=ALU.add),
                             reads=[bS[si], bdec, bpsn], writes=[bS[si]])
                        k.op("act", lambda: nc.scalar.copy(out=Sb[:, si, :], in_=S[:, si, :]), reads=[bS[si]], writes=[bSb[si]])
                    mi = st["m"] % 4
                    st["m"] += 1
                    k.op("act", lambda: nc.scalar.activation(out=sq[:], in_=po[0:64, :], func=AF.Square, accum_out=ms[mi][:]),
                         reads=[bpo], writes=[bsq, bms[mi]])
                    k.op("dve", lambda: nc.vector.tensor_scalar(out=ms[mi][:], in0=ms[mi][:], scalar1=1.0 / 512.0, scalar2=LN_EPS,
                                                                op0=ALU.mult, op1=ALU.add), reads=[bms[mi]], writes=[bms[mi]])
                    k.op("act", lambda: nc.scalar.activation(out=ms[mi][:], in_=ms[mi][:], func=AF.Sqrt), reads=[bms[mi]], writes=[bms[mi]])
                    k.op("dve", lambda: nc.vector.reciprocal(out=ms[mi][:], in_=ms[mi][:]), reads=[bms[mi]], writes=[bms[mi]])
                    gi = st["m"] % 2
                    k.op("dve", lambda: nc.vector.scalar_tensor_tensor(out=og[gi][:], in0=po[0:64, :], scalar=ms[mi][:, 0:1], in1=gng[:],
                                                                       op0=ALU.mult, op1=ALU.mult), reads=[bpo, bms[mi], bgc], writes=[bog[gi]])
                    k.op("dve", lambda: nc.vector.tensor_tensor(out=ogb[ri][:, h * 512:(h + 1) * 512], in0=og[gi][:],
                                                                in1=rsl[ri][:, h * 512:(h + 1) * 512], op=ALU.mult),
                         reads=[bog[gi], brsl[ri]], writes=[bogb[ri]])
                for g4 in range(4):
                    ti = st["t"] % 2
                    st["t"] += 1
                    for q in range(4):
                        dt_ = g4 * 4 + q
                        k.op("pe", lambda: nc.tensor.transpose(tb[ti][:, q * 64:(q + 1) * 64], ogb[ri][:, dt_ * 128:(dt_ + 1) * 128], identb[0:64, 0:64]),
                             reads=[bogb[ri], bgc], writes=[btb[ti]])
                    k.op("act", lambda: nc.scalar.copy(out=otst[oi][:, g4 * 4:(g4 + 1) * 4, csl],
                                                       in_=tb[ti][:, 0:256].rearrange("p (q c) -> p q c", q=4)),
                         reads=[btb[ti]], writes=[botst[oi]])
            k.dma("act", OT.rearrange("(dt p) t -> p dt t", p=128)[:, :, tsl], otst[oi][:], owner=botst[oi], reads=[botst[oi]],
                  writes=[k.buf("gOT", tt)])
        k.barrier()
        P.release(m)
        P.release(m0)
        run_outproj_from_dram(L, src, dst, OT, "gOT", cw["out"])

    def run_nsa(L, src, dst):
        jn = L // 3
        cw = conv[("nsa", L)]
        nin = nsa_in[jn]
        QT = P.dram_tmp("nsa_QT%d" % L, [2048, NT], BF16)
        KCT = P.dram_tmp("nsa_KCT%d" % L, [512, NT], BF16)
        KST = P.dram_tmp("nsa_KST%d" % L, [512, NT], BF16)
        KWT = P.dram_tmp("nsa_KWT%d" % L, [512, NT], BF16)
        VCT = P.dram_tmp("nsa_VCT%d" % L, [512, NT], BF16)
        VS = P.dram_tmp("nsa_VS%d" % L, [NT, 512], BF16)
        VW = P.dram_tmp("nsa_VW%d" % L, [NT, 512], BF16)
        GT = P.dram_tmp("nsa_GT%d" % L, [NT, 48], F32)
        OT = P.dram_tmp("nsa_OT%d" % L, [D, NT], BF16)
        SCALE = 128.0 ** -0.5
        m0 = P.mark()
        bnc = k.buf("nconst")
        bnc2 = k.buf("nconst2")
        invs = P.sb("n_invs", [128, 2], F32)
        gb = P.sb("n_gb", [128, 48], F32)
        validc = P.sb("n_validc", [128, T], BF16)
        cam = P.sb("n_cam", [128, 256], BF16)
        addc = P.sb("n_addc", [128, 16 * 32], F32)
        esel = P.sb("n_esel", [128, 16 * 128], BF16)
        identb = P.sb("n_idb", [128, 128], BF16)
        kcmp = P.sb("n_kcmp", [128, nseq * 4, 128], BF16)
        vaug = P.sb("n_vaug", [128, nseq * 4, 161], BF16)
        bkcmp, bvaug = k.buf("n_kcmp"), k.buf("n_vaug")
        k.dma("sp", invs[:], nsa_invs_in[:, :], owner=bnc, writes=[bnc])
        k.dma("sp", gb[:], nin["gate_b"][:, :], owner=bnc, writes=[bnc])
        k.dma("sp", validc[:], nsa_validc_in[:, :], owner=bnc, writes=[bnc])
        k.dma("sp", cam[:], nsa_cam_in[:, :], owner=bnc, writes=[bnc])
        k.dma("sp", addc[:], nsa_addc_in[:, :], owner=bnc, writes=[bnc])
        k.dma("sp", esel[:], nsa_esel_in[:, :], owner=bnc, writes=[bnc])
        for sg in range(nseq * 4):
            k.dma("sp", vaug[:, sg, 128:161], nsa_ovl_in[:, :], owner=bnc, writes=[bnc])
        k.op("act", lambda: nc.scalar.copy(out=identb[:], in_=ident[:]), reads=[b_const], writes=[bnc])

        class Rot:
            def __init__(self, items, bufs):
                self.items, self.bufs, self.i = items, bufs, 0

            def __call__(self):
                j = self.i % len(self.items)
                self.i += 1
                return self.items[j], self.bufs[j]

        wr, br_ = cw["rope"]
        wvc, bvc = cw["vc"]
        wtm, btm = cw["tm"]
        wgl, bgl = cw["gl"]
        blocks = []
        for tt in range(NTT):
            for i in range(28):
                blocks.append((wr[i], br_[i], 16, 256))
            blocks.append((wvc[0], bvc[0], 16, 512))
            blocks.append((wtm[0], btm[0], 16, 512))
            blocks.append((wtm[1], btm[1], 16, 512))
            blocks.append((wgl[0], bgl[0], 16, 48))
        P.ring_plan(blocks)
        m = P.mark()
        xT = P.sb("xT", [128, NDT, TT], BF16)
        bxT = k.buf("xT")
        posi = P.sb("posi", [128, TT], I32)
        ang = P.sb("ang", [128, TT], F32)
        kk_ = P.sb("kk_", [128, TT], F32)
        rr = P.sb("rr", [128, TT], F32)
        cos2 = P.sb("cos2", [128, TT], F32)
        sin2 = P.sb("sin2", [128, TT], F32)
        bposi, bang, bkk, brr, bcos, bsin = (k.buf(n) for n in ("posi", "ang", "kk_", "rr", "cos2", "sin2"))
        t1r = Rot([P.sb("t1_%d" % i, [128, TT], F32) for i in range(2)], [k.buf("t1", i) for i in range(2)])
        t2r = Rot([P.sb("t2_%d" % i, [128, TT], F32) for i in range(2)], [k.buf("t2", i) for i in range(2)])
        str_ = Rot([P.sb("rst_%d" % i, [128, TT], BF16) for i in range(4)], [k.buf("rst_", i) for i in range(4)])
        vst = Rot([P.sb("nvst%d" % i, [128, 4, 512], BF16) for i in range(2)], [k.buf("nvst", i) for i in range(2)])
        gst = P.sb("gst", [128, 4, 48], F32)
        gtmp = P.sb("gtmp", [128, 48], F32)
        bgst, bgtmp = k.buf("gst"), k.buf("gtmp")
        bank = Rot([P.ps("nb%d" % i, [128, 512]) for i in range(8)], [k.buf("nb", i) for i in range(8)])
        MAGIC = 12582912.0
        C1 = 6.28125
        C2 = 2.0 * math.pi - 6.28125
        XTbs = XTb[src].rearrange("(dt p) t -> p dt t", p=128)

        def sincos(dst_t, bdst, shift, signed):
            k.op("dve", lambda: nc.vector.tensor_scalar(out=rr[:], in0=ang[:], scalar1=shift, scalar2=None, op0=ALU.add),
                 reads=[bang], writes=[brr])
            k.op("dve", lambda: nc.vector.tensor_scalar(out=kk_[:], in0=rr[:], scalar1=1.0 / (2.0 * math.pi), scalar2=MAGIC,
                                                        op0=ALU.mult, op1=ALU.add), reads=[brr], writes=[bkk])
            k.op("dve", lambda: nc.vector.tensor_scalar(out=kk_[:], in0=kk_[:], scalar1=-MAGIC, scalar2=None, op0=ALU.add),
                 reads=[bkk], writes=[bkk])
            k.op("dve", lambda: nc.vector.scalar_tensor_tensor(out=rr[:], in0=kk_[:], scalar=-C1, in1=rr[:], op0=ALU.mult, op1=ALU.add),
                 reads=[bkk, brr], writes=[brr])
            k.op("dve", lambda: nc.vector.scalar_tensor_tensor(out=rr[:], in0=kk_[:], scalar=-C2, in1=rr[:], op0=ALU.mult, op1=ALU.add),
                 reads=[bkk, brr], writes=[brr])
            k.op("dve", lambda: nc.vector.tensor_scalar(out=rr[:], in0=rr[:], scalar1=3.1415925, scalar2=-3.1415925,
                                                        op0=ALU.min, op1=ALU.max), reads=[brr], writes=[brr])
            if signed:
                k.op("act", lambda: nc.scalar.activation(out=dst_t[:], in_=rr[:], func=AF.Sin, scale=invs[:, 1:2]),
                     reads=[brr, bnc], writes=[bdst])
            else:
                k.op("act", lambda: nc.scalar.activation(out=dst_t[:], in_=rr[:], func=AF.Sin), reads=[brr], writes=[bdst])

        for tt in range(NTT):
            tsl = slice(tt * TT, (tt + 1) * TT)
            k.dma("sp", xT[:], XTbs[:, :, tsl], owner=bxT, reads=[k.buf("XTb", src, tt)], writes=[bxT])
            k.dma("sp", posi[:], nsa_pos_in[:, tsl], owner=bposi, writes=[bposi])
            k.op("dve", lambda: nc.vector.tensor_copy(out=ang[:], in_=posi[:]), reads=[bposi], writes=[bang])
            k.op("dve", lambda: nc.vector.tensor_scalar(out=ang[:], in0=ang[:], scalar1=invs[:, 0:1], scalar2=None, op0=ALU.mult),
                 reads=[bang, bnc], writes=[bang])
            sincos(cos2, bcos, math.pi / 2.0, False)
            sincos(sin2, bsin, 0.0, False)
            for i in range(28):
                wbuf, wv = P.ring_next()
                sc = SCALE if i < 16 else 1.0
                p1, bp1 = bank()
                for kk in range(NDT):
                    k.op("pe", lambda: nc.tensor.matmul(p1[:], lhsT=wv[:, kk, 0:128], rhs=xT[:, kk, :], start=(kk == 0), stop=(kk == NDT - 1)),
                         reads=[wbuf, bxT], writes=[bp1])
                t1, bt1 = t1r()
                t2, bt2 = t2r()
                so, bso = str_()
                lo, hi = slice(0, 64), slice(64, 128)
                k.op("dve", lambda: nc.vector.scalar_tensor_tensor(out=t1[lo, :], in0=p1[lo, :], scalar=sc, in1=cos2[lo, :], op0=ALU.mult, op1=ALU.mult),
                     reads=[bp1, bcos], writes=[bt1])
                k.op("dve", lambda: nc.vector.scalar_tensor_tensor(out=t2[lo, :], in0=p1[hi, :], scalar=sc, in1=sin2[hi, :], op0=ALU.mult, op1=ALU.mult),
                     reads=[bp1, bsin], writes=[bt2])
                k.op("dve", lambda: nc.vector.tensor_tensor(out=so[lo, :], in0=t1[lo, :], in1=t2[lo, :], op=ALU.subtract), reads=[bt1, bt2], writes=[bso])
                k.op("dve", lambda: nc.vector.scalar_tensor_tensor(out=t1[hi, :], in0=p1[hi, :], scalar=sc, in1=cos2[hi, :], op0=ALU.mult, op1=ALU.mult),
                     reads=[bp1, bcos], writes=[bt1])
                k.op("dve", lambda: nc.vector.scalar_tensor_tensor(out=t2[hi, :], in0=p1[lo, :], scalar=sc, in1=sin2[lo, :], op0=ALU.mult, op1=ALU.mult),
                     reads=[bp1, bsin], writes=[bt2])
                k.op("dve", lambda: nc.vector.tensor_tensor(out=so[hi, :], in0=t1[hi, :], in1=t2[hi, :], op=ALU.add), reads=[bt1, bt2], writes=[bso])
                if i < 16:
                    dst_ap, dname = QT[i * 128:(i + 1) * 128, tsl], ("nQT", i, tt)
                elif i < 20:
                    dst_ap, dname = KCT[(i - 16) * 128:(i - 15) * 128, tsl], ("nKCT", i - 16, tt)
                elif i < 24:
                    dst_ap, dname = KST[(i - 20) * 128:(i - 19) * 128, tsl], ("nKST", i - 20, tt)
                else:
                    dst_ap, dname = KWT[(i - 24) * 128:(i - 23) * 128, tsl], ("nKWT", i - 24, tt)
                k.dma("act", dst_ap, so[:], owner=bso, reads=[bso], writes=[k.buf(*dname)])
            wbuf, wv = P.ring_next()
            for g in range(4):
                p1, bp1 = bank()
                for kk in range(NDT):
                    k.op("pe", lambda: nc.tensor.matmul(p1[:], lhsT=wv[:, kk, g * 128:(g + 1) * 128], rhs=xT[:, kk, :],
                                                        start=(kk == 0), stop=(kk == NDT - 1)), reads=[wbuf, bxT], writes=[bp1])
                so, bso = str_()
                k.op("act", lambda: nc.scalar.copy(out=so[:], in_=p1[:]), reads=[bp1], writes=[bso])
                k.dma("act", VCT[g * 128:(g + 1) * 128, tsl], so[:], owner=bso, reads=[bso], writes=[k.buf("nVCT", g, tt)])
            for which, DST, nm in ((0, VS, "nVS"), (1, VW, "nVW")):
                wbuf, wv = P.ring_next()
                vs_, bvs_ = vst()
                for sub in range(4):
                    p1, bp1 = bank()
                    for kk in range(NDT):
                        k.op("pe", lambda: nc.tensor.matmul(p1[:], lhsT=xT[:, kk, sub * 128:(sub + 1) * 128], rhs=wv[:, kk, :],
                                                            start=(kk == 0), stop=(kk == NDT - 1)), reads=[wbuf, bxT], writes=[bp1])
                    k.op("act", lambda: nc.scalar.copy(out=vs_[:, sub, :], in_=p1[:]), reads=[bp1], writes=[bvs_])
                k.dma("act", DST[tsl, :].rearrange("(s p) d -> p s d", p=128), vs_[:], owner=bvs_, reads=[bvs_], writes=[k.buf(nm, tt)])
            wbuf, wv = P.ring_next()
            for sub in range(4):
                p1, bp1 = bank()
                for kk in range(NDT):
                    k.op("pe", lambda: nc.tensor.matmul(p1[:, 0:48], lhsT=xT[:, kk, sub * 128:(sub + 1) * 128], rhs=wv[:, kk, 0:48],
                                                        start=(kk == 0), stop=(kk == NDT - 1)), reads=[wbuf, bxT], writes=[bp1])
                k.op("dve", lambda: nc.vector.tensor_tensor(out=gtmp[:], in0=p1[:, 0:48], in1=gb[:], op=ALU.add), reads=[bp1, bnc], writes=[bgtmp])
                k.op("act", lambda: nc.scalar.activation(out=gst[:, sub, :], in_=gtmp[:], func=AF.Sigmoid), reads=[bgtmp], writes=[bgst])
            k.dma("act", GT[tsl, :].rearrange("(s p) d -> p s d", p=128), gst[:], owner=bgst, reads=[bgst], writes=[k.buf("nGT", tt)])
        k.barrier()
        P.release(m)

        m = P.mark()
        w1 = P.sb("n_w1", [128, 2, 32, 128], BF16)
        w2 = P.sb("n_w2", [128, 2, 128], BF16)
        posT = P.sb("n_posT", [128, 64], BF16)
        sm_ = small[("nsa", jn)]
        for kv in range(2):
            k.dma("sp", w1[:, kv, :, :], sm_["w1"][kv].rearrange("(l p) j -> p l j", p=128), owner=bnc2, reads=[bsmall], writes=[bnc2])
            k.dma("sp", w2[:, kv, :], sm_["w2"][kv], owner=bnc2, reads=[bsmall], writes=[bnc2])
        k.dma("sp", posT[:], sm_["posT"][:, :], owner=bnc2, reads=[bsmall], writes=[bnc2])
        kct = P.sb("kct", [128, T], BF16)
        bkct = k.buf("kct")
        biasv = P.sb("biasv", [128, 2], F32)
        bbias = k.buf("biasv")
        xg = P.sb("xg", [128, 128], F32)
        x2 = P.sb("x2g", [128, 128], F32)
        gT = P.sb("gTg", [128, 128], BF16)
        bxg, bx2, bgT = k.buf("xg"), k.buf("x2g"), k.buf("gTg")
        bank = Rot([P.ps("cb%d" % i, [128, 512]) for i in range(4)], [k.buf("cb", i) for i in range(4)])
        for kv in range(2):
            pb, bpb = bank()
            for l in range(32):
                k.op("pe", lambda: nc.tensor.matmul(pb[:, 0:1], lhsT=w1[:, kv, l, :], rhs=posT[:, kv * 32 + l:kv * 32 + l + 1],
                                                    start=(l == 0), stop=(l == 31)), reads=[bnc2], writes=[bpb])
            k.op("dve", lambda: nc.vector.tensor_copy(out=biasv[:, kv:kv + 1], in_=pb[:, 0:1]), reads=[bpb], writes=[bbias])
        for sq_ in range(nseq):
            for g in range(4):
                sg = sq_ * 4 + g
                for kv in range(2):
                    SRC = KCT if kv == 0 else VCT
                    k.dma("sp", kct[:], SRC[g * 128:(g + 1) * 128, sq_ * T:(sq_ + 1) * T], owner=bkct,
                          reads=[k.buf("nKCT" if kv == 0 else "nVCT", g, tt_) for tt_ in range(sq_ * 4, sq_ * 4 + 4)], writes=[bkct])
                    pb, bpb = bank()
                    for l in range(32):
                        k.op("pe", lambda: nc.tensor.matmul(pb[:, 0:127], lhsT=w1[:, kv, l, :], rhs=kct[:, l:l + 16 * 126 + 1:16],
                                                            start=(l == 0), stop=(l == 31)), reads=[bnc2, bkct], writes=[bpb])
                    k.op("act", lambda: nc.scalar.activation(out=xg[:, 0:127], in_=pb[:, 0:127], func=AF.Identity, bias=biasv[:, kv:kv + 1]),
                         reads=[bpb, bbias], writes=[bxg])
                    k.op("dve", lambda: nc.vector.tensor_tensor(out=x2[:, 0:127], in0=xg[:, 0:127], in1=xg[:, 0:127], op=ALU.mult),
                         reads=[bxg], writes=[bx2])
                    k.op("dve", lambda: nc.vector.tensor_scalar(out=x2[:, 0:127], in0=x2[:, 0:127], scalar1=0.044715, scalar2=1.0,
                                                                op0=ALU.mult, op1=ALU.add), reads=[bx2], writes=[bx2])
                    k.op("dve", lambda: nc.vector.tensor_tensor(out=x2[:, 0:127], in0=x2[:, 0:127], in1=xg[:, 0:127], op=ALU.mult),
                         reads=[bx2, bxg], writes=[bx2])
                    k.op("act", lambda: nc.scalar.activation(out=x2[:, 0:127], in_=x2[:, 0:127], func=AF.Tanh, scale=math.sqrt(2.0 / math.pi)),
                         reads=[bx2], writes=[bx2])
                    k.op("dve", lambda: nc.vector.tensor_scalar(out=x2[:, 0:127], in0=x2[:, 0:127], scalar1=1.0, scalar2=0.5,
                                                                op0=ALU.add, op1=ALU.mult), reads=[bx2], writes=[bx2])
                    k.op("dve", lambda: nc.vector.tensor_tensor(out=gT[:, 0:127], in0=x2[:, 0:127], in1=xg[:, 0:127], op=ALU.mult),
                         reads=[bx2, bxg], writes=[bgT])
                    pc, bpc = bank()
                    if kv == 0:
                        k.op("pe", lambda: nc.tensor.matmul(pc[:, 0:127], lhsT=w2[:, 0, :], rhs=gT[:, 0:127], start=True, stop=True),
                             reads=[bnc2, bgT], writes=[bpc])
                        k.op("act", lambda: nc.scalar.copy(out=kcmp[:, sg, 0:127], in_=pc[:, 0:127]), reads=[bpc], writes=[bkcmp])
                    else:
                        k.op("pe", lambda: nc.tensor.matmul(pc[0:127, 0:128], lhsT=gT[:, 0:127], rhs=w2[:, 1, :], start=True, stop=True),
                             reads=[bnc2, bgT], writes=[bpc])
                        k.op("act", lambda: nc.scalar.copy(out=vaug[0:127, sg, 0:128], in_=pc[0:127, 0:128]), reads=[bpc, bnc], writes=[bvaug])
        k.barrier()
        P.release(m)

        m = P.mark()
        ksT = P.sb("ksT", [128, T], BF16)
        kwT = P.sb("kwT", [128, T], BF16)
        vsa = P.sb("vsa", [128, 16, 129], BF16)
        vwa = P.sb("vwa", [128, 16, 129], BF16)
        qT4 = P.sb("qT4", [128, 4, T], BF16)
        bksT, bkwT, bvsa, bvwa = k.buf("ksT"), k.buf("kwT"), k.buf("vsa"), k.buf("vwa")
        bqT4 = [k.buf("qT4", r) for r in range(4)]
        k.op("dve", lambda: nc.vector.memset(vsa[:, :, 128:129], 1.0), writes=[bvsa])
        k.op("dve", lambda: nc.vector.memset(vwa[:, :, 128:129], 1.0), writes=[bvwa])
        pTs = [P.sb("pTs%d" % i, [128, 16, TT], BF16) for i in range(2)]
        pTw = [P.sb("pTw%d" % i, [128, 8, TT], BF16) for i in range(2)]
        bpTs = [[k.buf("pTs", i, j) for j in range(16)] for i in range(2)]
        bpTw = [[k.buf("pTw", i, j) for j in range(8)] for i in range(2)]
        ec = Rot([P.sb("ec%d" % i, [128, TT], BF16) for i in range(2)], [k.buf("ec", i) for i in range(2)])
        ocmp2 = [P.sb("ocmp%d" % i, [128, 4, 4, 128], F32) for i in range(2)]
        bocmp2 = [[[k.buf("ocmp", i, r, q) for q in range(4)] for r in range(4)] for i in range(2)]
        imp2 = [P.sb("imp%d" % i, [128, 4, 32], F32) for i in range(2)]
        bimp2 = [[k.buf("imp", i, q) for q in range(4)] for i in range(2)]
        cnts = {"qt": 0, "h": 0}
        cmp3 = P.sb("cmp3", [128, 4, 32, 32], BF16)
        bcmp3 = k.buf("cmp3")
        rank = P.sb("rank", [128, 4, 32], F32)
        brank = k.buf("rank")
        selbT2 = [P.sb("selbT%d" % i, [128, TT], BF16) for i in range(2)]
        bselbT2 = [k.buf("selbT", i) for i in range(2)]
        for i in range(2):
            k.op("dve", lambda: nc.vector.memset(selbT2[i][:], 0.0), writes=[bselbT2[i]])
        impt = P.sb("impt", [128, 4, 32], F32)
        bimpt = k.buf("impt")
        gat2 = [P.sb("gat%d" % i, [128, 4, 48], F32) for i in range(2)]
        bgat2 = [k.buf("gat", i) for i in range(2)]
        sm = Rot([P.sb("sm%d" % i, [128, 16], F32) for i in range(4)], [k.buf("sm", i) for i in range(4)])
        acc = Rot([P.sb("acc%d" % i, [128, 4, 128], F32) for i in range(2)], [k.buf("acc", i) for i in range(2)])
        accb = Rot([P.sb("accb%d" % i, [128, 4, 128], BF16) for i in range(2)], [k.buf("accb", i) for i in range(2)])
        ots = Rot([P.sb("ots%d" % i, [128, TT], BF16) for i in range(2)], [k.buf("ots", i) for i in range(2)])
        bank = Rot([P.ps("ab%d" % i, [128, 512]) for i in range(3)], [k.buf("ab", i) for i in range(3)])
        pvbank = Rot([P.ps("ab%d" % i, [128, 512]) for i in range(3, 6)], [k.buf("ab", i) for i in range(3, 6)])
        tbank = Rot([P.ps("atb%d" % i, [128, 512], BF16) for i in range(2)], [k.buf("atb", i) for i in range(2)])

        for sq_ in range(nseq):
            s0 = sq_ * T
            for g in range(4):
                sg = sq_ * 4 + g
                tts = range(sq_ * 4, sq_ * 4 + 4)
                k.dma("sp", ksT[:], KST[g * 128:(g + 1) * 128, s0:s0 + T], owner=bksT, reads=[k.buf("nKST", g, t_) for t_ in tts], writes=[bksT])
                k.dma("sp", kwT[:], KWT[g * 128:(g + 1) * 128, s0:s0 + T], owner=bkwT, reads=[k.buf("nKWT", g, t_) for t_ in tts], writes=[bkwT])
                k.dma("sp", vsa[:, :, 0:128], VS[s0:s0 + T, g * 128:(g + 1) * 128].rearrange("(kt p) d -> p kt d", p=128), owner=bvsa,
                      reads=[k.buf("nVS", t_) for t_ in tts], writes=[bvsa])
                k.dma("sp", vwa[:, :, 0:128], VW[s0:s0 + T, g * 128:(g + 1) * 128].rearrange("(kt p) d -> p kt d", p=128), owner=bvwa,
                      reads=[k.buf("nVW", t_) for t_ in tts], writes=[bvwa])
                for r in range(4):
                    h = g * 4 + r
                    k.dma("sp", qT4[:, r, :], QT[h * 128:(h + 1) * 128, s0:s0 + T], owner=bqT4[r],
                          reads=[k.buf("nQT", h, t_) for t_ in tts], writes=[bqT4[r]])
                pvq = []
                for qt in range(4):
                    q0 = qt * TT
                    qsl = slice(q0, q0 + TT)
                    gat, bgat = gat2[cnts["qt"] % 2], bgat2[cnts["qt"] % 2]
                    k.dma("sp", gat[:], GT[s0 + q0:s0 + q0 + TT, :].rearrange("(s p) d -> p s d", p=128), owner=bgat,
                          reads=[k.buf("nGT", sq_ * 4 + qt)], writes=[bgat])
                    oi = cnts["qt"] % 2
                    cnts["qt"] += 1
                    ocmp, bocmp, imp, bimp = ocmp2[oi], bocmp2[oi], imp2[oi], bimp2[oi]
                    for r in range(4):
                        h = g * 4 + r
                        pb, bpb = bank()
                        k.op("pe", lambda: nc.tensor.matmul(pb[0:127, :], lhsT=kcmp[:, sg, 0:127], rhs=qT4[:, r, qsl], start=True, stop=True),
                             reads=[bkcmp, bqT4[r]], writes=[bpb])
                        e_, be_ = ec()
                        k.op("act", lambda: nc.scalar.activation(out=e_[0:127, :], in_=pb[0:127, :], func=AF.Exp), reads=[bpb], writes=[be_])
                        k.op("dve", lambda: nc.vector.tensor_tensor(out=e_[0:127, :], in0=e_[0:127, :], in1=validc[0:127, qsl], op=ALU.mult),
                             reads=[be_, bnc], writes=[be_])
                        po, bpo = bank()
                        pil, bpil = bank()
                        for qs in range(4):
                            k.op("pe", lambda: nc.tensor.matmul(po[:, qs * 128:(qs + 1) * 128], lhsT=e_[0:127, qs * 128:(qs + 1) * 128],
                                                                rhs=vaug[0:127, sg, 0:128], start=True, stop=True), reads=[be_, bvaug], writes=[bpo])
                            k.op("pe", lambda: nc.tensor.matmul(pil[:, qs * 33:(qs + 1) * 33], lhsT=e_[0:127, qs * 128:(qs + 1) * 128],
                                                                rhs=vaug[0:127, sg, 128:161], start=True, stop=True), reads=[be_, bvaug], writes=[bpil])
                        s_, bs_ = sm()
                        ilv = pil[:, 0:132].rearrange("p (q c) -> p q c", q=4)
                        k.op("dve", lambda: nc.vector.tensor_scalar(out=s_[:, 0:4], in0=ilv[:, :, 32], scalar1=1e-30, scalar2=None, op0=ALU.add),
                             reads=[bpil], writes=[bs_])
                        k.op("dve", lambda: nc.vector.reciprocal(out=s_[:, 4:8], in_=s_[:, 0:4]), reads=[bs_], writes=[bs_])
                        k.op("dve", lambda: nc.vector.tensor_tensor(out=s_[:, 8:12], in0=s_[:, 4:8], in1=gat[:, :, h], op=ALU.mult),
                             reads=[bs_, bgat], writes=[bs_])
                        k.op("dve", lambda: nc.vector.tensor_tensor(out=ocmp[:, r, :, :], in0=po[:, :].rearrange("p (q d) -> p q d", q=4),
                                                                    in1=s_[:, 8:12].unsqueeze(2).to_broadcast([128, 4, 128]), op=ALU.mult),
                             reads=[bpo, bs_], writes=[bocmp[r][0]])
                        if r == 0:
                            k.op("dve", lambda: nc.vector.tensor_tensor(out=imp[:, :, :], in0=ilv[:, :, 0:32],
                                                                        in1=s_[:, 4:8].unsqueeze(2).to_broadcast([128, 4, 32]), op=ALU.mult),
                                 reads=[bpil, bs_], writes=[bimp[0]])
                        else:
                            k.op("dve", lambda: nc.vector.tensor_tensor(out=impt[:, :, :], in0=ilv[:, :, 0:32],
                                                                        in1=s_[:, 4:8].unsqueeze(2).to_broadcast([128, 4, 32]), op=ALU.mult),
                                 reads=[bpil, bs_], writes=[bimpt])
                            k.op("dve", lambda: nc.vector.tensor_tensor(out=imp[:, :, :], in0=imp[:, :, :], in1=impt[:, :, :], op=ALU.add),
                                 reads=[bimp[0], bimpt], writes=[bimp[0]])
                    selbT, bselbT = selbT2[oi], bselbT2[oi]
                    need_sel = (qt * TT + TT - 1) // 64 + 1 > 16
                    if need_sel:
                        k.op("dve", lambda: nc.vector.tensor_tensor(out=imp[:, :, :], in0=imp[:, :, :],
                                                                    in1=addc[:, qt * 128:(qt + 1) * 128].rearrange("p (q m) -> p q m", q=4), op=ALU.add),
                             reads=[bimp[0], bnc], writes=[bimp[0]])
                        k.op("dve", lambda: nc.vector.tensor_tensor(out=cmp3[:], in0=imp[:, :, :].unsqueeze(2).to_broadcast([128, 4, 32, 32]),
                                                                    in1=imp[:, :, :].unsqueeze(3).to_broadcast([128, 4, 32, 32]), op=ALU.is_gt),
                             reads=[bimp[0]], writes=[bcmp3])
                        k.op("dve", lambda: nc.vector.reduce_sum(out=rank[:], in_=cmp3[:], axis=AX.X), reads=[bcmp3], writes=[brank])
                        k.op("dve", lambda: nc.vector.tensor_scalar(out=rank[:], in0=rank[:], scalar1=15.5, scalar2=-30000.0,
                                                                    op0=ALU.is_gt, op1=ALU.mult), reads=[brank], writes=[brank])
                        pt, bpt = bank()
                        for qs in range(4):
                            k.op("pe", lambda: nc.tensor.transpose(pt[0:32, qs * 128:(qs + 1) * 128], rank[:, qs, :], ident[:]), reads=[brank, b_const], writes=[bpt])
                        k.op("act", lambda: nc.scalar.copy(out=selbT[0:32, :], in_=pt[0:32, :]), reads=[bpt], writes=[bselbT])

                    def S_phase(r, pi):
                        pTs_, pTw_, bps_, bpw_ = pTs[pi], pTw[pi], bpTs[pi], bpTw[pi]
                        nks = qt * 4 + 4
                        for ki in range(nks):
                            pb, bpb = bank()
                            k.op("pe", lambda: nc.tensor.matmul(pb[:], lhsT=ksT[:, ki * 128:(ki + 1) * 128], rhs=qT4[:, r, qsl], start=True, stop=not need_sel),
                                 reads=[bksT, bqT4[r]], writes=[bpb])
                            if need_sel:
                                k.op("pe", lambda: nc.tensor.matmul(pb[:], lhsT=esel[:, ki * 128:(ki + 1) * 128], rhs=selbT[:, :], start=False, stop=True),
                                     reads=[bnc, bselbT], writes=[bpb])
                            k.op("act", lambda: nc.scalar.activation(out=pTs_[:, ki, :], in_=pb[:], func=AF.Exp), reads=[bpb], writes=[bps_[ki]])
                            if ki >= qt * 4:
                                qs = ki - qt * 4
                                k.op("dve", lambda: nc.vector.tensor_tensor(out=pTs_[:, ki, qs * 128:(qs + 1) * 128], in0=pTs_[:, ki, qs * 128:(qs + 1) * 128],
                                                                            in1=cam[:, 0:128], op=ALU.mult), reads=[bps_[ki], bnc], writes=[bps_[ki]])
                            if pvq:
                                pvq.pop(0)()
                        kw0 = max(0, qt * 4 - 4)
                        for ki in range(kw0, qt * 4 + 4):
                            wi = ki - kw0
                            if pvq:
                                pvq.pop(0)()
                            pb, bpb = bank()
                            k.op("pe", lambda: nc.tensor.matmul(pb[:], lhsT=kwT[:, ki * 128:(ki + 1) * 128], rhs=qT4[:, r, qsl], start=True, stop=True),
                                 reads=[bkwT, bqT4[r]], writes=[bpb])
                            k.op("act", lambda: nc.scalar.activation(out=pTw_[:, wi, :], in_=pb[:], func=AF.Exp), reads=[bpb], writes=[bpw_[wi]])
                            for qs in range(4):
                                qi = qt * 4 + qs
                                if ki == qi:
                                    mk = cam[:, 0:128]
                                elif ki == qi - 4:
                                    mk = cam[:, 128:256]
                                else:
                                    continue
                                k.op("dve", lambda: nc.vector.tensor_tensor(out=pTw_[:, wi, qs * 128:(qs + 1) * 128], in0=pTw_[:, wi, qs * 128:(qs + 1) * 128],
                                                                            in1=mk, op=ALU.mult), reads=[bpw_[wi], bnc], writes=[bpw_[wi]])

                    def PV_chunks(r, pi, qt=qt, g=g, ocmp=ocmp, bocmp=bocmp, gat=gat, bgat=bgat, q0=q0):
                        pTs_, pTw_, bps_, bpw_ = pTs[pi], pTw[pi], bpTs[pi], bpTw[pi]
                        h = g * 4 + r
                        kw0 = max(0, qt * 4 - 4)
                        st_ = {}
                        chunks = []

                        def c_alloc():
                            st_["po"], st_["bpo"] = pvbank()
                            st_["pw"], st_["bpw"] = pvbank()
                            st_["pl"], st_["bpl"] = pvbank()

                        def c_sel(qs):
                            if qs == 0:
                                c_alloc()
                            po, bpo, pl, bpl = st_["po"], st_["bpo"], st_["pl"], st_["bpl"]
                            qi = qt * 4 + qs
                            for ki in range(qi + 1):
                                k.op("pe", lambda: nc.tensor.matmul(po[:, qs * 128:(qs + 1) * 128], lhsT=pTs_[:, ki, qs * 128:(qs + 1) * 128], rhs=vsa[:, ki, 0:128],
                                                                    start=(ki == 0), stop=(ki == qi)), reads=[bps_[ki], bvsa], writes=[bpo])
                            for ki in range(qi + 1):
                                k.op("pe", lambda: nc.tensor.matmul(pl[:, 2 * qs:2 * qs + 1], lhsT=pTs_[:, ki, qs * 128:(qs + 1) * 128], rhs=vsa[:, ki, 128:129],
                                                                    start=(ki == 0), stop=(ki == qi)), reads=[bps_[ki], bvsa], writes=[bpl])

                        def c_win(qs):
                            pw, bpw, pl, bpl = st_["pw"], st_["bpw"], st_["pl"], st_["bpl"]
                            qi = qt * 4 + qs
                            kis = list(range(max(0, qi - 4), qi + 1))
                            for ki in kis:
                                k.op("pe", lambda: nc.tensor.matmul(pw[:, qs * 128:(qs + 1) * 128], lhsT=pTw_[:, ki - kw0, qs * 128:(qs + 1) * 128], rhs=vwa[:, ki, 0:128],
                                                                    start=(ki == kis[0]), stop=(ki == kis[-1])), reads=[bpw_[ki - kw0], bvwa], writes=[bpw])
                            for ki in kis:
                                k.op("pe", lambda: nc.tensor.matmul(pl[:, 2 * qs + 1:2 * qs + 2], lhsT=pTw_[:, ki - kw0, qs * 128:(qs + 1) * 128], rhs=vwa[:, ki, 128:129],
                                                                    start=(ki == kis[0]), stop=(ki == kis[-1])), reads=[bpw_[ki - kw0], bvwa], writes=[bpl])

                        def c_fin():
                            po, bpo, pw, bpw, pl, bpl = st_["po"], st_["bpo"], st_["pw"], st_["bpw"], st_["pl"], st_["bpl"]
                            s_, bs_ = sm()
                            a_, ba_ = acc()
                            t_, bt_ = acc()
                            ab_, bab_ = accb()
                            k.op("dve", lambda: nc.vector.reciprocal(out=s_[:, 0:8], in_=pl[:, 0:8]), reads=[bpl], writes=[bs_])
                            k.op("dve", lambda: nc.vector.tensor_tensor(out=s_[:, 8:16].rearrange("p (q b) -> p q b", q=4), in0=s_[:, 0:8].rearrange("p (q b) -> p q b", q=4),
                                                                        in1=gat[:, :, 16 + h:48:16], op=ALU.mult), reads=[bs_, bgat], writes=[bs_])
                            cg = s_[:, 8:16].rearrange("p (q b) -> p q b", q=4)
                            k.op("dve", lambda: nc.vector.tensor_tensor(out=a_[:], in0=po[:, :].rearrange("p (q d) -> p q d", q=4),
                                                                        in1=cg[:, :, 0].unsqueeze(2).to_broadcast([128, 4, 128]), op=ALU.mult),
                                 reads=[bpo, bs_], writes=[ba_])
                            k.op("dve", lambda: nc.vector.tensor_tensor(out=t_[:], in0=pw[:, :].rearrange("p (q d) -> p q d", q=4),
                                                                        in1=cg[:, :, 1].unsqueeze(2).to_broadcast([128, 4, 128]), op=ALU.mult),
                                 reads=[bpw, bs_], writes=[bt_])
                            k.op("dve", lambda: nc.vector.tensor_tensor(out=a_[:], in0=a_[:], in1=t_[:], op=ALU.add), reads=[ba_, bt_], writes=[ba_])
                            k.op("dve", lambda: nc.vector.tensor_tensor(out=ab_[:], in0=a_[:], in1=ocmp[:, r, :, :], op=ALU.add),
                                 reads=[ba_, bocmp[r][0]], writes=[bab_])
                            st_["ab"], st_["bab"] = ab_, bab_

                        def c_out():
                            ab_, bab_ = st_["ab"], st_["bab"]
                            tb_, btb_ = tbank()
                            for qs in range(4):
                                k.op("pe", lambda: nc.tensor.transpose(tb_[:, qs * 128:(qs + 1) * 128], ab_[:, qs, :], identb[:]), reads=[bab_, bnc], writes=[btb_])
                            o_, bo_ = ots()
                            k.op("act", lambda: nc.scalar.copy(out=o_[:], in_=tb_[:]), reads=[btb_], writes=[bo_])
                            k.dma("act", OT[h * 128:(h + 1) * 128, s0 + q0:s0 + q0 + TT], o_[:], owner=bo_, reads=[bo_], writes=[k.buf("nOT", h, sq_ * 4 + qt)])

                        for qs in range(4):
                            chunks.append(lambda qs=qs: c_sel(qs))
                            chunks.append(lambda qs=qs: c_win(qs))
                        chunks.append(c_fin)
                        chunks.append(c_out)
                        return chunks

                    for r in range(4):
                        pi = cnts["h"] % 2
                        cnts["h"] += 1
                        S_phase(r, pi)
                        while pvq:
                            pvq.pop(0)()
                        pvq.extend(PV_chunks(r, pi))
                    if qt == 3:
                        while pvq:
                            pvq.pop(0)()
        k.barrier()
        P.release(m)
        P.release(m0)
        run_outproj_from_dram(L, src, dst, OT, "nOTx", cw["out"])

    cur = 0
    import os
    if not os.environ.get("SKIP_IN"):
        phase_in_transpose(cur)
    for pi_, p_ in enumerate(plan):
        if pi_ + NAHEAD < len(plan):
            conv_phase(plan[pi_ + NAHEAD])
        if p_[0] == "ffn":
            run_ffn(p_[1], p_[2], cur, 1 - cur)
        elif p_[0] == "conv":
            run_conv(p_[1], cur, 1 - cur)
        elif p_[0] == "gla":
            run_gla(p_[1], cur, 1 - cur)
        elif p_[0] == "nsa":
            run_nsa(p_[1], cur, 1 - cur)
        cur = 1 - cur
    if not os.environ.get("SKIP_OUT"):
        phase_out_transpose(cur)
    k.barrier()
    return P


def make_inputs(P, core, nseq, x, ln_g, ln_b, ffn_w_in, ffn_w_out, conv_w_in=None, conv_w=None, conv_w_out=None,
                gla_w_in=None, gla_w_a2=None, gla_b_a=None, gla_norm_g=None, gla_w_out=None,
                positions=None, nsa_w_in=None, nsa_gate_b=None, nsa_cmp_pos=None, nsa_cmp_w1=None, nsa_cmp_w2=None, nsa_w_out=None, **rest):
    m = {}
    xs = np.ascontiguousarray(x[core * nseq:(core + 1) * nseq]).reshape(nseq * T, D)
    m["x"] = xs
    g = np.stack([ln_g, ln_b], axis=2)
    g = g.reshape(DEPTH, 3, 2, NDT, 128).transpose(4, 0, 1, 2, 3).reshape(128, -1)
    m["ln_gb"] = np.ascontiguousarray(g, dtype=np.float32)
    m["ident"] = np.eye(128, dtype=np.float32)
    for name in P.inputs:
        if name.startswith("nsa_"):
            m[name] = nsa_host_input(name, core, nseq, positions, nsa_w_in, nsa_gate_b, nsa_cmp_pos, nsa_cmp_w1, nsa_cmp_w2, nsa_w_out)
        elif name == "gla_w_in":
            m[name] = gla_w_in[0]
        elif name == "gla_w_out":
            m[name] = gla_w_out[0]
        elif name == "gla_const":
            s_ = np.arange(128)[:, None]
            t_ = np.arange(128)[None, :]
            same = (s_ // 64) == (t_ // 64)
            tri = ((s_ <= t_) & same).astype(np.float32) / 16.0
            mm = ((s_ > t_) & same).astype(np.float32) / 16.0
            m[name] = np.ascontiguousarray(np.concatenate([tri, mm], axis=1))
        elif name == "gla_wa2":
            m[name] = np.ascontiguousarray(np.concatenate([gla_w_a2[0], gla_b_a[0][None, :]], axis=0))
        elif name == "gla_ng":
            m[name] = np.ascontiguousarray(np.broadcast_to(gla_norm_g[0][None, :], (64, 512)))
        elif name == "gla_cm":
            m[name] = (np.arange(64)[:, None] <= np.arange(64)[None, :]).astype(np.float32)
        elif name == "conv_w_in":
            m[name] = conv_w_in[0]
        elif name == "conv_w_out":
            m[name] = conv_w_out[0]
        elif name == "conv_w":
            m[name] = np.ascontiguousarray(conv_w[0].reshape(3, NDT, 128).transpose(2, 0, 1).reshape(128, 3 * NDT))
        elif name.startswith("ffn_w_in_"):
            L, j = map(int, name.split("_")[-2:])
            m[name] = ffn_w_in[L, j]
        elif name.startswith("ffn_w_out_"):
            L, j = map(int, name.split("_")[-2:])
            m[name] = ffn_w_out[L, j]
    return m


def nsa_host_input(name, core, nseq, positions, nsa_w_in, nsa_gate_b, nsa_cmp_pos, nsa_cmp_w1, nsa_cmp_w2, nsa_w_out):
    bf = ml_dtypes.bfloat16
    if name[-2] == "_" and name[-1].isdigit():
        jn = int(name[-1])
        base = name[:-2]
        if base == "nsa_w_in":
            return nsa_w_in[jn]
        if base == "nsa_w_out":
            return nsa_w_out[jn]
        if base == "nsa_w1":
            return nsa_cmp_w1[jn]
        if base == "nsa_w2":
            return nsa_cmp_w2[jn]
        if base == "nsa_posT":
            return np.ascontiguousarray(nsa_cmp_pos[jn].transpose(2, 0, 1).reshape(128, 64))
        if base == "nsa_gateb":
            return np.ascontiguousarray(np.broadcast_to(nsa_gate_b[jn][None, :], (128, 48)))
    if name == "nsa_pos":
        p = np.ascontiguousarray(positions[core * nseq:(core + 1) * nseq]).reshape(1, nseq * T)
        return np.ascontiguousarray(np.broadcast_to(p, (128, nseq * T))).astype(np.int32)
    if name == "nsa_invs":
        inv = (np.float32(10000.0) ** (-np.arange(0, 128, 2, dtype=np.float32) / np.float32(128))).astype(np.float32)
        o = np.zeros((128, 2), np.float32)
        o[:64, 0] = inv
        o[64:, 0] = inv
        o[:64, 1] = -1.0
        o[64:, 1] = 1.0
        return o
    if name == "nsa_validc":
        n = np.arange(128)[:, None]
        t = np.arange(T)[None, :]
        v = ((16 * n + 31 <= t) & (n < 127)).astype(np.float32)
        return v.astype(bf)
    if name == "nsa_cam":
        kk = np.arange(128)[:, None]
        tt = np.arange(128)[None, :]
        return np.concatenate([(kk <= tt), (kk > tt)], axis=1).astype(np.float32).astype(bf)
    if name == "nsa_addc":
        t = np.arange(T)
        cur = t // 64
        blk = np.arange(32)[None, :]
        valid = blk <= cur[:, None]
        forced = (blk == 0) | (blk == cur[:, None]) | (blk == cur[:, None] - 1)
        a = np.where(valid, np.where(forced, 1000.0, 0.0), -1e30).astype(np.float32)
        return np.ascontiguousarray(a.reshape(16, 128, 32).transpose(1, 0, 2).reshape(128, 512))
    if name == "nsa_esel":
        mm = np.arange(128)[:, None]
        kk = np.arange(T)[None, :]
        return ((mm == kk // 64) & (mm < 32)).astype(np.float32).astype(bf)
    if name == "nsa_ovl":
        n = np.arange(128)[:, None]
        mblk = np.arange(32)[None, :]
        cs = n * 16
        ss = mblk * 64
        ov = ((cs < ss + 64) & (cs + 32 > ss) & (n < 127)).astype(np.float32)
        ones = (n < 127).astype(np.float32)
        return np.concatenate([ov, ones], axis=1).astype(bf)
    raise KeyError(name)


_CACHE = {}


def kernel(**inputs):
    inputs = {k_: np.asarray(v) for k_, v in inputs.items()}
    if "prog" not in _CACHE:
        _CACHE["prog"] = build_program(nseq=2)
    P = _CACHE["prog"]
    in_maps = [make_inputs(P, c, 2, **inputs) for c in range(NCORES)]
    res = run_bass_kernel_spmd(P.nc, in_maps, core_ids=list(range(NCORES)))
    outs = [np.asarray(r["out"]).reshape(2, T, D) for r in res.results]
    return np.concatenate(outs, axis=0).astype(np.float32)
```
